# Optimizing a Trainium2 kernel written in Bass

```python
import jax
import jax.numpy as jnp
from jax import lax
import numpy as np

D_MODEL = 1024
BATCH = 8
SEQ = 4096
DEPTH = 2

N_EVEN = (DEPTH + 1) // 2
N_ODD = DEPTH // 2
D_FF = 2816
GROUP_W = D_MODEL // 2
MIX_W = 2 * GROUP_W
EPS = 1e-6

RET_HEADS = 4
RET_DK = GROUP_W // RET_HEADS
RET_DV = GROUP_W // RET_HEADS
RET_CHUNK = 128
ROPE_BASE = 10000.0

GLA_HEADS = 4
GLA_DK = GROUP_W // (2 * GLA_HEADS)
GLA_DV = GROUP_W // GLA_HEADS
GLA_RANK = 16
GLA_TAU = 16.0
GLA_CHUNK = 64

LRU_W = GROUP_W
LRU_BLOCKS = 8
LRU_BS = LRU_W // LRU_BLOCKS
LRU_C = 8.0
CONV_W = 4

ML_HEADS = 4
ML_DH = GROUP_W // ML_HEADS
ML_CHUNK = 128

EVEN_SPLITS = (RET_HEADS * RET_DK, RET_HEADS * RET_DK, RET_HEADS * RET_DV, RET_HEADS * RET_DV,
               GLA_HEADS * GLA_DK, GLA_HEADS * GLA_DK, GLA_HEADS * GLA_DV, GLA_HEADS * GLA_DV, GLA_RANK)
EVEN_IN = sum(EVEN_SPLITS)
ODD_SPLITS = (LRU_W, LRU_W, GROUP_W, GROUP_W, GROUP_W, ML_HEADS, ML_HEADS)
ODD_IN = sum(ODD_SPLITS)

kernel_name = 'hybrid_retnet_gla_rglru_mlstm_macaron'


def rmsnorm(x, w):
    xf = x.astype(jnp.float32)
    y = xf * lax.rsqrt(jnp.mean(xf * xf, axis=-1, keepdims=True) + EPS)
    return (y * w.astype(jnp.float32)).astype(x.dtype)


def head_rms(x):
    xf = x.astype(jnp.float32)
    return xf * lax.rsqrt(jnp.mean(xf * xf, axis=-1, keepdims=True) + EPS)


def swiglu(x, w_gu, w_down):
    g, u = jnp.split(x @ w_gu, 2, axis=-1)
    return (jax.nn.silu(g) * u) @ w_down


def split_cols(y, sizes):
    return jnp.split(y, np.cumsum(sizes)[:-1].tolist(), axis=-1)


def to_chunks(x, chunk):
    B, S, H, d = x.shape
    return x.reshape(B, S // chunk, chunk, H, d).transpose(0, 3, 1, 2, 4)


def from_chunks(x):
    B, H, N, L, d = x.shape
    return x.transpose(0, 2, 3, 1, 4).reshape(B, N * L, H, d)


def scan_states(decay, upd):
    def step(s, du):
        d, u = du
        return d * s + u, s
    _, states = lax.scan(step, jnp.zeros_like(upd[0]), (decay, upd))
    return states


def rotary(x, pos):
    half = x.shape[-1] // 2
    inv = ROPE_BASE ** (-jnp.arange(half, dtype=jnp.float32) / half)
    ang = pos.astype(jnp.float32)[:, None] * inv[None, :]
    cos = jnp.cos(ang)[None, :, None, :]
    sin = jnp.sin(ang)[None, :, None, :]
    x1, x2 = x[..., :half], x[..., half:]
    return jnp.concatenate([x1 * cos - x2 * sin, x1 * sin + x2 * cos], axis=-1)


def causal_dwconv(x, w, b):
    out = lax.conv_general_dilated(x, w[:, None, :].astype(x.dtype), window_strides=(1,),
                                   padding=[(CONV_W - 1, 0)], dimension_numbers=('NWC', 'WIO', 'NWC'),
                                   feature_group_count=x.shape[-1])
    return out + b.astype(x.dtype)


def retention(q, k, v):
    B, S, H, _ = q.shape
    L = RET_CHUNK
    N = S // L
    log_g = jnp.log1p(-jnp.exp2(-5.0 - jnp.arange(H, dtype=jnp.float32)))
    qc, kc, vc = to_chunks(q, L), to_chunks(k, L), to_chunks(v, L)
    idx = jnp.arange(L, dtype=jnp.float32)
    rel = idx[:, None] - idx[None, :]
    causal = rel >= 0
    dmat = jnp.where(causal, jnp.exp(log_g[:, None, None] * jnp.where(causal, rel, 0.0)), 0.0)
    s = jnp.einsum('bhnld,bhnsd->bhnls', qc, kc) * dmat[None, :, None]
    intra = jnp.einsum('bhnls,bhnsv->bhnlv', s, vc)
    w_end = jnp.exp(log_g[:, None] * (L - 1.0 - idx)[None, :])
    upd = jnp.einsum('bhnsd,hs,bhnsv->nbhdv', kc, w_end, vc)
    decay = jnp.broadcast_to(jnp.exp(log_g * L)[None, None, :, None, None], (N, 1, H, 1, 1))
    states = scan_states(decay, upd)
    w_in = jnp.exp(log_g[:, None] * (idx + 1.0)[None, :])
    inter = jnp.einsum('bhnld,nbhdv->bhnlv', qc, states) * w_in[None, :, None, :, None]
    return from_chunks(intra + inter)


def gla(q, k, v, log_a):
    B, S, H, _ = q.shape
    L = GLA_CHUNK
    qc, kc, vc, gc = to_chunks(q, L), to_chunks(k, L), to_chunks(v, L), to_chunks(log_a, L)
    b = jnp.cumsum(gc, axis=3)
    b_mid = b[:, :, :, L // 2 - 1:L // 2]
    causal = jnp.tril(jnp.ones((L, L), dtype=bool))
    s = jnp.einsum('bhnld,bhnsd->bhnls', qc * jnp.exp(b - b_mid), kc * jnp.exp(b_mid - b))
    s = jnp.where(causal, s, 0.0)
    intra = jnp.einsum('bhnls,bhnsv->bhnlv', s, vc)
    b_end = b[:, :, :, -1:]
    upd = jnp.einsum('bhnsd,bhnsv->nbhdv', kc * jnp.exp(b_end - b), vc)
    decay = jnp.exp(b_end[:, :, :, 0]).transpose(2, 0, 1, 3)[..., None]
    states = scan_states(decay, upd)
    inter = jnp.einsum('bhnld,nbhdv->bhnlv', qc * jnp.exp(b), states)
    return from_chunks(intra + inter)


def rg_lru(xc, w_a, b_a, w_x, b_x, lam):
    B, S, C = xc.shape
    xb = xc.reshape(B, S, LRU_BLOCKS, LRU_BS)
    r = jax.nn.sigmoid(jnp.einsum('bsgi,gij->bsgj', xb, w_a).reshape(B, S, C) + b_a)
    i = jax.nn.sigmoid(jnp.einsum('bsgi,gij->bsgj', xb, w_x).reshape(B, S, C) + b_x)
    log_a = -LRU_C * r * jax.nn.softplus(-lam)
    a = jnp.exp(log_a)
    u = jnp.sqrt(-jnp.expm1(2.0 * log_a)) * (i * xc)

    def combine(left, right):
        a1, b1 = left
        a2, b2 = right
        return a1 * a2, a2 * b1 + b2

    _, h = lax.associative_scan(combine, (a, u), axis=1)
    return h


def mlstm(q, k, v, i_pre, f_pre):
    B, S, H, _ = q.shape
    L = ML_CHUNK
    N = S // L
    qc, kc, vc = to_chunks(q, L), to_chunks(k, L), to_chunks(v, L)
    ic = i_pre.reshape(B, N, L, H).transpose(0, 3, 1, 2)
    bc = jnp.cumsum(jax.nn.log_sigmoid(f_pre).reshape(B, N, L, H).transpose(0, 3, 1, 2), axis=-1)
    causal = jnp.tril(jnp.ones((L, L), dtype=bool))
    log_w = jnp.where(causal, bc[..., :, None] - bc[..., None, :] + ic[..., None, :], -jnp.inf)
    m_intra = jnp.max(log_w, axis=-1)
    b_end = bc[..., -1]
    a = b_end[..., None] - bc + ic
    a_max = jnp.max(a, axis=-1)
    w_end = jnp.exp(a - a_max[..., None])
    upd_c = jnp.einsum('bhns,bhnsd,bhnsv->nbhdv', w_end, kc, vc)
    upd_n = jnp.einsum('bhns,bhnsd->nbhd', w_end, kc)

    def step(carry, xs):
        c, n, m = carry
        uc, un, be, am = xs
        m_new = jnp.maximum(be + m, am)
        g_old = jnp.exp(be + m - m_new)
        g_new = jnp.exp(am - m_new)
        c_new = g_old[..., None, None] * c + g_new[..., None, None] * uc
        n_new = g_old[..., None] * n + g_new[..., None] * un
        return (c_new, n_new, m_new), (c, n, m)

    init = (jnp.zeros_like(upd_c[0]), jnp.zeros_like(upd_n[0]), jnp.zeros_like(b_end[:, :, 0]))
    _, (cs, ns, ms) = lax.scan(step, init, (upd_c, upd_n, b_end.transpose(2, 0, 1), a_max.transpose(2, 0, 1)))
    m_inter = bc + ms.transpose(1, 2, 0)[..., None]
    m_t = jnp.maximum(m_inter, m_intra)
    s = jnp.einsum('bhnld,bhnsd->bhnls', qc, kc) * jnp.exp(log_w - m_t[..., None])
    w_inter = jnp.exp(m_inter - m_t)
    num = jnp.einsum('bhnls,bhnsv->bhnlv', s, vc) + w_inter[..., None] * jnp.einsum('bhnld,nbhdv->bhnlv', qc, cs)
    den = jnp.sum(s, axis=-1) + w_inter * jnp.einsum('bhnld,nbhd->bhnl', qc, ns)
    h = num / jnp.maximum(jnp.abs(den), jnp.exp(-m_t))[..., None]
    return from_chunks(h)


def even_mixer(hn, w_in, w_lr_up, b_lr, head_norm_w, w_out):
    B, S, _ = hn.shape
    f32 = lambda t: t.astype(jnp.float32)
    rq, rk, rv, rg, gq, gk, gv, gg, glr = split_cols(hn @ w_in, EVEN_SPLITS)
    pos = jnp.arange(S)
    rq = rotary(f32(rq).reshape(B, S, RET_HEADS, RET_DK), pos)
    rk = rotary(f32(rk).reshape(B, S, RET_HEADS, RET_DK), pos) * (RET_DK ** -0.5)
    ret = retention(rq, rk, f32(rv).reshape(B, S, RET_HEADS, RET_DV))
    log_a = jax.nn.log_sigmoid(f32(glr) @ f32(w_lr_up) + f32(b_lr)) / GLA_TAU
    go = gla(f32(gq).reshape(B, S, GLA_HEADS, GLA_DK),
             f32(gk).reshape(B, S, GLA_HEADS, GLA_DK) * (GLA_DK ** -0.5),
             f32(gv).reshape(B, S, GLA_HEADS, GLA_DV),
             log_a.reshape(B, S, GLA_HEADS, GLA_DK))
    o = jnp.concatenate([head_rms(ret).reshape(B, S, -1) * jax.nn.silu(f32(rg)),
                         head_rms(go).reshape(B, S, -1) * jax.nn.silu(f32(gg))], axis=-1)
    o = o * f32(head_norm_w)
    return o.astype(hn.dtype) @ w_out


def odd_mixer(hn, w_in, lru_conv_w, lru_conv_b, lru_wa, lru_ba, lru_wx, lru_bx, lru_lambda,
              ml_conv_w, ml_conv_b, ml_wq, ml_wk, ml_bi, ml_bf, ml_norm_w, w_out):
    B, S, _ = hn.shape
    f32 = lambda t: t.astype(jnp.float32)
    ly, lx, mu, mv, mo, mi, mf = split_cols(hn @ w_in, ODD_SPLITS)
    lxc = causal_dwconv(f32(lx), f32(lru_conv_w), lru_conv_b)
    lru = rg_lru(lxc, f32(lru_wa), f32(lru_ba), f32(lru_wx), f32(lru_bx), f32(lru_lambda))
    lru = lru * jax.nn.gelu(f32(ly))
    mc = jax.nn.silu(causal_dwconv(f32(mu), f32(ml_conv_w), ml_conv_b)).reshape(B, S, ML_HEADS, ML_DH)
    q = jnp.einsum('bshi,hij->bshj', mc, f32(ml_wq))
    k = jnp.einsum('bshi,hij->bshj', mc, f32(ml_wk)) * (ML_DH ** -0.5)
    hm = mlstm(q, k, f32(mv).reshape(B, S, ML_HEADS, ML_DH), f32(mi) + f32(ml_bi), f32(mf) + f32(ml_bf))
    hm = jax.nn.sigmoid(f32(mo)).reshape(B, S, ML_HEADS, ML_DH) * hm
    hm = head_rms(hm).reshape(B, S, -1) * f32(ml_norm_w)
    o = jnp.concatenate([lru, hm], axis=-1)
    return o.astype(hn.dtype) @ w_out


def setup_inputs(seed: int = 0) -> dict:
    key = jax.random.key(seed)
    ks = jax.random.split(key, 40)
    nrm = lambda k, shape, scale: jax.random.normal(k, shape, dtype=jnp.float32) * scale
    gain = lambda k, shape: 1.0 + 0.02 * jax.random.normal(k, shape, dtype=jnp.float32)
    a8 = jax.random.uniform(ks[30], (N_ODD, LRU_W), minval=0.9, maxval=0.999, dtype=jnp.float32)
    a1 = a8 ** (1.0 / LRU_C)
    lam = jnp.log(a1) - jnp.log1p(-a1)
    bf = jnp.broadcast_to(jnp.linspace(3.0, 6.0, ML_HEADS, dtype=jnp.float32), (N_ODD, ML_HEADS))
    return {
        'x': nrm(ks[0], (BATCH, SEQ, D_MODEL), 1.0),
        'ffn1_norm': gain(ks[1], (DEPTH, D_MODEL)),
        'ffn1_wgu': nrm(ks[2], (DEPTH, D_MODEL, 2 * D_FF), D_MODEL ** -0.5),
        'ffn1_wd': nrm(ks[3], (DEPTH, D_FF, D_MODEL), D_FF ** -0.5),
        'mix_norm': gain(ks[4], (DEPTH, D_MODEL)),
        'ffn2_norm': gain(ks[5], (DEPTH, D_MODEL)),
        'ffn2_wgu': nrm(ks[6], (DEPTH, D_MODEL, 2 * D_FF), D_MODEL ** -0.5),
        'ffn2_wd': nrm(ks[7], (DEPTH, D_FF, D_MODEL), D_FF ** -0.5),
        'e_w_in': nrm(ks[8], (N_EVEN, D_MODEL, EVEN_IN), D_MODEL ** -0.5),
        'e_w_lr_up': nrm(ks[9], (N_EVEN, GLA_RANK, GLA_HEADS * GLA_DK), GLA_RANK ** -0.5),
        'e_b_lr': nrm(ks[10], (N_EVEN, GLA_HEADS * GLA_DK), 0.1),
        'e_head_norm': gain(ks[11], (N_EVEN, MIX_W)),
        'e_w_out': nrm(ks[12], (N_EVEN, MIX_W, D_MODEL), MIX_W ** -0.5),
        'o_w_in': nrm(ks[13], (N_ODD, D_MODEL, ODD_IN), D_MODEL ** -0.5),
        'o_lru_conv_w': nrm(ks[14], (N_ODD, CONV_W, LRU_W), CONV_W ** -0.5),
        'o_lru_conv_b': nrm(ks[15], (N_ODD, LRU_W), 0.01),
        'o_lru_wa': nrm(ks[16], (N_ODD, LRU_BLOCKS, LRU_BS, LRU_BS), LRU_BS ** -0.5),
        'o_lru_ba': nrm(ks[17], (N_ODD, LRU_W), 0.01),
        'o_lru_wx': nrm(ks[18], (N_ODD, LRU_BLOCKS, LRU_BS, LRU_BS), LRU_BS ** -0.5),
        'o_lru_bx': nrm(ks[19], (N_ODD, LRU_W), 0.01),
        'o_lru_lambda': lam,
        'o_ml_conv_w': nrm(ks[20], (N_ODD, CONV_W, GROUP_W), CONV_W ** -0.5),
        'o_ml_conv_b': nrm(ks[21], (N_ODD, GROUP_W), 0.01),
        'o_ml_wq': nrm(ks[22], (N_ODD, ML_HEADS, ML_DH, ML_DH), ML_DH ** -0.5),
        'o_ml_wk': nrm(ks[23], (N_ODD, ML_HEADS, ML_DH, ML_DH), ML_DH ** -0.5),
        'o_ml_bi': nrm(ks[24], (N_ODD, ML_HEADS), 0.1),
        'o_ml_bf': bf + nrm(ks[25], (N_ODD, ML_HEADS), 0.01),
        'o_ml_norm': gain(ks[26], (N_ODD, GROUP_W)),
        'o_w_out': nrm(ks[27], (N_ODD, MIX_W, D_MODEL), MIX_W ** -0.5),
        'final_norm': gain(ks[28], (D_MODEL,)),
    }


def reference(x, ffn1_norm, ffn1_wgu, ffn1_wd, mix_norm, ffn2_norm, ffn2_wgu, ffn2_wd,
              e_w_in, e_w_lr_up, e_b_lr, e_head_norm, e_w_out,
              o_w_in, o_lru_conv_w, o_lru_conv_b, o_lru_wa, o_lru_ba, o_lru_wx, o_lru_bx, o_lru_lambda,
              o_ml_conv_w, o_ml_conv_b, o_ml_wq, o_ml_wk, o_ml_bi, o_ml_bf, o_ml_norm, o_w_out,
              final_norm):
    h = x
    for layer in range(DEPTH):
        h = h + 0.5 * swiglu(rmsnorm(h, ffn1_norm[layer]), ffn1_wgu[layer], ffn1_wd[layer])
        hn = rmsnorm(h, mix_norm[layer])
        j = layer // 2
        if layer % 2 == 0:
            h = h + even_mixer(hn, e_w_in[j], e_w_lr_up[j], e_b_lr[j], e_head_norm[j], e_w_out[j])
        else:
            h = h + odd_mixer(hn, o_w_in[j], o_lru_conv_w[j], o_lru_conv_b[j], o_lru_wa[j], o_lru_ba[j],
                              o_lru_wx[j], o_lru_bx[j], o_lru_lambda[j], o_ml_conv_w[j], o_ml_conv_b[j],
                              o_ml_wq[j], o_ml_wk[j], o_ml_bi[j], o_ml_bf[j], o_ml_norm[j], o_w_out[j])
        h = h + 0.5 * swiglu(rmsnorm(h, ffn2_norm[layer]), ffn2_wgu[layer], ffn2_wd[layer])
    return rmsnorm(h, final_norm)
```

```python
import contextlib
import numpy as np
import concourse.bass as bass
import concourse.mybir as mybir
from concourse.bass_utils import run_bass_kernel_spmd

F32 = mybir.dt.float32
BF16 = mybir.dt.bfloat16
AF = mybir.ActivationFunctionType
ALU = mybir.AluOpType
AX = mybir.AxisListType

D = 1024
DFF = 2816
NF = DFF // 128
EPS = 1e-6


SAME_SYNC = {"pool", "act", "dve"}
LOOKAHEAD = 24
HOP = 0.20
FIX = {"dve": 0.09, "act": 0.09, "pool": 0.09}
PE_FIX = 0.004


class _Op:
    __slots__ = ("eng", "idx", "emit", "waits", "signal", "is_dma", "chan", "chan_val", "sigval",
                 "deps", "succ", "cost", "lat", "pidx", "nun", "ready", "start", "finish", "seg")


class _Dummy:
    def then_inc(self, *a, **k):
        return self


class _CostProxy:
    def __init__(self, eng):
        self.eng = eng
        self.cost = 0.0
        self.lat = 0.0

    def __getattr__(self, name):
        def f(*args, **kw):
            def fs(ap):
                sh = ap.shape
                n = 1
                for v in sh[1:]:
                    n *= int(v)
                return n
            if name == "matmul":
                rhs = args[2] if len(args) > 2 else kw["rhs"]
                self.cost += max(fs(rhs), 64) / 2400.0 + PE_FIX
            elif name == "transpose":
                self.cost += max(fs(args[1]), 64) / 2400.0 + 0.03
            elif name == "dma_start":
                o = kw.get("out", args[0] if args else None)
                nb = fs(o) * int(o.shape[0]) * (2 if o.dtype == BF16 else 4)
                self.cost += 0.6 if self.eng == "pool" else 0.08
                self.lat += 2.0 + nb / 160e3
            else:
                o = kw.get("out", args[0] if args else None)
                n = fs(o)
                if name == "tensor_tensor_scan":
                    n *= 2
                rate = {"dve": 960.0, "act": 1200.0, "pool": 800.0}.get(self.eng, 960.0)
                self.cost += FIX.get(self.eng, 0.09) + n / rate
                if kw.get("accum_out") is not None:
                    self.cost += FIX.get(self.eng, 0.09)
            return _Dummy()
        return f


class Sched:
    ENGS = ("pe", "act", "dve", "pool", "sp")

    def __init__(self, nc):
        self.nc = nc
        self.ops = []
        self.lastw = {}
        self.readers = {}
        self.chan_cnt = {}
        self.chan_sem = {}
        self.chan_eng = {}
        self.sigcnt = {e: 0 for e in self.ENGS}
        self.eng_sem = {}
        self.seg = 0
        self.total_est = 0.0
        self._stack = contextlib.ExitStack()
        for e in self.ENGS:
            self.eng_sem[e] = self._stack.enter_context(nc.semaphore("sem_" + e))

    def _chan(self, name, eng):
        if name not in self.chan_sem:
            self.chan_sem[name] = self._stack.enter_context(
                self.nc.semaphore("dq_" + str(name).replace(" ", "")))
            self.chan_cnt[name] = 0
            self.chan_eng[name] = eng
        assert self.chan_eng[name] == eng, "DMA channel used from two queues: %s" % (name,)
        return self.chan_sem[name]

    def add(self, eng, emit, reads=(), writes=(), dma=None):
        op = _Op()
        op.eng = eng
        op.emit = emit
        op.is_dma = dma is not None
        op.pidx = len(self.ops)
        op.signal = False
        op.waits = []
        op.sigval = None
        op.chan = dma
        op.chan_val = None
        op.seg = self.seg
        op.succ = []
        if op.is_dma:
            self._chan(dma, eng)
        px = _CostProxy(eng)
        emit(px)
        op.cost = px.cost
        op.lat = px.lat
        deps = {}
        for r in reads:
            w = self.lastw.get(r)
            if w is not None and w.seg == self.seg:
                deps[id(w)] = w
        for r in writes:
            w = self.lastw.get(r)
            if w is not None and w.seg == self.seg:
                deps[id(w)] = w
            for w in self.readers.get(r, ()):
                if w.seg == self.seg:
                    deps[id(w)] = w
        deps.pop(id(op), None)
        op.deps = list(deps.values())
        for d in op.deps:
            d.succ.append(op)
        for r in reads:
            self.readers.setdefault(r, []).append(op)
        for r in writes:
            self.lastw[r] = op
            self.readers[r] = []
        self.ops.append(op)
        return op

    def barrier(self):
        pass

    def _list_schedule(self):
        ops = self.ops
        avail = {e: [] for e in self.ENGS}
        free = {e: 0.0 for e in self.ENGS}
        order = {e: [] for e in self.ENGS}
        import bisect
        for op in ops:
            op.nun = len(op.deps)
            op.ready = 0.0
            if op.nun == 0:
                avail[op.eng].append((op.pidx, op))
        nleft = len(ops)
        while nleft:
            best = None
            bstart = None
            for e in self.ENGS:
                al = avail[e]
                if not al:
                    continue
                f = free[e]
                for k in range(min(LOOKAHEAD, len(al))):
                    op = al[k][1]
                    st = op.ready if op.ready > f else f
                    if bstart is None or st < bstart - 0.05 or (st < bstart + 0.05 and op.pidx < best.pidx):
                        if bstart is None or st < bstart + 0.05:
                            best, bstart = op, (st if bstart is None else min(st, bstart))
            op = best
            e = op.eng
            st = op.ready if op.ready > free[e] else free[e]
            op.start = st
            free[e] = st + op.cost
            op.finish = st + op.cost + op.lat
            order[e].append(op)
            al = avail[e]
            al.pop(bisect.bisect_left(al, (op.pidx,)))
            nleft -= 1
            for s_ in op.succ:
                hop = HOP if (s_.eng != e or op.is_dma or e in SAME_SYNC) else 0.0
                r = op.finish + hop
                if r > s_.ready:
                    s_.ready = r
                s_.nun -= 1
                if s_.nun == 0:
                    bisect.insort(avail[s_.eng], (s_.pidx, s_))
        self.total_est += max(free.values()) if ops else 0.0
        return order

    def flush(self):
        nc = self.nc
        order = self._list_schedule()
        for e in self.ENGS:
            for i, op in enumerate(order[e]):
                op.idx = i
                if op.is_dma:
                    self.chan_cnt[op.chan] += 16
                    op.chan_val = self.chan_cnt[op.chan]
        for e in self.ENGS:
            wd = {}
            for op in order[e]:
                best = {}
                for d in op.deps:
                    if d.is_dma:
                        key = ("c", d.chan)
                        if wd.get(key, 0) >= d.chan_val:
                            continue
                        if key not in best or best[key].chan_val < d.chan_val:
                            best[key] = d
                    else:
                        if d.eng == e and not op.is_dma and e not in SAME_SYNC:
                            continue
                        key = ("e", d.eng)
                        if wd.get(key, -1) >= d.idx:
                            continue
                        if key not in best or best[key].idx < d.idx:
                            best[key] = d
                for key, d in best.items():
                    if key[0] == "c":
                        wd[key] = d.chan_val
                    else:
                        wd[key] = d.idx
                        d.signal = True
                    op.waits.append(d)
        lasts = []
        for e in self.ENGS:
            comp = [op for op in order[e] if not op.is_dma]
            if comp:
                comp[-1].signal = True
                lasts.append(comp[-1])
        for e in self.ENGS:
            for op in order[e]:
                if op.signal and op.sigval is None:
                    self.sigcnt[e] += 1
                    op.sigval = self.sigcnt[e]
        chans = [(n, v) for n, v in self.chan_cnt.items() if v > 0]
        handles = {"pe": "tensor", "act": "scalar", "dve": "vector", "pool": "gpsimd", "sp": "sync"}
        sched = self

        def run(ename, eng):
            for op in order[ename]:
                for d in op.waits:
                    if d.is_dma:
                        eng.wait_ge(sched.chan_sem[d.chan], d.chan_val)
                    else:
                        eng.wait_ge(sched.eng_sem[d.eng], d.sigval)
                ins = op.emit(eng)
                if op.is_dma:
                    ins.then_inc(sched.chan_sem[op.chan], 16)
                elif op.signal:
                    ins.then_inc(sched.eng_sem[ename], 1)
            for d in lasts:
                eng.wait_ge(sched.eng_sem[d.eng], d.sigval)
            for n, v in chans:
                eng.wait_ge(sched.chan_sem[n], v)

        with nc.Block() as block:
            for ename in self.ENGS:
                deco = getattr(block, handles[ename])

                def _f(eng, _en=ename):
                    run(_en, eng)

                deco(_f)
        self.ops = []
        self.seg += 1

    def close(self):
        self._stack.close()


class Ctx:
    def __init__(self, nc, sc, S):
        self.nc = nc
        self.sc = sc
        self.S = S
        self.uid = 0

    def tag(self):
        self.uid += 1
        return "_%d" % self.uid


def _rstd_ops(sc, ss_ap, rstd_ap, nhalf_ap, n, key_in, key_out, key_tmp, tmp_ap):
    sc.add("dve", lambda e: e.tensor_scalar(tmp_ap, ss_ap, 1.0 / n, EPS, ALU.mult, ALU.add),
           reads=[key_in], writes=[key_tmp])
    sc.add("pool", lambda e: e.tensor_tensor(rstd_ap, tmp_ap, nhalf_ap, ALU.pow),
           reads=[key_tmp, "nhalf"], writes=[key_out])


def alloc_norm(sb, T, nld=2):
    B = {}
    B["xld"] = [sb("xld%d" % i, [128, D], F32) for i in range(nld)]
    B["xnt"] = [sb("xnt%d" % i, [128, D], BF16) for i in range(nld)]
    B["xnT"] = [sb("xnT%d" % i, [128, 8, T], BF16) for i in range(2)]
    B["gcol"] = sb("gcol", [128, 8], F32)
    B["ident"] = sb("ident", [128, 128], BF16)
    B["ss"] = sb("ss", [128, 8], F32)
    B["ms"] = sb("ms", [128, 8], F32)
    B["rstd"] = sb("rstd", [128, 8], F32)
    B["nhalf"] = sb("nhalf", [128, 1], F32)
    return B


def init_norm(sc, B, nrm_ap, ident_ap):
    nhalf, ident, gcol = B["nhalf"], B["ident"], B["gcol"]
    sc.add("pool", lambda e: e.memset(nhalf[:, :], -0.5), writes=["nhalf"])
    sc.add("pool", lambda e: e.dma_start(out=ident[:, :], in_=ident_ap), writes=["ident"], dma="k9")
    sc.add("sp", lambda e: e.dma_start(out=gcol[:, :], in_=nrm_ap.rearrange("(k p) -> p k", p=128),
                                       allow_slow_non_contiguous=True),
           writes=["gcol"], dma="k1")


def norm_sub(sc, B, src, srcname, pt_bf, ptkey, t, j, nsub=4):
    g = t * nsub + j
    sl = g % len(B["xld"])
    r0 = g * 128
    xld, xnt, xnT = B["xld"][sl], B["xnt"][sl], B["xnT"][t % 2]
    ss, ms, rstd, nhalf, gcol, ident = B["ss"], B["ms"], B["rstd"], B["nhalf"], B["gcol"], B["ident"]
    sc.add("sp", lambda e: e.dma_start(out=xld[:, :], in_=src[r0:r0 + 128, :]),
           reads=[(srcname, g)], writes=[("xld", sl)], dma="xld%d" % sl)
    c = g % 8
    sc.add("act", lambda e: e.activation(xnt[:, :], xld[:, :], AF.Square, accum_out=ss[:, c:c + 1]),
           reads=[("xld", sl)], writes=[("xnt", sl), ("ss", c)])
    _rstd_ops(sc, ss[:, c:c + 1], rstd[:, c:c + 1], nhalf[:, 0:1], float(D),
              ("ss", c), ("rstd", c), ("ms", c), ms[:, c:c + 1])
    sc.add("act", lambda e: e.activation(xnt[:, :], xld[:, :], AF.Copy, scale=rstd[:, c:c + 1]),
           reads=[("xld", sl), ("rstd", c)], writes=[("xnt", sl)])

    def tr(pe):
        ins = None
        for k in range(8):
            ins = pe.transpose(pt_bf[:, k * 128:(k + 1) * 128], xnt[:, k * 128:(k + 1) * 128], ident[:, :])
        return ins
    sc.add("pe", tr, reads=[("xnt", sl), "ident"], writes=[ptkey])
    b = t % 2
    sc.add("dve", lambda e: e.tensor_tensor(
        xnT[:, :, j * 128:(j + 1) * 128],
        pt_bf.rearrange("p (k t) -> p k t", k=8),
        gcol[:, :].unsqueeze(2).broadcast_to([128, 8, 128]), ALU.mult),
        reads=[ptkey, "gcol"], writes=[("xnT", b, j)])


def ffn_phase(cx, src, dst, srcname, dstname, nrm_ap, wgu_ap, wd_ap, ident_ap, fin_ap=None):
    nc, sc, S = cx.nc, cx.sc, cx.S
    T = 512
    NT = S // T
    tg = cx.tag()
    with contextlib.ExitStack() as st:
        def sb(name, shape, dt):
            return st.enter_context(nc.sbuf_tensor(name + tg, shape, dt))

        def ps(name, shape, dt):
            return st.enter_context(nc.psum_tensor(name + tg, shape, dt))

        wgu = sb("wgu", [128, 8, 2 * DFF], BF16)
        wd = sb("wd", [128, NF, D], BF16)
        B = alloc_norm(sb, T)
        xnT = B["xnT"]
        ss, ms, rstd, nhalf = B["ss"], B["ms"], B["rstd"], B["nhalf"]
        sil = [sb("sil%d" % i, [128, T], F32) for i in range(2)]
        hT = sb("hT", [128, NF, T], BF16)
        xres = [sb("xres%d" % i, [128, D], F32) for i in range(2)]
        if fin_ap is not None:
            grow = sb("grow", [128, D], F32)
            junk = sb("junk", [128, D], BF16)
        pg = [ps("pg%d" % i, [128, T], F32) for i in range(2)]
        pu = [ps("pu%d" % i, [128, T], F32) for i in range(2)]
        pt = ps("pt", [128, 8 * 128], BF16)
        po = [ps("po%d" % i, [128, 512], F32) for i in range(2)]

        init_norm(sc, B, nrm_ap, ident_ap)
        if fin_ap is not None:
            sc.add("sp", lambda e: e.dma_start(out=grow[:, :], in_=fin_ap.partition_broadcast(128)),
                   writes=["grow"], dma="k2")
        FG = [(0, 2), (2, 6), (6, 14), (14, 22)]
        fgrp = {}
        wgu_v = wgu_ap.rearrange("(k p) c -> p k c", p=128)
        for gi, (f0, f1) in enumerate(FG):
            for f in range(f0, f1):
                fgrp[f] = gi
            for part in range(2):
                c0 = part * DFF + f0 * 128
                c1 = part * DFF + f1 * 128
                sc.add("pool", lambda e, c0=c0, c1=c1: e.dma_start(out=wgu[:, :, c0:c1], in_=wgu_v[:, :, c0:c1]),
                       writes=[("wgu", gi, part)], dma="wgu%d" % (gi * 2 + part))
        DG = [(0, 8), (8, 15), (15, 22)]
        wd_v = wd_ap.rearrange("(f p) d -> p f d", p=128)
        for gi, (f0, f1) in enumerate(DG):
            sc.add("pool", lambda e, f0=f0, f1=f1: e.dma_start(out=wd[:, f0:f1, :], in_=wd_v[:, f0:f1, :]),
                   writes=[("wd", gi)], dma="wd%d" % gi)

        def gate_up(t, f):
            b = t % 2
            pb = f % 2

            def mm(pe):
                ins = None
                for k in range(8):
                    ins = pe.matmul(pg[pb][:, :], wgu[:, k, f * 128:(f + 1) * 128], xnT[b][:, k, :],
                                    start=(k == 0), stop=(k == 7))
                for k in range(8):
                    ins = pe.matmul(pu[pb][:, :], wgu[:, k, DFF + f * 128:DFF + (f + 1) * 128],
                                    xnT[b][:, k, :], start=(k == 0), stop=(k == 7))
                return ins
            sc.add("pe", mm, reads=[("wgu", fgrp[f], 0), ("wgu", fgrp[f], 1)] + [("xnT", b, j) for j in range(4)],
                   writes=[("pg", pb), ("pu", pb)])
            sc.add("act", lambda e: e.activation(sil[pb][:, :], pg[pb][:, :], AF.Silu),
                   reads=[("pg", pb)], writes=[("sil", pb)])
            sc.add("dve", lambda e: e.tensor_tensor(hT[:, f, :], sil[pb][:, :], pu[pb][:, :], ALU.mult),
                   reads=[("sil", pb), ("pu", pb)], writes=[("hT", f)])

        def down_sub(t, j):
            g = t * 4 + j
            sl = g % 2
            r0 = g * 128
            sc.add("sp", lambda e: e.dma_start(out=xres[sl][:, :], in_=src[r0:r0 + 128, :]),
                   reads=[(srcname, g)], writes=[("xres", sl)], dma="xres%d" % sl)
            for dh in range(2):
                def mm(pe, dh=dh):
                    ins = None
                    for f in range(NF):
                        ins = pe.matmul(po[dh][:, :], hT[:, f, j * 128:(j + 1) * 128],
                                        wd[:, f, dh * 512:(dh + 1) * 512],
                                        start=(f == 0), stop=(f == NF - 1))
                    return ins
                sc.add("pe", mm, reads=[("hT", f) for f in range(NF)] + [("wd", gi) for gi in range(3)],
                       writes=[("po", dh)])
                sc.add("dve", lambda e, dh=dh: e.scalar_tensor_tensor(
                    xres[sl][:, dh * 512:(dh + 1) * 512], po[dh][:, :], 0.5,
                    xres[sl][:, dh * 512:(dh + 1) * 512], ALU.mult, ALU.add),
                    reads=[("po", dh), ("xres", sl)], writes=[("xres", sl)])
            if fin_ap is not None:
                c = g % 8
                sc.add("act", lambda e: e.activation(junk[:, :], xres[sl][:, :], AF.Square,
                                                     accum_out=ss[:, c:c + 1]),
                       reads=[("xres", sl)], writes=["junk", ("ss", c)])
                _rstd_ops(sc, ss[:, c:c + 1], rstd[:, c:c + 1], nhalf[:, 0:1], float(D),
                          ("ss", c), ("rstd", c), ("ms", c), ms[:, c:c + 1])
                sc.add("dve", lambda e: e.scalar_tensor_tensor(
                    xres[sl][:, :], xres[sl][:, :], rstd[:, c:c + 1], grow[:, :], ALU.mult, ALU.mult),
                    reads=[("xres", sl), ("rstd", c), "grow"], writes=[("xres", sl)])
            sc.add("sp", lambda e: e.dma_start(out=dst[r0:r0 + 128, :], in_=xres[sl][:, :]),
                   reads=[("xres", sl)], writes=[(dstname, g)], dma="xst%d" % sl)

        for j in range(4):
            norm_sub(sc, B, src, srcname, pt[:, :], "pt", 0, j)
        inter = {3: 0, 8: 1, 13: 2, 18: 3}
        for t in range(NT):
            for f in range(NF):
                gate_up(t, f)
                if f in inter and t + 1 < NT:
                    norm_sub(sc, B, src, srcname, pt[:, :], "pt", t + 1, inter[f])
            for j in range(4):
                down_sub(t, j)
        sc.barrier()
        sc.flush()


RET_G = [1.0 - 2.0 ** (-5.0 - h) for h in range(4)]
LN_GK = float(np.log(0.125))


def even_phase(cx, src, dst, srcname, dstname, nrm_ap, W, C):
    nc, sc, S = cx.nc, cx.sc, cx.S
    T = 512
    NT = S // T
    tg = cx.tag()
    GL = [float(np.float64(g) ** 128) for g in RET_G]
    with contextlib.ExitStack() as st:
        def sb(name, shape, dt):
            return st.enter_context(nc.sbuf_tensor(name + tg, shape, dt))

        def ps(name, shape, dt):
            return st.enter_context(nc.psum_tensor(name + tg, shape, dt))

        win = sb("win", [128, 8, 3600], BF16)
        winp = sb("winp", [128, 8, 1024], BF16)
        wout = sb("wout", [128, 8, D], BF16)
        hcol = sb("hcol", [128, 8], F32)
        wlr = sb("wlr", [16, 256], F32)
        nblr = sb("nblr", [128, 2], F32)
        B = alloc_norm(sb, T)
        xnT = B["xnT"]
        nhalf = B["nhalf"]
        ident = B["ident"]
        xres = [sb("xres%d" % i, [128, D], F32) for i in range(2)]
        cst = sb("cst", [128, T], F32)
        snt = sb("snt", [128, T], F32)
        t1 = sb("t1", [128, T], F32)
        t2 = sb("t2", [128, T], F32)
        qT = sb("qT", [128, 4, T], BF16)
        kT = sb("kT", [128, 4, T], BF16)
        dq = sb("dq", [128, 4, 128], F32)
        dk = sb("dk", [128, 4, 128], F32)
        glrT = sb("glrT", [16, T], F32)
        ea = sb("ea", [128, T], F32)
        cs = sb("cs", [128, T], F32)
        eqt = sb("eqt", [128, T], F32)
        ekt = sb("ekt", [128, T], F32)
        gqz = sb("gqz", [128, 4, T], BF16)
        gkT = sb("gkT", [128, 2, T], BF16)
        rmask = sb("rmask", [128, T], F32)
        csm = sb("csm", [128, 2, 4], F32)
        emid = sb("emid", [128, 2, 4], F32)
        eend = sb("eend", [128, 2, 4], F32)
        eem = sb("eem", [128, 2, 4], F32)
        dcs = sb("dcs", [128, 2, 4], F32)
        vr = sb("vr", [128, 512], BF16)
        vg = sb("vg", [128, 512], BF16)
        gate = sb("gate", [128, D], F32)
        sT = sb("sT", [128, 512], BF16)
        sTg = sb("sTg", [128, 512], BF16)
        ktok = sb("ktok", [128, 512], BF16)
        kpad = sb("kpad", [128, 2, 2, 128], BF16)
        maskT = sb("maskT", [128, 128], BF16)
        A = sb("A", [128, 4, 128], F32)
        Sbf = sb("Sbf", [128, 4, 128], BF16)
        Sg = sb("Sg", [128, 2, 128], F32)
        Spbf = sb("Spbf", [128, 2, 128], BF16)
        gtmp = sb("gtmp", [128, 128], F32)
        o1 = sb("o1", [128, D], F32)
        o3 = [sb("o3%d" % i, [128, D], BF16) for i in range(2)]
        oT = sb("oT", [128, 8, 128], BF16)
        junk = sb("junk", [128, 128], BF16)
        ssq = sb("ssq", [128, 8], F32)
        msq = sb("msq", [128, 8], F32)
        rsq = sb("rsq", [128, 8], F32)
        PA = ps("PA", [128, 1024], F32)
        PB = ps("PB", [128, 1024], F32)
        PC = ps("PC", [128, 1024], F32)
        PD = ps("PD", [128, 1024], F32)
        Q = [PA[:, 0:512], PA[:, 512:1024], PB[:, 0:512], PB[:, 512:1024],
             PC[:, 0:512], PC[:, 512:1024], PD[:, 0:512], PD[:, 512:1024]]
        PT = PD[:, 0:512].bitcast(BF16)

        init_norm(sc, B, nrm_ap, C["ident"])
        w_in, w_out = W["w_in"], W["w_out"]
        w_in_v = w_in.rearrange("(k p) c -> p k c", p=128)
        EG = [(0, 512), (512, 1024), (2048, 2560), (3584, 3600), (1024, 2048), (2560, 3584)]
        egrp = {}

        def ld_grp(gi, c0, c1):
            sc.add("pool", lambda e: e.dma_start(out=win[:, :, c0:c1], in_=w_in_v[:, :, c0:c1]),
                   writes=[("win", gi)], dma="w%d" % gi)

        def ld_perm(which):
            for k in range(8):
                for half in range(2):
                    def ld(e, k=k, half=half):
                        o = winp[:, k, which * 512:(which + 1) * 512].rearrange(
                            "p (h two c) -> p h two c", two=2, c=64)[:, :, half, :]
                        i_ = w_in[k * 128:(k + 1) * 128, which * 512:(which + 1) * 512].rearrange(
                            "p (h two c) -> p h two c", two=2, c=64)[:, :, 1 - half, :]
                        return e.dma_start(out=o, in_=i_)
                    sc.add("pool", ld, writes=[("winp", which, k, half)], dma="wp%d" % which)
        ld_grp(0, *EG[0])
        ld_perm(0)
        ld_grp(1, *EG[1])
        ld_perm(1)
        for gi in range(2, 6):
            ld_grp(gi, *EG[gi])

        def wkeys(wt, c0):
            if wt is winp:
                wh = c0 // 512
                return [("winp", wh, k, hf) for k in range(8) for hf in range(2)]
            for gi, (a, b_) in enumerate(EG):
                if a <= c0 < b_:
                    return [("win", gi)]
            raise ValueError(c0)
        sc.add("sp", lambda e: e.dma_start(out=hcol[:, :], in_=W["head_norm"].rearrange("(k p) -> p k", p=128),
                                           allow_slow_non_contiguous=True), writes=["hcol"], dma="k3")
        for k in range(8):
            sc.add("sp", lambda e, k=k: e.dma_start(out=o1[:, :], in_=w_out[k * 128:(k + 1) * 128, :]),
                   writes=["o1"], dma="cst3")
            sc.add("dve", lambda e, k=k: e.tensor_scalar(wout[:, k, :], o1[:, :], hcol[:, k:k + 1], None, ALU.mult),
                   reads=["o1", "hcol"], writes=[("wout", k)])
        sc.add("sp", lambda e: e.dma_start(out=wlr[:, :], in_=W["w_lr_up"]), writes=["wlr"], dma="k4")
        sc.add("sp", lambda e: e.dma_start(out=nblr[:, :], in_=W["b_lr"].rearrange("(k p) -> p k", p=128),
                                           allow_slow_non_contiguous=True), writes=["nblr"], dma="k5")
        sc.add("dve", lambda e: e.tensor_scalar(nblr[:, :], nblr[:, :], -1.0, None, ALU.mult),
               reads=["nblr"], writes=["nblr"])
        sc.add("sp", lambda e: e.dma_start(out=dq[:, :, :], in_=C["dq"].partition_broadcast(128)),
               writes=["dq"], dma="k6")
        sc.add("sp", lambda e: e.dma_start(out=dk[:, :, :], in_=C["dk"].partition_broadcast(128)),
               writes=["dk"], dma="k7")
        sc.add("sp", lambda e: e.dma_start(out=rmask[:, :], in_=C["rmask"].partition_broadcast(128)),
               writes=["rmask"], dma="k8")
        sc.add("pool", lambda e: e.dma_start(out=maskT[:, :], in_=C["maskT"]), writes=["maskT"], dma="k10")
        sc.add("pool", lambda e: e.memset(A[:, :, :], 0.0), writes=["A"])
        sc.add("pool", lambda e: e.memset(Sbf[:, :, :], 0.0), writes=["Sbf"])
        sc.add("pool", lambda e: e.memset(Sg[:, :, :], 0.0), writes=["Sg"])
        sc.add("pool", lambda e: e.memset(kpad[:, :, :, :], 0.0), writes=["kpad"])
        sc.add("pool", lambda e: e.memset(gqz[:, :, :], 0.0), writes=[("gqT", 0), ("gqT", 1)])

        def proj_fm(out_ps, wt, c0, ncols, b, outkey, extra_reads=()):
            def mm(pe):
                ins = None
                for k in range(8):
                    ins = pe.matmul(out_ps, wt[:, k, c0:c0 + ncols], xnT[b][:, k, :],
                                    start=(k == 0), stop=(k == 7))
                return ins
            sc.add("pe", mm, reads=wkeys(wt, c0) + [("xnT", b, j) for j in range(4)] + list(extra_reads), writes=[outkey])

        def proj_tm(out_ps, c0, ncols, b, j, outkeys):
            def mm(pe):
                ins = None
                for n0 in range(0, ncols, 512):
                    for k in range(8):
                        ins = pe.matmul(out_ps[:, n0:n0 + 512], xnT[b][:, k, j * 128:(j + 1) * 128],
                                        win[:, k, c0 + n0:c0 + n0 + 512], start=(k == 0), stop=(k == 7))
                return ins
            sc.add("pe", mm, reads=wkeys(win, c0) + [("xnT", b, j)], writes=list(outkeys))

        def phase_b(t):
            b = t % 2
            c0 = t * T
            sc.add("sp", lambda e: e.dma_start(out=cst[:, :], in_=C["cos"][:, c0:c0 + T]),
                   writes=["cst"], dma="cs0")
            sc.add("sp", lambda e: e.dma_start(out=snt[:, :], in_=C["sin"][:, c0:c0 + T]),
                   writes=["snt"], dma="cs1")
            n = 0
            for which in range(2):
                for h in range(4):
                    qa, qb = (Q[0], Q[1]) if n % 2 == 0 else (Q[2], Q[3])
                    ka, kb = (("Q", 0), ("Q", 1)) if n % 2 == 0 else (("Q", 2), ("Q", 3))
                    n += 1
                    col = which * 512 + h * 128
                    proj_fm(qa, win, col, 128, b, ka)
                    proj_fm(qb, winp, col, 128, b, kb)
                    sc.add("dve", lambda e, qa=qa: e.tensor_tensor(t1[:, :], qa, cst[:, :], ALU.mult),
                           reads=[ka, "cst"], writes=["t1"])
                    sc.add("dve", lambda e, qb=qb: e.tensor_tensor(t2[:, :], qb, snt[:, :], ALU.mult),
                           reads=[kb, "snt"], writes=["t2"])
                    sc.add("pool", lambda e: e.tensor_tensor(t1[:, :], t1[:, :], t2[:, :], ALU.add),
                           reads=["t1", "t2"], writes=["t1"])
                    dst_t = qT if which == 0 else kT
                    dec = dq if which == 0 else dk
                    dkey = "dq" if which == 0 else "dk"
                    okey = ("qT", h) if which == 0 else ("kT", h)
                    sc.add("pool", lambda e, dst_t=dst_t, dec=dec, h=h: e.tensor_tensor(
                        dst_t[:, h, :].rearrange("p (c l) -> p c l", c=4),
                        t1[:, :].rearrange("p (c l) -> p c l", c=4),
                        dec[:, h, :].unsqueeze(1).broadcast_to([128, 4, 128]), ALU.mult),
                        reads=["t1", dkey], writes=[okey])
            proj_fm(Q[7][0:16, :], win, 3584, 16, b, ("Q", 7))
            sc.add("act", lambda e: e.activation(glrT[:, :], Q[7][0:16, :], AF.Copy),
                   reads=[("Q", 7)], writes=["glrT"])
            for p in range(2):
                sc.add("pe", lambda pe, p=p: pe.matmul(Q[7], wlr[:, p * 128:(p + 1) * 128], glrT[:, :],
                                                       start=True, stop=True),
                       reads=["wlr", "glrT"], writes=[("Q", 7)])
                sc.add("act", lambda e, p=p: e.activation(ea[:, :], Q[7], AF.Exp, bias=nblr[:, p:p + 1], scale=-1.0),
                       reads=[("Q", 7), "nblr"], writes=["ea"])
                sc.add("act", lambda e: e.activation(ea[:, :], ea[:, :], AF.Ln, bias=1.0),
                       reads=["ea"], writes=["ea"])
                sc.add("dve", lambda e: e.tensor_tensor_scan(cs[:, :], rmask[:, :], ea[:, :], 0.0, ALU.mult, ALU.add),
                       reads=["ea", "rmask"], writes=["cs"])
                cs3 = cs[:, :].rearrange("p (c l) -> p c l", c=4)
                sc.add("act", lambda e, p=p, cs3=cs3: e.activation(emid[:, p, :], cs3[:, :, 63], AF.Exp, scale=-1.0 / 16),
                       reads=["cs"], writes=[("emid", p)])
                sc.add("act", lambda e, p=p, cs3=cs3: e.activation(eend[:, p, :], cs3[:, :, 127], AF.Exp, scale=-1.0 / 16),
                       reads=["cs"], writes=[("eend", p)])
                sc.add("dve", lambda e, p=p, cs3=cs3: e.tensor_tensor(dcs[:, p, :], cs3[:, :, 63], cs3[:, :, 127], ALU.subtract),
                       reads=["cs"], writes=[("dcs", p)])
                sc.add("act", lambda e, p=p: e.activation(eem[:, p, :], dcs[:, p, :], AF.Exp, scale=1.0 / 16),
                       reads=[("dcs", p)], writes=[("eem", p)])
                sc.add("dve", lambda e, p=p, cs3=cs3: e.tensor_copy(csm[:, p, :], cs3[:, :, 63]),
                       reads=["cs"], writes=[("csm", p)])
                sc.add("dve", lambda e, p=p, cs3=cs3: e.tensor_tensor(
                    cs3, cs3, csm[:, p, :].unsqueeze(2).broadcast_to([128, 4, 128]), ALU.subtract),
                    reads=["cs", ("csm", p)], writes=["cs"])
                sc.add("act", lambda e: e.activation(eqt[:, :], cs[:, :], AF.Exp, scale=-1.0 / 16),
                       reads=["cs"], writes=["eqt"])
                sc.add("act", lambda e: e.activation(ekt[:, :], cs[:, :], AF.Exp, scale=1.0 / 16, bias=LN_GK),
                       reads=["cs"], writes=["ekt"])
                proj_fm(Q[4], win, 2048 + p * 128, 128, b, ("Q", 4))
                proj_fm(Q[5], win, 2304 + p * 128, 128, b, ("Q", 5))
                for hh in range(2):
                    rs = slice(hh * 64, (hh + 1) * 64)
                    sc.add("dve", lambda e, p=p, hh=hh, rs=rs: e.tensor_tensor(gqz[rs, 2 * p + hh, :], Q[4][rs, :],
                                                                             eqt[rs, :], ALU.mult),
                           reads=[("Q", 4), "eqt"], writes=[("gqT", p)])
                sc.add("dve", lambda e, p=p: e.tensor_tensor(gkT[:, p, :], Q[5], ekt[:, :], ALU.mult),
                       reads=[("Q", 5), "ekt"], writes=[("gkT", p)])

        def chunk_front(t, c):
            b = t % 2
            cl = slice(c * 128, (c + 1) * 128)
            proj_tm(PA, 1024, 512, b, c, [("Q", 0)])
            sc.add("act", lambda e: e.activation(vr[:, :], Q[0], AF.Copy), reads=[("Q", 0)], writes=["vr"])
            proj_tm(PA[:, 512:1024], 2560, 512, b, c, [("Q", 1)])
            sc.add("act", lambda e: e.activation(vg[:, :], Q[1], AF.Copy), reads=[("Q", 1)], writes=["vg"])
            proj_tm(PB[:, 0:512], 1536, 512, b, c, [("Q", 2)])
            proj_tm(PB[:, 512:1024], 3072, 512, b, c, [("Q", 3)])
            sc.add("act", lambda e: e.activation(gate[:, :], PB[:, :], AF.Silu),
                   reads=[("Q", 2), ("Q", 3)], writes=["gate"])

            def mm_s(pe):
                ins = None
                for h in range(4):
                    ins = pe.matmul(Q[7][:, h * 128:(h + 1) * 128], kT[:, h, cl], qT[:, h, cl], start=True, stop=True)
                return ins
            sc.add("pe", mm_s, reads=[("qT", h) for h in range(4)] + [("kT", h) for h in range(4)], writes=[("Q", 7)])
            m4 = maskT[:, :].unsqueeze(1).broadcast_to([128, 4, 128])
            sc.add("dve", lambda e: e.tensor_tensor(sT[:, :].rearrange("p (h l) -> p h l", h=4),
                                                    Q[7].rearrange("p (h l) -> p h l", h=4), m4, ALU.mult),
                   reads=[("Q", 7), "maskT"], writes=["sT"])

            def mm_sg(pe):
                ins = None
                for h in range(4):
                    p, hh = h // 2, h % 2
                    ins = pe.matmul(Q[1][:, h * 128:(h + 1) * 128], gkT[:, p, cl], gqz[:, h, cl],
                                    start=True, stop=True)
                return ins
            sc.add("pe", mm_sg, reads=[("gqT", 0), ("gqT", 1), ("gkT", 0), ("gkT", 1)], writes=[("Q", 1)])
            sc.add("dve", lambda e: e.tensor_tensor(sTg[:, :].rearrange("p (h l) -> p h l", h=4),
                                                    Q[1].rearrange("p (h l) -> p h l", h=4), m4, ALU.mult),
                   reads=[("Q", 1), "maskT"], writes=["sTg"])

            def tr(pe):
                ins = None
                for h in range(4):
                    ins = pe.transpose(PT[:, h * 128:(h + 1) * 128], kT[:, h, cl], ident[:, :])
                for p in range(2):
                    ins = pe.transpose(PT[:, 512 + p * 128:512 + (p + 1) * 128], gkT[:, p, cl], ident[:, :])
                return ins
            sc.add("pe", tr, reads=[("kT", h) for h in range(4)] + [("gkT", 0), ("gkT", 1), "ident"], writes=["PT"])
            sc.add("dve", lambda e: e.tensor_copy(ktok[:, :], PT[:, 0:512]), reads=["PT"], writes=["ktok"])
            kp_ap = bass.AP(kpad, 0, [[512, 128], [256, 2], [192, 2], [1, 64]])
            sc.add("dve", lambda e: e.tensor_copy(kp_ap, PT[:, 512:768].rearrange("p (a b c) -> p a b c", a=2, b=2)),
                   reads=["PT"], writes=["kpad"])
            for p in range(2):
                sc.add("act", lambda e, p=p: e.activation(Spbf[:, p, :], Sg[:, p, :], AF.Copy,
                                                          scale=emid[:, p, c:c + 1]),
                       reads=[("Sg", p), ("emid", p)], writes=[("Spbf", p)])

            def mm_o(pe):
                ins = None
                for h in range(4):
                    hs = slice(h * 128, (h + 1) * 128)
                    pe.matmul(PC[:, hs], sT[:, hs], vr[:, hs], start=True, stop=False)
                    ins = pe.matmul(PC[:, hs], qT[:, h, cl], Sbf[:, h, :], start=False, stop=True)
                for h in range(4):
                    p, hh = h // 2, h % 2
                    hs = slice(h * 128, (h + 1) * 128)
                    og = slice(512 + h * 128, 512 + (h + 1) * 128)
                    pe.matmul(PC[:, og], sTg[:, hs], vg[:, hs], start=True, stop=False)
                    ins = pe.matmul(PC[:, og], gqz[:, h, cl], Spbf[:, p, :], start=False, stop=True)
                return ins
            sc.add("pe", mm_o, reads=["sT", "sTg", "vr", "vg", "Sbf", ("Spbf", 0), ("Spbf", 1)]
                   + [("qT", h) for h in range(4)] + [("gqT", 0), ("gqT", 1)], writes=[("Q", 4), ("Q", 5)])

            def mm_u(pe):
                ins = None
                for h in range(4):
                    hs = slice(h * 128, (h + 1) * 128)
                    ins = pe.matmul(Q[7][:, hs], ktok[:, hs], vr[:, hs], start=True, stop=True)
                return ins
            sc.add("pe", mm_u, reads=["ktok", "vr", "sT"], writes=[("Q", 7)])
            for h in range(4):
                hs = slice(h * 128, (h + 1) * 128)
                sc.add("dve", lambda e, h=h, hs=hs: e.scalar_tensor_tensor(A[:, h, :], A[:, h, :], GL[h], Q[7][:, hs],
                                                                       ALU.mult, ALU.add),
                       reads=[("Q", 7), "A"], writes=["A"])
                sc.add("act", lambda e, h=h: e.activation(Sbf[:, h, :], A[:, h, :], AF.Copy, scale=GL[h]),
                       reads=["A"], writes=["Sbf"])

            def mm_ug(pe):
                ins = None
                for p in range(2):
                    pe.matmul(Q[0][:, p * 128:(p + 1) * 128], kpad[:, p, 0, :], vg[:, (2 * p) * 128:(2 * p + 1) * 128],
                              start=True, stop=False)
                    ins = pe.matmul(Q[0][:, p * 128:(p + 1) * 128], kpad[:, p, 1, :],
                                    vg[:, (2 * p + 1) * 128:(2 * p + 2) * 128], start=False, stop=True)
                return ins
            sc.add("pe", mm_ug, reads=["kpad", "vg"], writes=[("Q", 0)])
            for p in range(2):
                sc.add("dve", lambda e, p=p: e.tensor_scalar(gtmp[:, :], Q[0][:, p * 128:(p + 1) * 128],
                                                             eem[:, p, c:c + 1], None, ALU.mult),
                       reads=[("Q", 0), ("eem", p)], writes=["gtmp"])
                sc.add("dve", lambda e, p=p: e.scalar_tensor_tensor(Sg[:, p, :], Sg[:, p, :], eend[:, p, c:c + 1],
                                                                    gtmp[:, :], ALU.mult, ALU.add),
                       reads=["gtmp", ("Sg", p), ("eend", p)], writes=[("Sg", p)])

            for h8 in range(8):
                hs = slice(h8 * 128, (h8 + 1) * 128)
                sc.add("act", lambda e, h8=h8, hs=hs: e.activation(junk[:, :], PC[:, hs], AF.Square,
                                                                   accum_out=ssq[:, h8:h8 + 1]),
                       reads=[("Q", 4), ("Q", 5)], writes=["junk", "ssq"])
            _rstd_ops(sc, ssq[:, :], rsq[:, :], nhalf[:, 0:1].broadcast_to([128, 8]), 128.0, "ssq", "rsq", "msq", msq[:, :])
            sc.add("dve", lambda e: e.tensor_tensor(o1[:, :].rearrange("p (h v) -> p h v", h=8),
                                                    PC[:, :].rearrange("p (h v) -> p h v", h=8),
                                                    rsq[:, :].unsqueeze(2).broadcast_to([128, 8, 128]), ALU.mult),
                   reads=[("Q", 4), ("Q", 5), "rsq"], writes=["o1"])
            g = t * 4 + c
            ob = g % 2
            sc.add("pool", lambda e: e.tensor_tensor(o3[ob][:, :], o1[:, :], gate[:, :], ALU.mult),
                   reads=["o1", "gate"], writes=[("o3", ob)])

        def chunk_back(t, c):
            g = t * 4 + c
            ob = g % 2
            sl = g % 2
            r0 = g * 128
            sc.add("sp", lambda e: e.dma_start(out=xres[sl][:, :], in_=src[r0:r0 + 128, :]),
                   reads=[(srcname, g)], writes=[("xres", sl)], dma="xres%d" % sl)

            def tr(pe):
                ins = None
                for m in range(8):
                    ins = pe.transpose(PT[:, m * 128:(m + 1) * 128], o3[ob][:, m * 128:(m + 1) * 128], ident[:, :])
                return ins
            sc.add("pe", tr, reads=[("o3", ob), "ident"], writes=["PT"])
            sc.add("act", lambda e: e.activation(oT[:, :, :].rearrange("p m t -> p (m t)"), PT[:, :], AF.Copy),
                   reads=["PT"], writes=["oT"])
            for dh in range(2):
                def mm(pe, dh=dh):
                    ins = None
                    for m in range(8):
                        ins = pe.matmul(Q[2 + dh], oT[:, m, :], wout[:, m, dh * 512:(dh + 1) * 512],
                                        start=(m == 0), stop=(m == 7))
                    return ins
                sc.add("pe", mm, reads=["oT"] + [("wout", m) for m in range(8)], writes=[("Q", 2 + dh)])
                sc.add("dve", lambda e, dh=dh: e.tensor_tensor(xres[sl][:, dh * 512:(dh + 1) * 512], Q[2 + dh],
                                                               xres[sl][:, dh * 512:(dh + 1) * 512], ALU.add),
                       reads=[("Q", 2 + dh), ("xres", sl)], writes=[("xres", sl)])
            sc.add("sp", lambda e: e.dma_start(out=dst[r0:r0 + 128, :], in_=xres[sl][:, :]),
                   reads=[("xres", sl)], writes=[(dstname, g)], dma="xst%d" % sl)

        for j in range(4):
            norm_sub(sc, B, src, srcname, PT, "PT", 0, j)
        pending = None
        for t in range(NT):
            phase_b(t)
            for c in range(4):
                chunk_front(t, c)
                if pending is not None:
                    chunk_back(*pending)
                pending = (t, c)
                if t + 1 < NT:
                    norm_sub(sc, B, src, srcname, PT, "PT", t + 1, c)
        chunk_back(*pending)
        sc.barrier()
        sc.flush()


GELU_C = 2.0 * 0.7978845608028654


def odd_phase(cx, src, dst, srcname, dstname, nrm_ap, W, C):
    nc, sc, S = cx.nc, cx.sc, cx.S
    T = 512
    NT = S // T
    tg = cx.tag()
    with contextlib.ExitStack() as st:
        def sb(name, shape, dt):
            return st.enter_context(nc.sbuf_tensor(name + tg, shape, dt))

        def ps(name, shape, dt):
            return st.enter_context(nc.psum_tensor(name + tg, shape, dt))

        win = sb("win", [128, 8, 2568], BF16)
        wout = sb("wout", [128, 8, D], BF16)
        wbd = sb("wbd", [128, 2, 4, 128], BF16)
        wqk = sb("wqk", [128, 2, 4, 128], BF16)
        B = alloc_norm(sb, T, nld=1)
        xnT = B["xnT"]
        nhalf = B["nhalf"]
        ident = B["ident"]
        identF = sb("identF", [128, 128], F32)
        sel = sb("sel", [4, 4, 128], F32)
        xres = [sb("xres%d" % i, [128, D], F32) for i in range(2)]
        ncol = sb("ncol", [128, 4], F32)
        lcw = sb("lcw", [128, 4, 4], F32)
        mcw = sb("mcw", [128, 4, 4], F32)
        pcol = sb("pcol", [128, 6, 4], F32)
        sp8 = sb("sp8", [128, 4], F32)
        sp16 = sb("sp16", [128, 4], F32)
        hb = sb("hb", [128, 2, 4], F32)
        hsp = sb("hsp", [128, 2, 4], F32)
        phalf = sb("phalf", [128, 1], F32)
        sw = [sb("sw%d" % i, [128, 4], F32) for i in range(6)]
        gb = sb("gb", [4, 2], F32)
        ngb = sb("ngb", [4, 1], F32)
        lxbuf = sb("lxbuf", [128, 4, 3 + T], F32)
        mubuf = sb("mubuf", [128, 4, 3 + T], F32)
        tA2 = [sb("tA%d" % i, [128, T], F32) for i in range(2)]
        tB2 = [sb("tB%d" % i, [128, T], F32) for i in range(2)]
        tC2 = [sb("tC%d" % i, [128, T], F32) for i in range(2)]
        tD2 = [sb("tD%d" % i, [128, T], F32) for i in range(2)]
        tE1 = sb("tE", [128, T], F32)
        tE2 = [tE1, tE1]
        tF2 = [sb("tF%d" % i, [128, T], F32) for i in range(2)]
        mA = sb("mA", [128, T], F32)
        mB = sb("mB", [128, T], F32)
        lxcb1 = sb("lxcb", [128, T], BF16)
        lxcb2 = [lxcb1, lxcb1]
        lruT2 = [sb("lruT%d" % i, [128, 4, T], BF16) for i in range(2)]
        hst = sb("hst", [128, 4], F32)
        mcb = sb("mcb", [128, 4, T], BF16)
        qT2 = [sb("qT%d" % i, [128, 4, T], BF16) for i in range(2)]
        kT2 = [sb("kT%d" % i, [128, 4, T], BF16) for i in range(2)]
        rmask = sb("rmask", [4, T], F32)
        rneg = sb("rneg", [4, T], F32)
        g_ = sb("g_", [4, T], F32)
        spf = sb("spf", [4, T], F32)
        csf = sb("csf", [4, T], F32)
        cm = sb("cm", [4, T], F32)
        wr = spf
        f1r = sb("f1r", [4, T], F32)
        f2r = sb("f2r", [4, T], F32)
        enr = sb("enr", [4, T], F32)
        dtm = sb("dtm", [4, T], F32)
        mall = sb("mall", [4, 8], F32)
        mx = sb("mx", [4, 4], F32)
        ncse = sb("ncse", [4, 4], F32)
        amax = sb("amax", [4, 4], F32)
        gx = sb("gx", [4, 8], F32)
        gt = sb("gt", [4, 4], F32)
        tcol2 = [sb("tcol%d" % i, [128, 4, 4, 4], F32) for i in range(2)]
        gsc2 = [sb("gsc%d" % i, [128, 4, 8], F32) for i in range(2)]
        vtok = sb("vtok", [128, 512], BF16)
        sigo = sb("sigo", [128, 512], F32)
        onec = sb("onec", [128, 1], BF16)
        maskT = sb("maskT", [128, 128], BF16)
        pT = sb("pT", [128, 4, 128], BF16)
        kw = sb("kw", [128, 4, 128], BF16)
        Cst = sb("Cst", [128, 4, 128], F32)
        Cbf = sb("Cbf", [128, 4, 128], BF16)
        nst = sb("nst", [128, 4], F32)
        nbf = sb("nbf", [128, 4], BF16)
        tm1 = sb("tm1", [128, 512], F32)
        tm2 = sb("tm2", [128, 512], F32)
        tm3 = tm2
        wcb = sb("wcb", [128, 4], BF16)
        f12 = sb("f12", [128, 8], F32)
        tn2 = sb("tn2", [128, 4], F32)
        tn = sb("tn", [128, 4], F32)
        dn = sb("dn", [128, 4], F32)
        dn2 = sb("dn2", [128, 4], F32)
        rden = sb("rden", [128, 4], F32)
        hm = sb("hm", [128, 512], F32)
        junk = sb("junk", [128, 128], BF16)
        ssq = sb("ssq", [128, 4], F32)
        msq = sb("msq", [128, 4], F32)
        rsq = sb("rsq", [128, 4], F32)
        o3 = [sb("o3%d" % i, [128, 512], BF16) for i in range(2)]
        oTm = sb("oTm", [128, 4, 128], BF16)
        PA = ps("PA", [128, 1024], F32)
        PB = ps("PB", [128, 1024], F32)
        PC = ps("PC", [128, 1024], F32)
        PD = ps("PD", [128, 1024], F32)
        Q = [PA[:, 0:512], PA[:, 512:1024], PB[:, 0:512], PB[:, 512:1024],
             PC[:, 0:512], PC[:, 512:1024], PD[:, 0:512], PD[:, 512:1024]]
        PT = PD[:, 0:512].bitcast(BF16)

        init_norm(sc, B, nrm_ap, C["ident"])
        w_in, w_out = W["w_in"], W["w_out"]
        kc = [0]

        def kch():
            kc[0] += 1
            return "ok%d" % kc[0]

        def cload(eng, out_ap, in_ap, key, **kw_):
            sc.add(eng, lambda e: e.dma_start(out=out_ap, in_=in_ap, **kw_), writes=[key], dma=kch())

        w_in_v = w_in.rearrange("(k p) c -> p k c", p=128)
        OG = [(0, 1024), (1024, 1536), (2560, 2568), (1536, 2560)]
        for gi, (c0_, c1_) in enumerate(OG):
            sc.add("pool", lambda e, c0_=c0_, c1_=c1_: e.dma_start(out=win[:, :, c0_:c1_], in_=w_in_v[:, :, c0_:c1_]),
                   writes=[("win", gi)], dma="w%d" % gi)

        def wkeys(c0):
            for gi, (a, b_) in enumerate(OG):
                if a <= c0 < b_:
                    return [("win", gi)]
            raise ValueError(c0)
        sc.add("pool", lambda e: e.memset(wbd[:, :, :, :], 0.0), writes=["wbd"])
        for ai, nm in enumerate(("lru_wa", "lru_wx")):
            for bb in range(2):
                sc.add("pool", lambda e, ai=ai, nm=nm, bb=bb: e.dma_start(
                    out=wbd[bb * 64:(bb + 1) * 64, ai, :, bb * 64:(bb + 1) * 64],
                    in_=W[nm].rearrange("(c two) i j -> two i c j", two=2)[bb]),
                    reads=["wbd"], writes=[("wbdb", ai, bb)], dma="wb%d" % (ai * 2 + bb))
        for ai, nm in enumerate(("ml_wq", "ml_wk")):
            sc.add("pool", lambda e, ai=ai, nm=nm: e.dma_start(out=wqk[:, ai, :, :], in_=W[nm].rearrange("h i j -> i h j")),
                   writes=[("wqk", ai)], dma="wq%d" % ai)
        cload("pool", identF[:, :], C["identF"], "identF")
        cload("pool", sel[:, :, :], C["sel"], "sel")
        cload("pool", maskT[:, :], C["maskT"], "maskT")
        cload("sp", ncol[:, :], W["ml_norm"].rearrange("(k p) -> p k", p=128), "ncol", allow_slow_non_contiguous=True)
        for j in range(4):
            cload("sp", lcw[:, j, :], W["lru_conv_w"][j].rearrange("(k p) -> p k", p=128), ("lcw", j),
                  allow_slow_non_contiguous=True)
            cload("sp", mcw[:, j, :], W["ml_conv_w"][j].rearrange("(k p) -> p k", p=128), ("mcw", j),
                  allow_slow_non_contiguous=True)
        for qi, nm in enumerate(("lru_conv_b", "ml_conv_b", "lru_ba", "lru_bx", "lru_lambda")):
            cload("sp", pcol[:, qi, :], W[nm].rearrange("(k p) -> p k", p=128), ("pcol", qi),
                  allow_slow_non_contiguous=True)
        cload("sp", gb[:, 0:1], W["ml_bi"].rearrange("(h o) -> h o", o=1), ("gb", 0))
        cload("sp", gb[:, 1:2], W["ml_bf"].rearrange("(h o) -> h o", o=1), ("gb", 1))
        sc.add("dve", lambda e: e.tensor_scalar(ngb[:, :], gb[:, 1:2], -1.0, None, ALU.mult),
               reads=[("gb", 1)], writes=["ngb"])
        cload("sp", rmask[:, :], C["rmask"].partition_broadcast(4), "rmask")
        cload("sp", rneg[:, :], C["rneg"].partition_broadcast(4), "rneg")
        for k in range(8):
            sc.add("sp", lambda e, k=k: e.dma_start(out=hm[:, :].rearrange("p (a b) -> p a b", a=1)[:, 0, :],
                                                    in_=w_out[k * 128:(k + 1) * 128, 0:512]),
                   writes=["hm"], dma="cst3")
            sc.add("sp", lambda e, k=k: e.dma_start(out=sigo[:, :], in_=w_out[k * 128:(k + 1) * 128, 512:1024]),
                   writes=["sigo"], dma="cst4")
            if k < 4:
                sc.add("act", lambda e, k=k: e.activation(wout[:, k, 0:512], hm[:, :], AF.Copy, scale=0.5),
                       reads=["hm"], writes=[("wout", k)])
                sc.add("act", lambda e, k=k: e.activation(wout[:, k, 512:1024], sigo[:, :], AF.Copy, scale=0.5),
                       reads=["sigo"], writes=[("wout", k)])
            else:
                sc.add("dve", lambda e, k=k: e.tensor_scalar(wout[:, k, 0:512], hm[:, :], ncol[:, k - 4:k - 3], None,
                                                             ALU.mult), reads=["hm", "ncol"], writes=[("wout", k)])
                sc.add("dve", lambda e, k=k: e.tensor_scalar(wout[:, k, 512:1024], sigo[:, :], ncol[:, k - 4:k - 3],
                                                             None, ALU.mult), reads=["sigo", "ncol"], writes=[("wout", k)])
        lam = pcol[:, 4, :]
        s0, s1, s2, s3, s4, s5 = [t_[:, :] for t_ in sw]
        sc.add("dve", lambda e: e.tensor_scalar(s0, lam, -1.0, None, ALU.mult), reads=[("pcol", 4)], writes=["s0"])
        sc.add("dve", lambda e: e.tensor_tensor(s0, s0, lam, ALU.max), reads=[("pcol", 4), "s0"], writes=["s0"])
        sc.add("act", lambda e: e.activation(s1, s0, AF.Exp, scale=-1.0), reads=["s0"], writes=["s1"])
        sc.add("dve", lambda e: e.tensor_scalar(s2, s1, 2.0, None, ALU.add), reads=["s1"], writes=["s2"])
        sc.add("dve", lambda e: e.reciprocal(s2, s2), reads=["s2"], writes=["s2"])
        sc.add("dve", lambda e: e.tensor_tensor(s2, s2, s1, ALU.mult), reads=["s2", "s1"], writes=["s2"])
        sc.add("dve", lambda e: e.tensor_tensor(s3, s2, s2, ALU.mult), reads=["s2"], writes=["s3"])
        sc.add("dve", lambda e: e.memset(s4, 1.0 / 15.0), writes=["s4"])
        for kk in (13, 11, 9, 7, 5, 3, 1):
            sc.add("dve", lambda e: e.tensor_tensor(s4, s4, s3, ALU.mult), reads=["s4", "s3"], writes=["s4"])
            sc.add("dve", lambda e, kk=kk: e.tensor_scalar(s4, s4, 1.0 / kk, None, ALU.add), reads=["s4"], writes=["s4"])
        sc.add("dve", lambda e: e.tensor_tensor(s4, s4, s2, ALU.mult), reads=["s4", "s2"], writes=["s4"])
        sc.add("dve", lambda e: e.tensor_scalar(s5, lam, -1.0, 0.0, ALU.mult, ALU.max), reads=[("pcol", 4)], writes=["s5"])
        sc.add("dve", lambda e: e.scalar_tensor_tensor(s5, s4, 2.0, s5, ALU.mult, ALU.add), reads=["s4", "s5"], writes=["s5"])
        sc.add("dve", lambda e: e.tensor_scalar(sp8[:, :], s5, -8.0, None, ALU.mult), reads=["s5"], writes=["sp8"])
        sc.add("dve", lambda e: e.tensor_scalar(sp16[:, :], s5, -16.0, None, ALU.mult), reads=["s5"], writes=["sp16"])
        sc.add("dve", lambda e: e.tensor_scalar(hsp[:, 0, :], s5, -4.0, None, ALU.mult), reads=["s5"], writes=["hsp"])
        sc.add("dve", lambda e: e.tensor_scalar(hsp[:, 1, :], s5, -8.0, None, ALU.mult), reads=["s5", "hsp"], writes=["hsp"])
        sc.add("dve", lambda e: e.tensor_scalar(hb[:, 0, :], pcol[:, 2, :], 0.5, None, ALU.mult), reads=[("pcol", 2)], writes=["hb"])
        sc.add("dve", lambda e: e.tensor_scalar(hb[:, 1, :], pcol[:, 3, :], 0.5, None, ALU.mult), reads=[("pcol", 3), "hb"], writes=["hb"])
        sc.add("pool", lambda e: e.memset(phalf[:, :], 0.5), writes=["phalf"])
        sc.add("pool", lambda e: e.memset(lxbuf[:, :, 0:3], 0.0), writes=["lxbuf"])
        sc.add("pool", lambda e: e.memset(mubuf[:, :, 0:3], 0.0), writes=["mubuf"])
        sc.add("pool", lambda e: e.memset(hst[:, :], 0.0), writes=["hst"])
        sc.add("pool", lambda e: e.memset(mall[:, :], 0.0), writes=["mall"])
        sc.add("pool", lambda e: e.memset(Cst[:, :, :], 0.0), writes=["Cst"])
        sc.add("pool", lambda e: e.memset(Cbf[:, :, :], 0.0), writes=["Cbf"])
        sc.add("pool", lambda e: e.memset(nst[:, :], 0.0), writes=["nst"])
        sc.add("pool", lambda e: e.memset(nbf[:, :], 0.0), writes=["nbf"])
        sc.add("pool", lambda e: e.memset(onec[:, :], 1.0), writes=["onec"])

        def proj_fm(out_ps, c0, ncols, b, outkey):
            def mm(pe):
                ins = None
                for k in range(8):
                    ins = pe.matmul(out_ps, win[:, k, c0:c0 + ncols], xnT[b][:, k, :], start=(k == 0), stop=(k == 7))
                return ins
            sc.add("pe", mm, reads=wkeys(c0) + [("xnT", b, j) for j in range(4)], writes=[outkey])

        def proj_tm(out_ps, c0, b, j, outkey):
            def mm(pe):
                ins = None
                for k in range(8):
                    ins = pe.matmul(out_ps, xnT[b][:, k, j * 128:(j + 1) * 128], win[:, k, c0:c0 + 512],
                                    start=(k == 0), stop=(k == 7))
                return ins
            sc.add("pe", mm, reads=wkeys(c0) + [("xnT", b, j)], writes=[outkey])

        def conv_pool(buf, c, taps, bias_ap, out_ap, tmp_ap, bufkey, outkey, tapkey, biaskey, tmpkey):
            sc.add("pool", lambda e: e.tensor_scalar(out_ap, buf[:, c, 0:T], taps[:, 0, c:c + 1], bias_ap, ALU.mult, ALU.add),
                   reads=[bufkey, (tapkey, 0), biaskey], writes=[outkey])
            for j in range(1, 4):
                sc.add("pool", lambda e, j=j: e.tensor_scalar(tmp_ap, buf[:, c, j:j + T], taps[:, j, c:c + 1], None, ALU.mult),
                       reads=[bufkey, (tapkey, j)], writes=[tmpkey])
                sc.add("pool", lambda e: e.tensor_tensor(out_ap, out_ap, tmp_ap, ALU.add),
                       reads=[outkey, tmpkey], writes=[outkey])

        def conv(buf, c, taps, bias_ap, out_ap, bufkey, outkey, tapkey, biaskey):
            sc.add("dve", lambda e: e.tensor_scalar(out_ap, buf[:, c, 0:T], taps[:, 0, c:c + 1], bias_ap, ALU.mult, ALU.add),
                   reads=[bufkey, (tapkey, 0), biaskey], writes=[outkey])
            for j in range(1, 4):
                sc.add("dve", lambda e, j=j: e.scalar_tensor_tensor(out_ap, buf[:, c, j:j + T], taps[:, j, c:c + 1], out_ap,
                                                                    ALU.mult, ALU.add),
                       reads=[bufkey, outkey, (tapkey, j)], writes=[outkey])

        def lru_piece(t, c):
            b = t % 2
            lruT, qT, kT, tcol, gsc = lruT2[b], qT2[b], kT2[b], tcol2[b], gsc2[b]
            pp = c % 2
            tA, tB, tC, tD, tE, tF, lxcb = [x[pp] for x in (tA2, tB2, tC2, tD2, tE2, tF2, lxcb2)]
            if True:
                qa = Q[(2 * c) % 4]
                qk = ("Q", (2 * c) % 4)
                qb = Q[(2 * c + 1) % 4]
                qbk = ("Q", (2 * c + 1) % 4)
                proj_fm(qa, 512 + c * 128, 128, b, qk)
                sc.add("act", lambda e, c=c, qa=qa: e.activation(lxbuf[:, c, 3:3 + T], qa, AF.Copy),
                       reads=[qk], writes=[("lxbuf", c)])
                proj_fm(qb, c * 128, 128, b, qbk)
                sc.add("act", lambda e, qb=qb: e.activation(tF[:, :], qb, AF.Copy), reads=[qbk], writes=[("tF", pp)])
                conv(lxbuf, c, lcw, pcol[:, 0, c:c + 1], tA[:, :], ("lxbuf", c), ("tA", pp), "lcw", ("pcol", 0))
                sc.add("pool", lambda e, c=c: e.tensor_copy(lxbuf[:, c, 0:3], lxbuf[:, c, T:T + 3]),
                       reads=[("tA", pp), ("lxbuf", c)], writes=[("lxbuf", c)])
                sc.add("act", lambda e: e.activation(lxcb[:, :], tA[:, :], AF.Copy), reads=[("tA", pp)], writes=["lxcb"])
                for ai in range(2):
                    sc.add("pe", lambda pe, ai=ai, c=c: pe.matmul(Q[4 + ai], wbd[:, ai, c, :], lxcb[:, :],
                                                                   start=True, stop=True),
                           reads=["lxcb", "wbd"] + [("wbdb", a_, bb) for a_ in range(2) for bb in range(2)],
                           writes=[("Q", 4 + ai)])
                sc.add("act", lambda e, c=c: e.activation(tB[:, :], Q[4], AF.Tanh, bias=hb[:, 0, c:c + 1], scale=0.5),
                       reads=[("Q", 4), "hb"], writes=[("tB", pp)])
                sc.add("act", lambda e, c=c: e.activation(tC[:, :], Q[5], AF.Tanh, bias=hb[:, 1, c:c + 1], scale=0.5),
                       reads=[("Q", 5), "hb"], writes=[("tC", pp)])
                sc.add("dve", lambda e: e.scalar_tensor_tensor(tC[:, :], tC[:, :], 1.0, tA[:, :], ALU.add, ALU.mult),
                       reads=[("tC", pp), ("tA", pp)], writes=[("tC", pp)])
                sc.add("act", lambda e: e.activation(tA[:, :], tF[:, :], AF.Square), reads=[("tF", pp), ("tC", pp)], writes=[("tA", pp)])
                sc.add("dve", lambda e: e.tensor_scalar(tA[:, :], tA[:, :], 0.044715, 1.0, ALU.mult, ALU.add),
                       reads=[("tA", pp)], writes=[("tA", pp)])
                sc.add("pool", lambda e: e.tensor_tensor(tA[:, :], tA[:, :], tF[:, :], ALU.mult),
                       reads=[("tA", pp), ("tF", pp)], writes=[("tA", pp)])
                sc.add("act", lambda e: e.activation(tA[:, :], tA[:, :], AF.Tanh, scale=0.5 * GELU_C),
                       reads=[("tA", pp)], writes=[("tA", pp)])
                sc.add("dve", lambda e: e.scalar_tensor_tensor(tF[:, :], tA[:, :], 1.0, tF[:, :], ALU.add, ALU.mult),
                       reads=[("tA", pp), ("tF", pp)], writes=[("tF", pp)])
                sc.add("act", lambda e, c=c: e.activation(tD[:, :], tB[:, :], AF.Exp, scale=hsp[:, 0, c:c + 1],
                                                          bias=hsp[:, 0, c:c + 1]),
                       reads=[("tB", pp), "hsp"], writes=[("tD", pp)])
                sc.add("act", lambda e, c=c: e.activation(tE[:, :], tB[:, :], AF.Exp, scale=hsp[:, 1, c:c + 1],
                                                          bias=hsp[:, 1, c:c + 1]),
                       reads=[("tB", pp), "hsp"], writes=["tE"])
                sc.add("act", lambda e: e.activation(tE[:, :], tE[:, :], AF.Sqrt, scale=-0.25, bias=0.25),
                       reads=["tE"], writes=["tE"])
                sc.add("pool", lambda e: e.tensor_tensor(tC[:, :], tC[:, :], tE[:, :], ALU.mult),
                       reads=[("tC", pp), "tE"], writes=[("tC", pp)])
                sc.add("dve", lambda e, c=c: e.tensor_tensor_scan(tB[:, :], tD[:, :], tC[:, :], hst[:, c:c + 1],
                                                                  ALU.mult, ALU.add),
                       reads=[("tD", pp), ("tC", pp), "hst", ("tB", pp)], writes=[("tB", pp)])
                sc.add("act", lambda e, c=c: e.activation(hst[:, c:c + 1], tB[:, T - 1:T], AF.Copy),
                       reads=[("tB", pp)], writes=["hst"])
                sc.add("pool", lambda e, c=c: e.tensor_tensor(lruT[:, c, :], tB[:, :], tF[:, :], ALU.mult),
                       reads=[("tB", pp), ("tF", pp)], writes=[("lruT", b, c)])
        def ml_piece(t, h):
            b = t % 2
            lruT, qT, kT, tcol, gsc = lruT2[b], qT2[b], kT2[b], tcol2[b], gsc2[b]
            if True:
                qa = Q[h % 4]
                qk = ("Q", h % 4)
                proj_fm(qa, 1024 + h * 128, 128, b, qk)
                sc.add("act", lambda e, h=h, qa=qa: e.activation(mubuf[:, h, 3:3 + T], qa, AF.Copy),
                       reads=[qk], writes=[("mubuf", h)])
                conv(mubuf, h, mcw, pcol[:, 1, h:h + 1], mA[:, :], ("mubuf", h), "mA", "mcw", ("pcol", 1))
                sc.add("pool", lambda e, h=h: e.tensor_copy(mubuf[:, h, 0:3], mubuf[:, h, T:T + 3]),
                       reads=["mA", ("mubuf", h)], writes=[("mubuf", h)])
                sc.add("act", lambda e: e.activation(mB[:, :], mA[:, :], AF.Tanh, scale=0.5), reads=["mA", "mB"], writes=["mB"])
                sc.add("dve", lambda e, h=h: e.scalar_tensor_tensor(mcb[:, h, :], mB[:, :], 1.0, mA[:, :], ALU.add, ALU.mult),
                       reads=["mA", "mB"], writes=[("mcb", h)])
                for ai in range(2):
                    sc.add("pe", lambda pe, ai=ai, h=h: pe.matmul(Q[4 + ai], wqk[:, ai, h, :], mcb[:, h, :],
                                                                   start=True, stop=True),
                           reads=[("mcb", h)] + [("wqk", a_) for a_ in range(2)], writes=[("Q", 4 + ai)])
                sc.add("act", lambda e, h=h: e.activation(qT[:, h, :], Q[4], AF.Copy, scale=0.5), reads=[("Q", 4)],
                       writes=[("qT", b, h)])
                sc.add("act", lambda e, h=h: e.activation(kT[:, h, :], Q[5], AF.Copy, scale=0.5 * 128.0 ** -0.5),
                       reads=[("Q", 5)], writes=[("kT", b, h)])
        def gates_piece(t):
            b = t % 2
            lruT, qT, kT, tcol, gsc = lruT2[b], qT2[b], kT2[b], tcol2[b], gsc2[b]
            proj_fm(Q[6][0:4, :], 2560, 4, b, ("Q", 6))
            proj_fm(Q[7][0:4, :], 2564, 4, b, ("Q", 7))
            sc.add("act", lambda e: e.activation(spf[:, :], Q[7][0:4, :], AF.Exp, scale=-1.0, bias=ngb[:, 0:1]),
                   reads=[("Q", 7), "ngb"], writes=["spf"])
            sc.add("act", lambda e: e.activation(spf[:, :], spf[:, :], AF.Ln, bias=1.0), reads=["spf"], writes=["spf"])
            sc.add("dve", lambda e: e.tensor_tensor_scan(csf[:, :], rmask[:, :], spf[:, :], 0.0, ALU.mult, ALU.add),
                   reads=["spf", "rmask"], writes=["csf"])
            sc.add("dve", lambda e: e.scalar_tensor_tensor(g_[:, :], Q[6][0:4, :], gb[:, 0:1], csf[:, :], ALU.add, ALU.add),
                   reads=[("Q", 6), ("gb", 0), "csf"], writes=["g"])
            g3 = g_[:, :].rearrange("p (c l) -> p c l", c=4)
            csf3 = csf[:, :].rearrange("p (c l) -> p c l", c=4)
            sc.add("dve", lambda e: e.reduce_max(mx[:, :], g3, AX.X), reads=["g"], writes=["mx"])
            sc.add("dve", lambda e: e.tensor_scalar(ncse[:, :], csf3[:, :, 127], -1.0, None, ALU.mult),
                   reads=["csf"], writes=["ncse"])
            sc.add("dve", lambda e: e.tensor_tensor(amax[:, :], mx[:, :], ncse[:, :], ALU.add),
                   reads=["mx", "ncse"], writes=["amax"])
            sc.add("dve", lambda e: e.tensor_tensor_scan(mall[:, 1:5], ncse[:, :], amax[:, :], mall[:, 0:1], ALU.add, ALU.max),
                   reads=["ncse", "amax", "mall"], writes=["mall"])
            sc.add("dve", lambda e: e.tensor_tensor_scan(cm[:, :], rneg[:, :], g_[:, :], -1.0e30, ALU.add, ALU.max),
                   reads=["g", "rneg"], writes=["cm"])
            cm3 = cm[:, :].rearrange("p (c l) -> p c l", c=4)
            m_in = mall[:, 0:4].unsqueeze(2).broadcast_to([4, 4, 128])
            mxb = mx[:, :].unsqueeze(2).broadcast_to([4, 4, 128])
            sc.add("dve", lambda e: e.tensor_tensor(cm3, cm3, m_in, ALU.max), reads=["cm", "mall"], writes=["cm"])
            d3 = dtm[:, :].rearrange("p (c l) -> p c l", c=4)
            sc.add("pool", lambda e: e.tensor_tensor(d3, g3, mxb, ALU.subtract), reads=["g", "mx"], writes=["dtm"])
            sc.add("act", lambda e: e.activation(wr[:, :], dtm[:, :], AF.Exp), reads=["dtm"], writes=["spf"])
            sc.add("pool", lambda e: e.tensor_tensor(d3, cm3, mxb, ALU.subtract), reads=["cm", "mx", "dtm"], writes=["dtm"])
            sc.add("act", lambda e: e.activation(f1r[:, :], dtm[:, :], AF.Exp, scale=-1.0), reads=["dtm"], writes=["f1r"])
            sc.add("pool", lambda e: e.tensor_tensor(d3, cm3, m_in, ALU.subtract), reads=["cm", "mall", "dtm"], writes=["dtm"])
            sc.add("act", lambda e: e.activation(f2r[:, :], dtm[:, :], AF.Exp, scale=-1.0), reads=["dtm"], writes=["f2r"])
            sc.add("pool", lambda e: e.tensor_tensor(dtm[:, :], csf[:, :], cm[:, :], ALU.subtract),
                   reads=["cm", "csf", "dtm"], writes=["dtm"])
            sc.add("act", lambda e: e.activation(enr[:, :], dtm[:, :], AF.Exp), reads=["dtm"], writes=["enr"])
            sc.add("dve", lambda e: e.tensor_tensor(gt[:, :], ncse[:, :], mall[:, 0:4], ALU.add),
                   reads=["ncse", "mall"], writes=["gt"])
            sc.add("dve", lambda e: e.tensor_tensor(gt[:, :], gt[:, :], mall[:, 1:5], ALU.subtract),
                   reads=["gt", "mall"], writes=["gt"])
            sc.add("act", lambda e: e.activation(gx[:, 0:4], gt[:, :], AF.Exp), reads=["gt"], writes=["gx"])
            sc.add("dve", lambda e: e.tensor_tensor(gt[:, :], amax[:, :], mall[:, 1:5], ALU.subtract),
                   reads=["amax", "mall", "gx"], writes=["gt"])
            sc.add("act", lambda e: e.activation(gx[:, 4:8], gt[:, :], AF.Exp), reads=["gt"], writes=["gx"])
            sc.add("dve", lambda e: e.tensor_copy(mall[:, 0:1], mall[:, 4:5]), reads=["mall", "gt", "cm", "dtm"],
                   writes=["mall"])
            def mm_g(pe):
                ins = None
                for h in range(4):
                    ins = pe.matmul(Q[6][:, h * 8:(h + 1) * 8], sel[:, h, :], gx[:, :], start=True, stop=True)
                return ins
            sc.add("pe", mm_g, reads=["sel", "gx"], writes=[("Q", 6)])
            sc.add("act", lambda e: e.activation(gsc[:, :, :].rearrange("p h x -> p (h x)"), Q[6][:, 0:32], AF.Copy),
                   reads=[("Q", 6)], writes=[("gsc", b)])
            def tr(pe):
                ins = None
                for c in range(4):
                    for qi, row in enumerate((wr, f1r, f2r, enr)):
                        o0 = (c * 4 + qi) * 4
                        ins = pe.transpose(Q[7][:, o0:o0 + 4], row[:, c * 128:(c + 1) * 128], identF[0:4, 0:4])
                return ins
            sc.add("pe", tr, reads=["spf", "f1r", "f2r", "enr", "identF"], writes=[("Q", 7)])
            sc.add("act", lambda e: e.activation(tcol[:, :, :, :].rearrange("p c q h -> p (c q h)"), Q[7][:, 0:64], AF.Copy),
                   reads=[("Q", 7)], writes=[("tcol", b)])

        def chunk_front(t, c):
            b = t % 2
            lruT, qT, kT, tcol, gsc = lruT2[b], qT2[b], kT2[b], tcol2[b], gsc2[b]
            cl = slice(c * 128, (c + 1) * 128)
            wcolf = tcol[:, c, 0, :]
            f1c = tcol[:, c, 1, :]
            f2c = tcol[:, c, 2, :]
            enc = tcol[:, c, 3, :]

            def b4(ap):
                return ap.unsqueeze(2).broadcast_to([128, 4, 128])

            def v4(ap):
                return ap.rearrange("p (h v) -> p h v", h=4)

            proj_tm(Q[0], 1536, b, c, ("Q", 0))
            sc.add("dve", lambda e: e.tensor_tensor(v4(vtok[:, :]), v4(Q[0]), b4(wcolf), ALU.mult),
                   reads=[("Q", 0), ("tcol", b)], writes=["vtok"])
            sc.add("act", lambda e: e.activation(wcb[:, :], wcolf, AF.Copy), reads=[("tcol", b)], writes=["wcb"])
            proj_tm(Q[1], 2048, b, c, ("Q", 1))
            sc.add("act", lambda e: e.activation(sigo[:, :], Q[1], AF.Tanh, scale=0.5), reads=[("Q", 1)], writes=["sigo"])

            def mm_s(pe):
                ins = None
                for h in range(4):
                    ins = pe.matmul(Q[2][:, h * 128:(h + 1) * 128], kT[:, h, cl], qT[:, h, cl], start=True, stop=True)
                return ins
            sc.add("pe", mm_s, reads=[("qT", b, h) for h in range(4)] + [("kT", b, h) for h in range(4)], writes=[("Q", 2)])
            sc.add("dve", lambda e: e.tensor_tensor(pT[:, :, :], v4(Q[2]),
                                                    maskT[:, :].unsqueeze(1).broadcast_to([128, 4, 128]), ALU.mult),
                   reads=[("Q", 2), "maskT"], writes=["pT"])

            def tr(pe):
                ins = None
                for h in range(4):
                    ins = pe.transpose(PT[:, h * 128:(h + 1) * 128], kT[:, h, cl], ident[:, :])
                return ins
            sc.add("pe", tr, reads=[("kT", b, h) for h in range(4)] + ["ident"], writes=[("Q", 6)])
            sc.add("act", lambda e: e.activation(kw[:, :, :].rearrange("p h d -> p (h d)"), PT[:, 0:512], AF.Copy),
                   reads=[("Q", 6)], writes=["kw"])

            def mm_x(pe):
                ins = None
                for h in range(4):
                    hs = slice(h * 128, (h + 1) * 128)
                    pe.matmul(Q[3][:, hs], pT[:, h, :], vtok[:, hs], start=True, stop=True)
                    pe.matmul(Q[4][:, hs], qT[:, h, cl], Cbf[:, h, :], start=True, stop=True)
                    pe.matmul(Q[5][:, h:h + 1], pT[:, h, :], wcb[:, h:h + 1], start=True, stop=True)
                    ins = pe.matmul(Q[5][:, 4 + h:5 + h], qT[:, h, cl], nbf[:, h:h + 1], start=True, stop=True)
                return ins
            sc.add("pe", mm_x, reads=["pT", "vtok", "Cbf", "nbf", "wcb"] + [("qT", b, h) for h in range(4)],
                   writes=[("Q", 3), ("Q", 4), ("Q", 5)])

            def mm_u(pe):
                ins = None
                for h in range(4):
                    hs = slice(h * 128, (h + 1) * 128)
                    pe.matmul(Q[7][:, hs], kw[:, h, :], vtok[:, hs], start=True, stop=True)
                    ins = pe.matmul(Q[1][:, h:h + 1], kw[:, h, :], wcb[:, h:h + 1], start=True, stop=True)
                return ins
            sc.add("pe", mm_u, reads=["kw", "vtok", "wcb"], writes=[("Q", 7), ("Q", 1)])

            sc.add("dve", lambda e: e.tensor_tensor(tn[:, :], Q[5][:, 4:8], f2c, ALU.mult),
                   reads=[("Q", 5), ("tcol", b)], writes=["tn"])
            sc.add("dve", lambda e: e.tensor_tensor(dn[:, :], Q[5][:, 0:4], f1c, ALU.mult),
                   reads=[("Q", 5), ("tcol", b)], writes=["dn"])
            sc.add("dve", lambda e: e.tensor_tensor(dn[:, :], dn[:, :], tn[:, :], ALU.add), reads=["dn", "tn"], writes=["dn"])
            sc.add("dve", lambda e: e.tensor_scalar(dn2[:, :], dn[:, :], -1.0, None, ALU.mult), reads=["dn"], writes=["dn2"])
            sc.add("dve", lambda e: e.tensor_tensor(dn[:, :], dn[:, :], dn2[:, :], ALU.max), reads=["dn", "dn2"], writes=["dn"])
            sc.add("dve", lambda e: e.tensor_tensor(dn[:, :], dn[:, :], enc, ALU.max), reads=["dn", ("tcol", b)], writes=["dn"])
            sc.add("dve", lambda e: e.reciprocal(rden[:, :], dn[:, :]), reads=["dn"], writes=["rden"])
            sc.add("dve", lambda e: e.scalar_tensor_tensor(f12[:, 0:4], f1c, 0.5, rden[:, :], ALU.mult, ALU.mult),
                   reads=["rden", ("tcol", b)], writes=["f12a"])
            sc.add("dve", lambda e: e.scalar_tensor_tensor(f12[:, 4:8], f2c, 0.5, rden[:, :], ALU.mult, ALU.mult),
                   reads=["rden", ("tcol", b)], writes=["f12b"])
            sc.add("dve", lambda e: e.tensor_tensor(v4(tm1[:, :]), v4(Q[4]), b4(f12[:, 4:8]), ALU.mult),
                   reads=[("Q", 4), "f12b"], writes=["tm1"])
            sc.add("dve", lambda e: e.tensor_tensor(v4(tm2[:, :]), v4(Q[3]), b4(f12[:, 0:4]), ALU.mult),
                   reads=[("Q", 3), "f12a"], writes=["tm2"])
            sc.add("pool", lambda e: e.tensor_tensor(tm1[:, :], tm1[:, :], tm2[:, :], ALU.add),
                   reads=["tm1", "tm2"], writes=["tm1"])
            sc.add("dve", lambda e: e.scalar_tensor_tensor(hm[:, :], sigo[:, :], 1.0, tm1[:, :], ALU.add, ALU.mult),
                   reads=["tm1", "sigo"], writes=["hm"])
            for h in range(4):
                hs = slice(h * 128, (h + 1) * 128)
                sc.add("act", lambda e, h=h, hs=hs: e.activation(junk[:, :], hm[:, hs], AF.Square, accum_out=ssq[:, h:h + 1]),
                       reads=["hm"], writes=["junk", "ssq"])
            _rstd_ops(sc, ssq[:, :], rsq[:, :], nhalf[:, 0:1].broadcast_to([128, 4]), 128.0, "ssq", "rsq", "msq", msq[:, :])
            g = t * 4 + c
            ob = g % 2
            sc.add("pool", lambda e: e.tensor_tensor(v4(o3[ob][:, :]), v4(hm[:, :]), b4(rsq[:, :]), ALU.mult),
                   reads=["hm", "rsq"], writes=[("o3", ob)])

            gold = gsc[:, :, c]
            gnew = gsc[:, :, 4 + c]
            sc.add("dve", lambda e: e.tensor_tensor(v4(tm3[:, :]), v4(Q[7]), b4(gnew), ALU.mult),
                   reads=[("Q", 7), ("gsc", b)], writes=["tm2"])
            sc.add("pool", lambda e: e.tensor_tensor(Cst[:, :, :], Cst[:, :, :], b4(gold), ALU.mult),
                   reads=["Cst", ("gsc", b)], writes=["Cst"])
            sc.add("pool", lambda e: e.tensor_tensor(Cst[:, :, :], Cst[:, :, :], v4(tm3[:, :]), ALU.add),
                   reads=["Cst", "tm2"], writes=["Cst"])
            sc.add("act", lambda e: e.activation(Cbf[:, :, :].rearrange("p h v -> p (h v)"),
                                                 Cst[:, :, :].rearrange("p h v -> p (h v)"), AF.Copy),
                   reads=["Cst"], writes=["Cbf"])
            sc.add("dve", lambda e: e.tensor_tensor(tn2[:, :], Q[1][:, 0:4], gnew, ALU.mult),
                   reads=[("Q", 1), ("gsc", b)], writes=["tn2"])
            sc.add("dve", lambda e: e.tensor_tensor(nst[:, :], nst[:, :], gold, ALU.mult), reads=["nst", ("gsc", b)], writes=["nst"])
            sc.add("dve", lambda e: e.tensor_tensor(nst[:, :], nst[:, :], tn2[:, :], ALU.add), reads=["nst", "tn2"], writes=["nst"])
            sc.add("act", lambda e: e.activation(nbf[:, :], nst[:, :], AF.Copy), reads=["nst"], writes=["nbf"])

        def chunk_back(t, c):
            b = t % 2
            lruT = lruT2[b]
            g = t * 4 + c
            ob = g % 2
            sl = g % 2
            r0 = g * 128
            cl = slice(c * 128, (c + 1) * 128)
            sc.add("sp", lambda e: e.dma_start(out=xres[sl][:, :], in_=src[r0:r0 + 128, :]),
                   reads=[(srcname, g)], writes=[("xres", sl)], dma="xres%d" % sl)

            def tr(pe):
                ins = None
                for m in range(4):
                    ins = pe.transpose(PT[:, 512 + m * 128:512 + (m + 1) * 128], o3[ob][:, m * 128:(m + 1) * 128], ident[:, :])
                return ins
            sc.add("pe", tr, reads=[("o3", ob), "ident"], writes=[("Q", 6)])
            sc.add("act", lambda e: e.activation(oTm[:, :, :].rearrange("p m t -> p (m t)"), PT[:, 512:1024], AF.Copy),
                   reads=[("Q", 6)], writes=["oTm"])
            for dh in range(2):
                def mm(pe, dh=dh):
                    ins = None
                    for m in range(4):
                        ins = pe.matmul(Q[dh], lruT[:, m, cl], wout[:, m, dh * 512:(dh + 1) * 512],
                                        start=(m == 0), stop=False)
                    for m in range(4):
                        ins = pe.matmul(Q[dh], oTm[:, m, :], wout[:, 4 + m, dh * 512:(dh + 1) * 512],
                                        start=False, stop=(m == 3))
                    return ins
                sc.add("pe", mm, reads=["oTm"] + [("lruT", b, m) for m in range(4)] + [("wout", m) for m in range(8)],
                       writes=[("Q", dh)])
                sc.add("dve", lambda e, dh=dh: e.tensor_tensor(xres[sl][:, dh * 512:(dh + 1) * 512], Q[dh],
                                                               xres[sl][:, dh * 512:(dh + 1) * 512], ALU.add),
                       reads=[("Q", dh), ("xres", sl)], writes=[("xres", sl)])
            sc.add("sp", lambda e: e.dma_start(out=dst[r0:r0 + 128, :], in_=xres[sl][:, :]),
                   reads=[("xres", sl)], writes=[(dstname, g)], dma="xst%d" % sl)

        def phase_b_piece(t, i):
            lru_piece(t, i)
            ml_piece(t, i)

        for j in range(4):
            norm_sub(sc, B, src, srcname, PT[:, 0:1024], ("Q", 6), 0, j)
        for i in range(4):
            phase_b_piece(0, i)
        gates_piece(0)
        for t in range(NT):
            pending = None
            nxt = t + 1 < NT
            for c in range(4):
                chunk_front(t, c)
                if pending is not None:
                    chunk_back(*pending)
                pending = (t, c)
                if nxt:
                    if c < 2:
                        norm_sub(sc, B, src, srcname, PT[:, 0:1024], ("Q", 6), t + 1, 2 * c)
                        norm_sub(sc, B, src, srcname, PT[:, 0:1024], ("Q", 6), t + 1, 2 * c + 1)
                    else:
                        phase_b_piece(t + 1, 2 * (c - 2))
                        phase_b_piece(t + 1, 2 * (c - 2) + 1)
            chunk_back(*pending)
            if nxt:
                gates_piece(t + 1)
        sc.barrier()
        sc.flush()


def build_program(S, phases=("ffn1_0",), raw_last=False):
    nc = bass.Bass("TRN2", target_bir_lowering=False)

    def din(name, shape):
        return nc.dram_tensor(name, list(shape), F32, kind="ExternalInput").ap()

    x = din("x", [S, D])
    W = {}
    W["ffn1_norm"] = din("ffn1_norm", [2, D])
    W["ffn1_wgu"] = din("ffn1_wgu", [2, D, 2 * DFF])
    W["ffn1_wd"] = din("ffn1_wd", [2, DFF, D])
    W["ffn2_norm"] = din("ffn2_norm", [2, D])
    W["ffn2_wgu"] = din("ffn2_wgu", [2, D, 2 * DFF])
    W["ffn2_wd"] = din("ffn2_wd", [2, DFF, D])
    W["final_norm"] = din("final_norm", [D])
    W["mix_norm"] = din("mix_norm", [2, D])
    E = {"w_in": din("e_w_in", [1, D, 3600])[0], "w_lr_up": din("e_w_lr_up", [1, 16, 256])[0],
         "b_lr": din("e_b_lr", [1, 256])[0], "head_norm": din("e_head_norm", [1, D])[0],
         "w_out": din("e_w_out", [1, D, D])[0]}
    c_ident = din("c_ident", [128, 128])
    C = {"ident": c_ident, "cos": din("c_cos", [128, S]), "sin": din("c_sin", [128, S]),
         "dq": din("c_dq", [4, 128]), "dk": din("c_dk", [4, 128]), "maskT": din("c_maskT", [128, 128]),
         "rmask": din("c_rmask", [512]), "rneg": din("c_rneg", [512]), "identF": c_ident,
         "perm": din("c_perm", [128, 128]),
         "sel": din("c_sel", [4, 4, 128])}
    O = {"w_in": din("o_w_in", [1, D, 2568])[0], "w_out": din("o_w_out", [1, D, D])[0],
         "lru_conv_w": din("o_lru_conv_w", [1, 4, 512])[0], "lru_conv_b": din("o_lru_conv_b", [1, 512])[0],
         "lru_wa": din("o_lru_wa", [1, 8, 64, 64])[0], "lru_ba": din("o_lru_ba", [1, 512])[0],
         "lru_wx": din("o_lru_wx", [1, 8, 64, 64])[0], "lru_bx": din("o_lru_bx", [1, 512])[0],
         "lru_lambda": din("o_lru_lambda", [1, 512])[0],
         "ml_conv_w": din("o_ml_conv_w", [1, 4, 512])[0], "ml_conv_b": din("o_ml_conv_b", [1, 512])[0],
         "ml_wq": din("o_ml_wq", [1, 4, 128, 128])[0], "ml_wk": din("o_ml_wk", [1, 4, 128, 128])[0],
         "ml_bi": din("o_ml_bi", [1, 4])[0], "ml_bf": din("o_ml_bf", [1, 4])[0],
         "ml_norm": din("o_ml_norm", [1, 512])[0]}
    out = nc.dram_tensor("out", [S, D], F32, kind="ExternalOutput").ap()
    hA = nc.dram_tensor("hA", [S, D], F32).ap()
    hB = nc.dram_tensor("hB", [S, D], F32).ap()

    sc = Sched(nc)
    cx = Ctx(nc, sc, S)
    bufs = {"x": x, "hA": hA, "hB": hB, "out": out}
    cur = "x"
    plist = list(phases)
    for i, ph in enumerate(plist):
        last = i == len(plist) - 1
        if last:
            nxt = "out"
        else:
            nxt = "hA" if cur != "hA" else "hB"
        kind, layer = ph.rsplit("_", 1)
        layer = int(layer)
        if kind in ("ffn1", "ffn2"):
            fin = W["final_norm"] if (kind == "ffn2" and layer == 1 and not raw_last) else None
            ffn_phase(cx, bufs[cur], bufs[nxt], cur, nxt, W[kind + "_norm"][layer],
                      W[kind + "_wgu"][layer], W[kind + "_wd"][layer], c_ident, fin_ap=fin)
        elif kind == "mix" and layer == 1:
            odd_phase(cx, bufs[cur], bufs[nxt], cur, nxt, W["mix_norm"][1], O, C)
        elif kind == "mix" and layer == 0:
            even_phase(cx, bufs[cur], bufs[nxt], cur, nxt, W["mix_norm"][0], E, C)
        else:
            raise ValueError(ph)
        cur = nxt
    sc.close()
    return nc


def host_consts(S):
    c = {"c_ident": np.eye(128, dtype=np.float32)}
    half = 64
    inv = 10000.0 ** (-np.arange(half, dtype=np.float64) / half)
    pos = np.arange(S, dtype=np.float64)
    ang = pos[None, :] * inv[:, None]
    cos = np.cos(ang).astype(np.float32)
    sin = np.sin(ang).astype(np.float32)
    c["c_cos"] = np.concatenate([cos, cos], 0)
    c["c_sin"] = np.concatenate([-sin, sin], 0)
    l = np.arange(128, dtype=np.float64)
    gam = np.array(RET_G, dtype=np.float64)
    c["c_dq"] = (gam[:, None] ** (l[None, :] + 1.0)).astype(np.float32)
    c["c_dk"] = (gam[:, None] ** (-(l[None, :] + 1.0)) * 128.0 ** -0.5).astype(np.float32)
    c["c_maskT"] = (np.arange(128)[:, None] <= np.arange(128)[None, :]).astype(np.float32)
    rm = np.ones(512, dtype=np.float32)
    rm[::128] = 0.0
    c["c_rmask"] = rm
    rn = np.zeros(512, dtype=np.float32)
    rn[::128] = -1.0e30
    c["c_rneg"] = rn
    sel = np.zeros((4, 4, 128), dtype=np.float32)
    for h in range(4):
        sel[h, h, :] = 1.0
    c["c_sel"] = sel
    pm = np.zeros((128, 128), dtype=np.float32)
    for d_ in range(128):
        pm[(d_ + 64) % 128, d_] = 1.0
    c["c_perm"] = pm
    return c


def prep_inputs(ins):
    return {k: np.ascontiguousarray(v, dtype=np.float32) for k, v in ins.items()}


ALL_PHASES = ("ffn1_0", "mix_0", "ffn2_0", "ffn1_1", "mix_1", "ffn2_1")
_PROG_CACHE = {}


def kernel(**inputs):
    x = np.asarray(inputs["x"], dtype=np.float32)
    Bn, S, _ = x.shape
    if S not in _PROG_CACHE:
        _PROG_CACHE[S] = build_program(S, phases=ALL_PHASES)
    nc = _PROG_CACHE[S]
    shared = {k: np.ascontiguousarray(np.asarray(v), dtype=np.float32) for k, v in inputs.items() if k != "x"}
    shared.update(host_consts(S))
    in_maps = []
    for b in range(Bn):
        m = dict(shared)
        m["x"] = np.ascontiguousarray(x[b])
        in_maps.append(m)
    res = run_bass_kernel_spmd(nc, in_maps, core_ids=list(range(Bn)))
    return np.stack([np.asarray(r["out"], dtype=np.float32) for r in res.results], axis=0)
```

```python
import contextlib
import numpy as np
import concourse.bass as bass
import concourse.mybir as mybir
from concourse.bass_utils import run_bass_kernel_spmd

F32 = mybir.dt.float32
BF16 = mybir.dt.bfloat16
AF = mybir.ActivationFunctionType
ALU = mybir.AluOpType
AX = mybir.AxisListType

D = 1024
DFF = 2816
NF = DFF // 128
EPS = 1e-6


SAME_SYNC = {"pool", "act", "dve"}
LOOKAHEAD = 24
HOP = 0.20
FIX = {"dve": 0.09, "act": 0.09, "pool": 0.09}
PE_FIX = 0.004


class _Op:
    __slots__ = ("eng", "idx", "emit", "waits", "signal", "is_dma", "chan", "chan_val", "sigval",
                 "deps", "succ", "cost", "lat", "pidx", "nun", "ready", "start", "finish", "seg")


_PSUM_NAMES = {"PT", "pt"}


def _is_psum_key(r):
    if isinstance(r, tuple):
        return r[0] in ("Q", "pg", "pu", "po")
    return r in _PSUM_NAMES


class _Dummy:
    def then_inc(self, *a, **k):
        return self


class _CostProxy:
    def __init__(self, eng):
        self.eng = eng
        self.cost = 0.0
        self.lat = 0.0

    def __getattr__(self, name):
        def f(*args, **kw):
            def fs(ap):
                sh = ap.shape
                n = 1
                for v in sh[1:]:
                    n *= int(v)
                return n
            if name == "matmul":
                rhs = args[2] if len(args) > 2 else kw["rhs"]
                self.cost += max(fs(rhs), 64) / 2400.0 + PE_FIX
            elif name == "transpose":
                self.cost += max(fs(args[1]), 64) / 2400.0 + 0.03
            elif name == "dma_start":
                o = kw.get("out", args[0] if args else None)
                nb = fs(o) * int(o.shape[0]) * (2 if o.dtype == BF16 else 4)
                self.cost += 0.6 if self.eng == "pool" else 0.08
                self.lat += 2.0 + nb / 160e3
            else:
                o = kw.get("out", args[0] if args else None)
                n = fs(o)
                if name == "tensor_tensor_scan":
                    n *= 2
                rate = {"dve": 960.0, "act": 1200.0, "pool": 800.0}.get(self.eng, 960.0)
                self.cost += FIX.get(self.eng, 0.09) + n / rate
                if kw.get("accum_out") is not None:
                    self.cost += FIX.get(self.eng, 0.09)
            return _Dummy()
        return f


class Sched:
    ENGS = ("pe", "act", "dve", "pool", "sp")

    def __init__(self, nc):
        self.nc = nc
        self.ops = []
        self.lastw = {}
        self.readers = {}
        self.chan_cnt = {}
        self.chan_sem = {}
        self.chan_eng = {}
        self.sigcnt = {e: 0 for e in self.ENGS}
        self.eng_sem = {}
        self.seg = 0
        self.total_est = 0.0
        self._stack = contextlib.ExitStack()
        for e in self.ENGS:
            self.eng_sem[e] = self._stack.enter_context(nc.semaphore("sem_" + e))

    def _chan(self, name, eng):
        if name not in self.chan_sem:
            self.chan_sem[name] = self._stack.enter_context(
                self.nc.semaphore("dq_" + str(name).replace(" ", "")))
            self.chan_cnt[name] = 0
            self.chan_eng[name] = eng
        assert self.chan_eng[name] == eng, "DMA channel used from two queues: %s" % (name,)
        return self.chan_sem[name]

    def add(self, eng, emit, reads=(), writes=(), dma=None):
        op = _Op()
        op.eng = eng
        op.emit = emit
        op.is_dma = dma is not None
        op.pidx = len(self.ops)
        op.signal = False
        op.waits = []
        op.sigval = None
        op.chan = dma
        op.chan_val = None
        op.seg = self.seg
        op.succ = []
        if op.is_dma:
            self._chan(dma, eng)
        px = _CostProxy(eng)
        emit(px)
        op.cost = px.cost
        op.lat = px.lat
        deps = {}
        for r in reads:
            w = self.lastw.get(r)
            if w is not None and w.seg == self.seg:
                deps[id(w)] = w
            if _is_psum_key(r):
                for w2 in self.readers.get(r, ()):
                    if w2.seg == self.seg and w2.eng != eng:
                        deps[id(w2)] = w2
        for r in writes:
            w = self.lastw.get(r)
            if w is not None and w.seg == self.seg:
                deps[id(w)] = w
            for w in self.readers.get(r, ()):
                if w.seg == self.seg:
                    deps[id(w)] = w
        deps.pop(id(op), None)
        op.deps = list(deps.values())
        for d in op.deps:
            d.succ.append(op)
        for r in reads:
            self.readers.setdefault(r, []).append(op)
        for r in writes:
            self.lastw[r] = op
            self.readers[r] = []
        self.ops.append(op)
        return op

    def barrier(self):
        pass

    def _list_schedule(self):
        ops = self.ops
        avail = {e: [] for e in self.ENGS}
        free = {e: 0.0 for e in self.ENGS}
        order = {e: [] for e in self.ENGS}
        import bisect
        for op in ops:
            op.nun = len(op.deps)
            op.ready = 0.0
            if op.nun == 0:
                avail[op.eng].append((op.pidx, op))
        nleft = len(ops)
        while nleft:
            best = None
            bstart = None
            for e in self.ENGS:
                al = avail[e]
                if not al:
                    continue
                f = free[e]
                for k in range(min(LOOKAHEAD, len(al))):
                    op = al[k][1]
                    st = op.ready if op.ready > f else f
                    if bstart is None or st < bstart - 0.05 or (st < bstart + 0.05 and op.pidx < best.pidx):
                        if bstart is None or st < bstart + 0.05:
                            best, bstart = op, (st if bstart is None else min(st, bstart))
            op = best
            e = op.eng
            st = op.ready if op.ready > free[e] else free[e]
            op.start = st
            free[e] = st + op.cost
            op.finish = st + op.cost + op.lat
            order[e].append(op)
            al = avail[e]
            al.pop(bisect.bisect_left(al, (op.pidx,)))
            nleft -= 1
            for s_ in op.succ:
                hop = HOP if (s_.eng != e or op.is_dma or e in SAME_SYNC) else 0.0
                r = op.finish + hop
                if r > s_.ready:
                    s_.ready = r
                s_.nun -= 1
                if s_.nun == 0:
                    bisect.insort(avail[s_.eng], (s_.pidx, s_))
        self.total_est += max(free.values()) if ops else 0.0
        return order

    def flush(self):
        nc = self.nc
        order = self._list_schedule()
        for e in self.ENGS:
            for i, op in enumerate(order[e]):
                op.idx = i
                if op.is_dma:
                    self.chan_cnt[op.chan] += 16
                    op.chan_val = self.chan_cnt[op.chan]
        for e in self.ENGS:
            wd = {}
            for op in order[e]:
                best = {}
                for d in op.deps:
                    if d.is_dma:
                        key = ("c", d.chan)
                        if wd.get(key, 0) >= d.chan_val:
                            continue
                        if key not in best or best[key].chan_val < d.chan_val:
                            best[key] = d
                    else:
                        if d.eng == e and not op.is_dma and e not in SAME_SYNC:
                            continue
                        key = ("e", d.eng)
                        if wd.get(key, -1) >= d.idx:
                            continue
                        if key not in best or best[key].idx < d.idx:
                            best[key] = d
                for key, d in best.items():
                    if key[0] == "c":
                        wd[key] = d.chan_val
                    else:
                        wd[key] = d.idx
                        d.signal = True
                    op.waits.append(d)
        lasts = []
        for e in self.ENGS:
            comp = [op for op in order[e] if not op.is_dma]
            if comp:
                comp[-1].signal = True
                lasts.append(comp[-1])
        for e in self.ENGS:
            for op in order[e]:
                if op.signal and op.sigval is None:
                    self.sigcnt[e] += 1
                    op.sigval = self.sigcnt[e]
        chans = [(n, v) for n, v in self.chan_cnt.items() if v > 0]
        handles = {"pe": "tensor", "act": "scalar", "dve": "vector", "pool": "gpsimd", "sp": "sync"}
        sched = self

        def run(ename, eng):
            for op in order[ename]:
                for d in op.waits:
                    if d.is_dma:
                        eng.wait_ge(sched.chan_sem[d.chan], d.chan_val)
                    else:
                        eng.wait_ge(sched.eng_sem[d.eng], d.sigval)
                ins = op.emit(eng)
                if op.is_dma:
                    ins.then_inc(sched.chan_sem[op.chan], 16)
                elif op.signal:
                    ins.then_inc(sched.eng_sem[ename], 1)
            for d in lasts:
                eng.wait_ge(sched.eng_sem[d.eng], d.sigval)
            for n, v in chans:
                eng.wait_ge(sched.chan_sem[n], v)

        with nc.Block() as block:
            for ename in self.ENGS:
                deco = getattr(block, handles[ename])

                def _f(eng, _en=ename):
                    run(_en, eng)

                deco(_f)
        self.ops = []
        self.seg += 1

    def close(self):
        self._stack.close()


class Ctx:
    def __init__(self, nc, sc, S):
        self.nc = nc
        self.sc = sc
        self.S = S
        self.uid = 0

    def tag(self):
        self.uid += 1
        return "_%d" % self.uid


def _rstd_ops(sc, ss_ap, rstd_ap, nhalf_ap, n, key_in, key_out, key_tmp, tmp_ap):
    sc.add("dve", lambda e: e.tensor_scalar(tmp_ap, ss_ap, 1.0 / n, EPS, ALU.mult, ALU.add),
           reads=[key_in], writes=[key_tmp])
    sc.add("pool", lambda e: e.tensor_tensor(rstd_ap, tmp_ap, nhalf_ap, ALU.pow),
           reads=[key_tmp, "nhalf"], writes=[key_out])


def alloc_norm(sb, T, nld=2):
    B = {}
    B["xld"] = [sb("xld%d" % i, [128, D], F32) for i in range(nld)]
    B["xnt"] = [sb("xnt%d" % i, [128, D], BF16) for i in range(nld)]
    B["xnT"] = [sb("xnT%d" % i, [128, 8, T], BF16) for i in range(2)]
    B["gcol"] = sb("gcol", [128, 8], F32)
    B["ident"] = sb("ident", [128, 128], BF16)
    B["ss"] = sb("ss", [128, 8], F32)
    B["ms"] = sb("ms", [128, 8], F32)
    B["rstd"] = sb("rstd", [128, 8], F32)
    B["nhalf"] = sb("nhalf", [128, 1], F32)
    return B


def init_norm(sc, B, nrm_ap, ident_ap):
    nhalf, ident, gcol = B["nhalf"], B["ident"], B["gcol"]
    sc.add("pool", lambda e: e.memset(nhalf[:, :], -0.5), writes=["nhalf"])
    sc.add("pool", lambda e: e.dma_start(out=ident[:, :], in_=ident_ap), writes=["ident"], dma="k9")
    sc.add("sp", lambda e: e.dma_start(out=gcol[:, :], in_=nrm_ap.rearrange("(k p) -> p k", p=128),
                                       allow_slow_non_contiguous=True),
           writes=["gcol"], dma="k1")


def norm_sub(sc, B, src, srcname, pt_bf, ptkey, t, j, nsub=4):
    g = t * nsub + j
    sl = g % len(B["xld"])
    r0 = g * 128
    xld, xnt, xnT = B["xld"][sl], B["xnt"][sl], B["xnT"][t % 2]
    ss, ms, rstd, nhalf, gcol, ident = B["ss"], B["ms"], B["rstd"], B["nhalf"], B["gcol"], B["ident"]
    sc.add("sp", lambda e: e.dma_start(out=xld[:, :], in_=src[r0:r0 + 128, :]),
           reads=[(srcname, g)], writes=[("xld", sl)], dma="xld%d" % sl)
    c = g % 8
    sc.add("act", lambda e: e.activation(xnt[:, :], xld[:, :], AF.Square, accum_out=ss[:, c:c + 1]),
           reads=[("xld", sl)], writes=[("xnt", sl), ("ss", c)])
    _rstd_ops(sc, ss[:, c:c + 1], rstd[:, c:c + 1], nhalf[:, 0:1], float(D),
              ("ss", c), ("rstd", c), ("ms", c), ms[:, c:c + 1])
    sc.add("act", lambda e: e.activation(xnt[:, :], xld[:, :], AF.Copy, scale=rstd[:, c:c + 1]),
           reads=[("xld", sl), ("rstd", c)], writes=[("xnt", sl)])

    def tr(pe):
        ins = None
        for k in range(8):
            ins = pe.transpose(pt_bf[:, k * 128:(k + 1) * 128], xnt[:, k * 128:(k + 1) * 128], ident[:, :])
        return ins
    sc.add("pe", tr, reads=[("xnt", sl), "ident"], writes=[ptkey])
    b = t % 2
    sc.add("dve", lambda e: e.tensor_tensor(
        xnT[:, :, j * 128:(j + 1) * 128],
        pt_bf.rearrange("p (k t) -> p k t", k=8),
        gcol[:, :].unsqueeze(2).broadcast_to([128, 8, 128]), ALU.mult),
        reads=[ptkey, "gcol"], writes=[("xnT", b, j)])


def ffn_phase(cx, src, dst, srcname, dstname, nrm_ap, wgu_ap, wd_ap, ident_ap, fin_ap=None):
    nc, sc, S = cx.nc, cx.sc, cx.S
    T = 512
    NT = S // T
    tg = cx.tag()
    with contextlib.ExitStack() as st:
        def sb(name, shape, dt):
            return st.enter_context(nc.sbuf_tensor(name + tg, shape, dt))

        def ps(name, shape, dt):
            return st.enter_context(nc.psum_tensor(name + tg, shape, dt))

        wgu = sb("wgu", [128, 8, 2 * DFF], BF16)
        wd = sb("wd", [128, NF, D], BF16)
        B = alloc_norm(sb, T)
        xnT = B["xnT"]
        ss, ms, rstd, nhalf = B["ss"], B["ms"], B["rstd"], B["nhalf"]
        sil = [sb("sil%d" % i, [128, T], F32) for i in range(2)]
        hT = sb("hT", [128, NF, T], BF16)
        xres = [sb("xres%d" % i, [128, D], F32) for i in range(2)]
        if fin_ap is not None:
            grow = sb("grow", [128, D], F32)
            junk = sb("junk", [128, D], BF16)
        pg = [ps("pg%d" % i, [128, T], F32) for i in range(2)]
        pu = [ps("pu%d" % i, [128, T], F32) for i in range(2)]
        pt = ps("pt", [128, 8 * 128], BF16)
        po = [ps("po%d" % i, [128, 512], F32) for i in range(2)]

        init_norm(sc, B, nrm_ap, ident_ap)
        if fin_ap is not None:
            sc.add("sp", lambda e: e.dma_start(out=grow[:, :], in_=fin_ap.partition_broadcast(128)),
                   writes=["grow"], dma="k2")
        FG = [(0, 2), (2, 6), (6, 14), (14, 22)]
        fgrp = {}
        wgu_v = wgu_ap.rearrange("(k p) c -> p k c", p=128)
        for gi, (f0, f1) in enumerate(FG):
            for f in range(f0, f1):
                fgrp[f] = gi
            for part in range(2):
                c0 = part * DFF + f0 * 128
                c1 = part * DFF + f1 * 128
                sc.add("pool", lambda e, c0=c0, c1=c1: e.dma_start(out=wgu[:, :, c0:c1], in_=wgu_v[:, :, c0:c1]),
                       writes=[("wgu", gi, part)], dma="wgu%d" % (gi * 2 + part))
        DG = [(0, 8), (8, 15), (15, 22)]
        wd_v = wd_ap.rearrange("(f p) d -> p f d", p=128)
        for gi, (f0, f1) in enumerate(DG):
            sc.add("pool", lambda e, f0=f0, f1=f1: e.dma_start(out=wd[:, f0:f1, :], in_=wd_v[:, f0:f1, :]),
                   writes=[("wd", gi)], dma="wd%d" % gi)

        def gate_up(t, f):
            b = t % 2
            pb = f % 2

            def mm(pe):
                ins = None
                for k in range(8):
                    ins = pe.matmul(pg[pb][:, :], wgu[:, k, f * 128:(f + 1) * 128], xnT[b][:, k, :],
                                    start=(k == 0), stop=(k == 7))
                for k in range(8):
                    ins = pe.matmul(pu[pb][:, :], wgu[:, k, DFF + f * 128:DFF + (f + 1) * 128],
                                    xnT[b][:, k, :], start=(k == 0), stop=(k == 7))
                return ins
            sc.add("pe", mm, reads=[("wgu", fgrp[f], 0), ("wgu", fgrp[f], 1)] + [("xnT", b, j) for j in range(4)],
                   writes=[("pg", pb), ("pu", pb)])
            sc.add("act", lambda e: e.activation(sil[pb][:, :], pg[pb][:, :], AF.Silu),
                   reads=[("pg", pb)], writes=[("sil", pb)])
            sc.add("dve", lambda e: e.tensor_tensor(hT[:, f, :], sil[pb][:, :], pu[pb][:, :], ALU.mult),
                   reads=[("sil", pb), ("pu", pb)], writes=[("hT", f)])

        def down_sub(t, j):
            g = t * 4 + j
            sl = g % 2
            r0 = g * 128
            sc.add("sp", lambda e: e.dma_start(out=xres[sl][:, :], in_=src[r0:r0 + 128, :]),
                   reads=[(srcname, g)], writes=[("xres", sl)], dma="xres%d" % sl)
            for dh in range(2):
                def mm(pe, dh=dh):
                    ins = None
                    for f in range(NF):
                        ins = pe.matmul(po[dh][:, :], hT[:, f, j * 128:(j + 1) * 128],
                                        wd[:, f, dh * 512:(dh + 1) * 512],
                                        start=(f == 0), stop=(f == NF - 1))
                    return ins
                sc.add("pe", mm, reads=[("hT", f) for f in range(NF)] + [("wd", gi) for gi in range(3)],
                       writes=[("po", dh)])
                sc.add("dve", lambda e, dh=dh: e.scalar_tensor_tensor(
                    xres[sl][:, dh * 512:(dh + 1) * 512], po[dh][:, :], 0.5,
                    xres[sl][:, dh * 512:(dh + 1) * 512], ALU.mult, ALU.add),
                    reads=[("po", dh), ("xres", sl)], writes=[("xres", sl)])
            if fin_ap is not None:
                c = g % 8
                sc.add("act", lambda e: e.activation(junk[:, :], xres[sl][:, :], AF.Square,
                                                     accum_out=ss[:, c:c + 1]),
                       reads=[("xres", sl)], writes=["junk", ("ss", c)])
                _rstd_ops(sc, ss[:, c:c + 1], rstd[:, c:c + 1], nhalf[:, 0:1], float(D),
                          ("ss", c), ("rstd", c), ("ms", c), ms[:, c:c + 1])
                sc.add("dve", lambda e: e.scalar_tensor_tensor(
                    xres[sl][:, :], xres[sl][:, :], rstd[:, c:c + 1], grow[:, :], ALU.mult, ALU.mult),
                    reads=[("xres", sl), ("rstd", c), "grow"], writes=[("xres", sl)])
            sc.add("sp", lambda e: e.dma_start(out=dst[r0:r0 + 128, :], in_=xres[sl][:, :]),
                   reads=[("xres", sl)], writes=[(dstname, g)], dma="xst%d" % sl)

        for j in range(4):
            norm_sub(sc, B, src, srcname, pt[:, :], "pt", 0, j)
        inter = {3: 0, 8: 1, 13: 2, 18: 3}
        for t in range(NT):
            for f in range(NF):
                gate_up(t, f)
                if f in inter and t + 1 < NT:
                    norm_sub(sc, B, src, srcname, pt[:, :], "pt", t + 1, inter[f])
            for j in range(4):
                down_sub(t, j)
        sc.barrier()
        sc.flush()


RET_G = [1.0 - 2.0 ** (-5.0 - h) for h in range(4)]
LN_GK = float(np.log(0.125))


def even_phase(cx, src, dst, srcname, dstname, nrm_ap, W, C):
    nc, sc, S = cx.nc, cx.sc, cx.S
    T = 512
    NT = S // T
    tg = cx.tag()
    GL = [float(np.float64(g) ** 128) for g in RET_G]
    with contextlib.ExitStack() as st:
        def sb(name, shape, dt):
            return st.enter_context(nc.sbuf_tensor(name + tg, shape, dt))

        def ps(name, shape, dt):
            return st.enter_context(nc.psum_tensor(name + tg, shape, dt))

        win = sb("win", [128, 8, 3600], BF16)
        pm = sb("pm", [128, 128], BF16)
        qbb = sb("qbb", [128, T], BF16)
        wout = sb("wout", [128, 8, D], BF16)
        hcol = sb("hcol", [128, 8], F32)
        wlr = sb("wlr", [16, 256], F32)
        nblr = sb("nblr", [128, 2], F32)
        B = alloc_norm(sb, T)
        xnT = B["xnT"]
        nhalf = B["nhalf"]
        ident = B["ident"]
        xres = [sb("xres%d" % i, [128, D], F32) for i in range(2)]
        cst = sb("cst", [128, T], F32)
        snt = sb("snt", [128, T], F32)
        t1 = sb("t1", [128, T], F32)
        t2 = sb("t2", [128, T], F32)
        qT = sb("qT", [128, 4, T], BF16)
        kT = sb("kT", [128, 4, T], BF16)
        dq = sb("dq", [128, 4, 128], F32)
        dk = sb("dk", [128, 4, 128], F32)
        glrT = sb("glrT", [16, T], F32)
        ea = sb("ea", [128, T], F32)
        cs = sb("cs", [128, T], F32)
        eqt = sb("eqt", [128, T], F32)
        ekt = sb("ekt", [128, T], F32)
        gqz = sb("gqz", [128, 4, T], BF16)
        gkT = sb("gkT", [128, 2, T], BF16)
        rmask = sb("rmask", [128, T], F32)
        csm = sb("csm", [128, 2, 4], F32)
        emid = sb("emid", [128, 2, 4], F32)
        eend = sb("eend", [128, 2, 4], F32)
        eem = sb("eem", [128, 2, 4], F32)
        dcs = sb("dcs", [128, 2, 4], F32)
        vr = sb("vr", [128, 512], BF16)
        vg = sb("vg", [128, 512], BF16)
        gate = sb("gate", [128, D], F32)
        sT = sb("sT", [128, 512], BF16)
        sTg = sb("sTg", [128, 512], BF16)
        ktok = sb("ktok", [128, 512], BF16)
        kpad = sb("kpad", [128, 2, 2, 128], BF16)
        maskT = sb("maskT", [128, 128], BF16)
        A = sb("A", [128, 4, 128], F32)
        Sbf = sb("Sbf", [128, 4, 128], BF16)
        Sg = sb("Sg", [128, 2, 128], F32)
        Spbf = sb("Spbf", [128, 2, 128], BF16)
        gtmp = sb("gtmp", [128, 128], F32)
        o1 = sb("o1", [128, D], F32)
        o3 = [sb("o3%d" % i, [128, D], BF16) for i in range(2)]
        oT = sb("oT", [128, 8, 128], BF16)
        junk = sb("junk", [128, 128], BF16)
        ssq = sb("ssq", [128, 8], F32)
        msq = sb("msq", [128, 8], F32)
        rsq = sb("rsq", [128, 8], F32)
        PA = ps("PA", [128, 1024], F32)
        PB = ps("PB", [128, 1024], F32)
        PC = ps("PC", [128, 1024], F32)
        PD = ps("PD", [128, 1024], F32)
        Q = [PA[:, 0:512], PA[:, 512:1024], PB[:, 0:512], PB[:, 512:1024],
             PC[:, 0:512], PC[:, 512:1024], PD[:, 0:512], PD[:, 512:1024]]
        PT = PD[:, 0:512].bitcast(BF16)

        init_norm(sc, B, nrm_ap, C["ident"])
        w_in, w_out = W["w_in"], W["w_out"]
        w_in_v = w_in.rearrange("(k p) c -> p k c", p=128)
        EG = [(0, 512), (512, 1024), (2048, 2560), (3584, 3600), (1024, 2048), (2560, 3584)]
        egrp = {}

        def ld_grp(gi, c0, c1):
            sc.add("pool", lambda e: e.dma_start(out=win[:, :, c0:c1], in_=w_in_v[:, :, c0:c1]),
                   writes=[("win", gi)], dma="w%d" % gi)

        ld_grp(0, *EG[0])
        ld_grp(1, *EG[1])
        for gi in range(2, 6):
            ld_grp(gi, *EG[gi])

        def wkeys(wt, c0):
            for gi, (a, b_) in enumerate(EG):
                if a <= c0 < b_:
                    return [("win", gi)]
            raise ValueError(c0)
        sc.add("sp", lambda e: e.dma_start(out=hcol[:, :], in_=W["head_norm"].rearrange("(k p) -> p k", p=128),
                                           allow_slow_non_contiguous=True), writes=["hcol"], dma="k3")
        for k in range(8):
            sc.add("sp", lambda e, k=k: e.dma_start(out=o1[:, :], in_=w_out[k * 128:(k + 1) * 128, :]),
                   writes=["o1"], dma="cst3")
            sc.add("dve", lambda e, k=k: e.tensor_scalar(wout[:, k, :], o1[:, :], hcol[:, k:k + 1], None, ALU.mult),
                   reads=["o1", "hcol"], writes=[("wout", k)])
        sc.add("sp", lambda e: e.dma_start(out=wlr[:, :], in_=W["w_lr_up"]), writes=["wlr"], dma="k4")
        sc.add("sp", lambda e: e.dma_start(out=nblr[:, :], in_=W["b_lr"].rearrange("(k p) -> p k", p=128),
                                           allow_slow_non_contiguous=True), writes=["nblr"], dma="k5")
        sc.add("dve", lambda e: e.tensor_scalar(nblr[:, :], nblr[:, :], -1.0, None, ALU.mult),
               reads=["nblr"], writes=["nblr"])
        sc.add("sp", lambda e: e.dma_start(out=dq[:, :, :], in_=C["dq"].partition_broadcast(128)),
               writes=["dq"], dma="k6")
        sc.add("sp", lambda e: e.dma_start(out=dk[:, :, :], in_=C["dk"].partition_broadcast(128)),
               writes=["dk"], dma="k7")
        sc.add("sp", lambda e: e.dma_start(out=rmask[:, :], in_=C["rmask"].partition_broadcast(128)),
               writes=["rmask"], dma="k8")
        sc.add("pool", lambda e: e.dma_start(out=maskT[:, :], in_=C["maskT"]), writes=["maskT"], dma="k10")
        sc.add("pool", lambda e: e.dma_start(out=pm[:, :], in_=C["perm"]), writes=["pm"], dma="k11")
        sc.add("pool", lambda e: e.memset(A[:, :, :], 0.0), writes=["A"])
        sc.add("pool", lambda e: e.memset(Sbf[:, :, :], 0.0), writes=["Sbf"])
        sc.add("pool", lambda e: e.memset(Sg[:, :, :], 0.0), writes=["Sg"])
        sc.add("pool", lambda e: e.memset(kpad[:, :, :, :], 0.0), writes=["kpad"])
        sc.add("pool", lambda e: e.memset(gqz[:, :, :], 0.0), writes=[("gqT", 0), ("gqT", 1)])

        def proj_fm(out_ps, wt, c0, ncols, b, outkey, extra_reads=()):
            def mm(pe):
                ins = None
                for k in range(8):
                    ins = pe.matmul(out_ps, wt[:, k, c0:c0 + ncols], xnT[b][:, k, :],
                                    start=(k == 0), stop=(k == 7))
                return ins
            sc.add("pe", mm, reads=wkeys(wt, c0) + [("xnT", b, j) for j in range(4)] + list(extra_reads), writes=[outkey])

        def proj_tm(out_ps, c0, ncols, b, j, outkeys):
            def mm(pe):
                ins = None
                for n0 in range(0, ncols, 512):
                    for k in range(8):
                        ins = pe.matmul(out_ps[:, n0:n0 + 512], xnT[b][:, k, j * 128:(j + 1) * 128],
                                        win[:, k, c0 + n0:c0 + n0 + 512], start=(k == 0), stop=(k == 7))
                return ins
            sc.add("pe", mm, reads=wkeys(win, c0) + [("xnT", b, j)], writes=list(outkeys))

        def phase_b(t):
            b = t % 2
            c0 = t * T
            sc.add("sp", lambda e: e.dma_start(out=cst[:, :], in_=C["cos"][:, c0:c0 + T]),
                   writes=["cst"], dma="cs0")
            sc.add("sp", lambda e: e.dma_start(out=snt[:, :], in_=C["sin"][:, c0:c0 + T]),
                   writes=["snt"], dma="cs1")
            n = 0
            for which in range(2):
                for h in range(4):
                    qa, qb = (Q[0], Q[1]) if n % 2 == 0 else (Q[2], Q[3])
                    ka, kb = (("Q", 0), ("Q", 1)) if n % 2 == 0 else (("Q", 2), ("Q", 3))
                    n += 1
                    col = which * 512 + h * 128
                    proj_fm(qa, win, col, 128, b, ka)
                    sc.add("act", lambda e, qa=qa: e.activation(qbb[:, :], qa, AF.Copy), reads=[ka], writes=["qbb"])
                    sc.add("pe", lambda pe, qb=qb: pe.matmul(qb, pm[:, :], qbb[:, :], start=True, stop=True),
                           reads=["qbb", "pm"], writes=[kb])
                    sc.add("dve", lambda e, qa=qa: e.tensor_tensor(t1[:, :], qa, cst[:, :], ALU.mult),
                           reads=[ka, "cst", "qbb"], writes=["t1"])
                    sc.add("dve", lambda e, qb=qb: e.tensor_tensor(t2[:, :], qb, snt[:, :], ALU.mult),
                           reads=[kb, "snt"], writes=["t2"])
                    sc.add("pool", lambda e: e.tensor_tensor(t1[:, :], t1[:, :], t2[:, :], ALU.add),
                           reads=["t1", "t2"], writes=["t1"])
                    dst_t = qT if which == 0 else kT
                    dec = dq if which == 0 else dk
                    dkey = "dq" if which == 0 else "dk"
                    okey = ("qT", h) if which == 0 else ("kT", h)
                    sc.add("pool", lambda e, dst_t=dst_t, dec=dec, h=h: e.tensor_tensor(
                        dst_t[:, h, :].rearrange("p (c l) -> p c l", c=4),
                        t1[:, :].rearrange("p (c l) -> p c l", c=4),
                        dec[:, h, :].unsqueeze(1).broadcast_to([128, 4, 128]), ALU.mult),
                        reads=["t1", dkey], writes=[okey])
            proj_fm(Q[7][0:16, :], win, 3584, 16, b, ("Q", 7))
            sc.add("act", lambda e: e.activation(glrT[:, :], Q[7][0:16, :], AF.Copy),
                   reads=[("Q", 7)], writes=["glrT"])
            for p in range(2):
                sc.add("pe", lambda pe, p=p: pe.matmul(Q[7], wlr[:, p * 128:(p + 1) * 128], glrT[:, :],
                                                       start=True, stop=True),
                       reads=["wlr", "glrT"], writes=[("Q", 7)])
                sc.add("act", lambda e, p=p: e.activation(ea[:, :], Q[7], AF.Exp, bias=nblr[:, p:p + 1], scale=-1.0),
                       reads=[("Q", 7), "nblr"], writes=["ea"])
                sc.add("act", lambda e: e.activation(ea[:, :], ea[:, :], AF.Ln, bias=1.0),
                       reads=["ea"], writes=["ea"])
                sc.add("dve", lambda e: e.tensor_tensor_scan(cs[:, :], rmask[:, :], ea[:, :], 0.0, ALU.mult, ALU.add),
                       reads=["ea", "rmask"], writes=["cs"])
                cs3 = cs[:, :].rearrange("p (c l) -> p c l", c=4)
                sc.add("act", lambda e, p=p, cs3=cs3: e.activation(emid[:, p, :], cs3[:, :, 63], AF.Exp, scale=-1.0 / 16),
                       reads=["cs"], writes=[("emid", p)])
                sc.add("act", lambda e, p=p, cs3=cs3: e.activation(eend[:, p, :], cs3[:, :, 127], AF.Exp, scale=-1.0 / 16),
                       reads=["cs"], writes=[("eend", p)])
                sc.add("dve", lambda e, p=p, cs3=cs3: e.tensor_tensor(dcs[:, p, :], cs3[:, :, 63], cs3[:, :, 127], ALU.subtract),
                       reads=["cs"], writes=[("dcs", p)])
                sc.add("act", lambda e, p=p: e.activation(eem[:, p, :], dcs[:, p, :], AF.Exp, scale=1.0 / 16),
                       reads=[("dcs", p)], writes=[("eem", p)])
                sc.add("dve", lambda e, p=p, cs3=cs3: e.tensor_copy(csm[:, p, :], cs3[:, :, 63]),
                       reads=["cs"], writes=[("csm", p)])
                sc.add("dve", lambda e, p=p, cs3=cs3: e.tensor_tensor(
                    cs3, cs3, csm[:, p, :].unsqueeze(2).broadcast_to([128, 4, 128]), ALU.subtract),
                    reads=["cs", ("csm", p)], writes=["cs"])
                sc.add("act", lambda e: e.activation(eqt[:, :], cs[:, :], AF.Exp, scale=-1.0 / 16),
                       reads=["cs"], writes=["eqt"])
                sc.add("act", lambda e: e.activation(ekt[:, :], cs[:, :], AF.Exp, scale=1.0 / 16, bias=LN_GK),
                       reads=["cs"], writes=["ekt"])
                proj_fm(Q[4], win, 2048 + p * 128, 128, b, ("Q", 4))
                proj_fm(Q[5], win, 2304 + p * 128, 128, b, ("Q", 5))
                for hh in range(2):
                    rs = slice(hh * 64, (hh + 1) * 64)
                    sc.add("dve", lambda e, p=p, hh=hh, rs=rs: e.tensor_tensor(gqz[rs, 2 * p + hh, :], Q[4][rs, :],
                                                                             eqt[rs, :], ALU.mult),
                           reads=[("Q", 4), "eqt"], writes=[("gqT", p)])
                sc.add("dve", lambda e, p=p: e.tensor_tensor(gkT[:, p, :], Q[5], ekt[:, :], ALU.mult),
                       reads=[("Q", 5), "ekt"], writes=[("gkT", p)])

        def chunk_front(t, c):
            b = t % 2
            cl = slice(c * 128, (c + 1) * 128)
            proj_tm(PA, 1024, 512, b, c, [("Q", 0)])
            sc.add("act", lambda e: e.activation(vr[:, :], Q[0], AF.Copy), reads=[("Q", 0)], writes=["vr"])
            proj_tm(PA[:, 512:1024], 2560, 512, b, c, [("Q", 1)])
            sc.add("act", lambda e: e.activation(vg[:, :], Q[1], AF.Copy), reads=[("Q", 1)], writes=["vg"])
            proj_tm(PB[:, 0:512], 1536, 512, b, c, [("Q", 2)])
            proj_tm(PB[:, 512:1024], 3072, 512, b, c, [("Q", 3)])
            sc.add("act", lambda e: e.activation(gate[:, :], PB[:, :], AF.Silu),
                   reads=[("Q", 2), ("Q", 3)], writes=["gate"])

            def mm_s(pe):
                ins = None
                for h in range(4):
                    ins = pe.matmul(Q[7][:, h * 128:(h + 1) * 128], kT[:, h, cl], qT[:, h, cl], start=True, stop=True)
                return ins
            sc.add("pe", mm_s, reads=[("qT", h) for h in range(4)] + [("kT", h) for h in range(4)], writes=[("Q", 7)])
            m4 = maskT[:, :].unsqueeze(1).broadcast_to([128, 4, 128])
            sc.add("dve", lambda e: e.tensor_tensor(sT[:, :].rearrange("p (h l) -> p h l", h=4),
                                                    Q[7].rearrange("p (h l) -> p h l", h=4), m4, ALU.mult),
                   reads=[("Q", 7), "maskT"], writes=["sT"])

            def mm_sg(pe):
                ins = None
                for h in range(4):
                    p, hh = h // 2, h % 2
                    ins = pe.matmul(Q[1][:, h * 128:(h + 1) * 128], gkT[:, p, cl], gqz[:, h, cl],
                                    start=True, stop=True)
                return ins
            sc.add("pe", mm_sg, reads=[("gqT", 0), ("gqT", 1), ("gkT", 0), ("gkT", 1)], writes=[("Q", 1)])
            sc.add("dve", lambda e: e.tensor_tensor(sTg[:, :].rearrange("p (h l) -> p h l", h=4),
                                                    Q[1].rearrange("p (h l) -> p h l", h=4), m4, ALU.mult),
                   reads=[("Q", 1), "maskT"], writes=["sTg"])

            def tr(pe):
                ins = None
                for h in range(4):
                    ins = pe.transpose(PT[:, h * 128:(h + 1) * 128], kT[:, h, cl], ident[:, :])
                for p in range(2):
                    ins = pe.transpose(PT[:, 512 + p * 128:512 + (p + 1) * 128], gkT[:, p, cl], ident[:, :])
                return ins
            sc.add("pe", tr, reads=[("kT", h) for h in range(4)] + [("gkT", 0), ("gkT", 1), "ident"], writes=["PT"])
            sc.add("dve", lambda e: e.tensor_copy(ktok[:, :], PT[:, 0:512]), reads=["PT"], writes=["ktok"])
            kp_ap = bass.AP(kpad, 0, [[512, 128], [256, 2], [192, 2], [1, 64]])
            sc.add("dve", lambda e: e.tensor_copy(kp_ap, PT[:, 512:768].rearrange("p (a b c) -> p a b c", a=2, b=2)),
                   reads=["PT"], writes=["kpad"])
            for p in range(2):
                sc.add("act", lambda e, p=p: e.activation(Spbf[:, p, :], Sg[:, p, :], AF.Copy,
                                                          scale=emid[:, p, c:c + 1]),
                       reads=[("Sg", p), ("emid", p)], writes=[("Spbf", p)])

            def mm_o(pe):
                ins = None
                for h in range(4):
                    hs = slice(h * 128, (h + 1) * 128)
                    pe.matmul(PC[:, hs], sT[:, hs], vr[:, hs], start=True, stop=False)
                    ins = pe.matmul(PC[:, hs], qT[:, h, cl], Sbf[:, h, :], start=False, stop=True)
                for h in range(4):
                    p, hh = h // 2, h % 2
                    hs = slice(h * 128, (h + 1) * 128)
                    og = slice(512 + h * 128, 512 + (h + 1) * 128)
                    pe.matmul(PC[:, og], sTg[:, hs], vg[:, hs], start=True, stop=False)
                    ins = pe.matmul(PC[:, og], gqz[:, h, cl], Spbf[:, p, :], start=False, stop=True)
                return ins
            sc.add("pe", mm_o, reads=["sT", "sTg", "vr", "vg", "Sbf", ("Spbf", 0), ("Spbf", 1)]
                   + [("qT", h) for h in range(4)] + [("gqT", 0), ("gqT", 1)], writes=[("Q", 4), ("Q", 5)])

            def mm_u(pe):
                ins = None
                for h in range(4):
                    hs = slice(h * 128, (h + 1) * 128)
                    ins = pe.matmul(Q[7][:, hs], ktok[:, hs], vr[:, hs], start=True, stop=True)
                return ins
            sc.add("pe", mm_u, reads=["ktok", "vr", "sT"], writes=[("Q", 7)])
            for h in range(4):
                hs = slice(h * 128, (h + 1) * 128)
                sc.add("dve", lambda e, h=h, hs=hs: e.scalar_tensor_tensor(A[:, h, :], A[:, h, :], GL[h], Q[7][:, hs],
                                                                       ALU.mult, ALU.add),
                       reads=[("Q", 7), "A"], writes=["A"])
                sc.add("act", lambda e, h=h: e.activation(Sbf[:, h, :], A[:, h, :], AF.Copy, scale=GL[h]),
                       reads=["A"], writes=["Sbf"])

            def mm_ug(pe):
                ins = None
                for p in range(2):
                    pe.matmul(Q[0][:, p * 128:(p + 1) * 128], kpad[:, p, 0, :], vg[:, (2 * p) * 128:(2 * p + 1) * 128],
                              start=True, stop=False)
                    ins = pe.matmul(Q[0][:, p * 128:(p + 1) * 128], kpad[:, p, 1, :],
                                    vg[:, (2 * p + 1) * 128:(2 * p + 2) * 128], start=False, stop=True)
                return ins
            sc.add("pe", mm_ug, reads=["kpad", "vg"], writes=[("Q", 0)])
            for p in range(2):
                sc.add("dve", lambda e, p=p: e.tensor_scalar(gtmp[:, :], Q[0][:, p * 128:(p + 1) * 128],
                                                             eem[:, p, c:c + 1], None, ALU.mult),
                       reads=[("Q", 0), ("eem", p)], writes=["gtmp"])
                sc.add("dve", lambda e, p=p: e.scalar_tensor_tensor(Sg[:, p, :], Sg[:, p, :], eend[:, p, c:c + 1],
                                                                    gtmp[:, :], ALU.mult, ALU.add),
                       reads=["gtmp", ("Sg", p), ("eend", p)], writes=[("Sg", p)])

            for h8 in range(8):
                hs = slice(h8 * 128, (h8 + 1) * 128)
                sc.add("act", lambda e, h8=h8, hs=hs: e.activation(junk[:, :], PC[:, hs], AF.Square,
                                                                   accum_out=ssq[:, h8:h8 + 1]),
                       reads=[("Q", 4), ("Q", 5)], writes=["junk", "ssq"])
            _rstd_ops(sc, ssq[:, :], rsq[:, :], nhalf[:, 0:1].broadcast_to([128, 8]), 128.0, "ssq", "rsq", "msq", msq[:, :])
            sc.add("dve", lambda e: e.tensor_tensor(o1[:, :].rearrange("p (h v) -> p h v", h=8),
                                                    PC[:, :].rearrange("p (h v) -> p h v", h=8),
                                                    rsq[:, :].unsqueeze(2).broadcast_to([128, 8, 128]), ALU.mult),
                   reads=[("Q", 4), ("Q", 5), "rsq"], writes=["o1"])
            g = t * 4 + c
            ob = g % 2
            sc.add("pool", lambda e: e.tensor_tensor(o3[ob][:, :], o1[:, :], gate[:, :], ALU.mult),
                   reads=["o1", "gate"], writes=[("o3", ob)])

        def chunk_back(t, c):
            g = t * 4 + c
            ob = g % 2
            sl = g % 2
            r0 = g * 128
            sc.add("sp", lambda e: e.dma_start(out=xres[sl][:, :], in_=src[r0:r0 + 128, :]),
                   reads=[(srcname, g)], writes=[("xres", sl)], dma="xres%d" % sl)

            def tr(pe):
                ins = None
                for m in range(8):
                    ins = pe.transpose(PT[:, m * 128:(m + 1) * 128], o3[ob][:, m * 128:(m + 1) * 128], ident[:, :])
                return ins
            sc.add("pe", tr, reads=[("o3", ob), "ident"], writes=["PT"])
            sc.add("act", lambda e: e.activation(oT[:, :, :].rearrange("p m t -> p (m t)"), PT[:, :], AF.Copy),
                   reads=["PT"], writes=["oT"])
            for dh in range(2):
                def mm(pe, dh=dh):
                    ins = None
                    for m in range(8):
                        ins = pe.matmul(Q[2 + dh], oT[:, m, :], wout[:, m, dh * 512:(dh + 1) * 512],
                                        start=(m == 0), stop=(m == 7))
                    return ins
                sc.add("pe", mm, reads=["oT"] + [("wout", m) for m in range(8)], writes=[("Q", 2 + dh)])
                sc.add("dve", lambda e, dh=dh: e.tensor_tensor(xres[sl][:, dh * 512:(dh + 1) * 512], Q[2 + dh],
                                                               xres[sl][:, dh * 512:(dh + 1) * 512], ALU.add),
                       reads=[("Q", 2 + dh), ("xres", sl)], writes=[("xres", sl)])
            sc.add("sp", lambda e: e.dma_start(out=dst[r0:r0 + 128, :], in_=xres[sl][:, :]),
                   reads=[("xres", sl)], writes=[(dstname, g)], dma="xst%d" % sl)

        for j in range(4):
            norm_sub(sc, B, src, srcname, PT, "PT", 0, j)
        pending = None
        for t in range(NT):
            phase_b(t)
            for c in range(4):
                chunk_front(t, c)
                if pending is not None:
                    chunk_back(*pending)
                pending = (t, c)
                if t + 1 < NT:
                    norm_sub(sc, B, src, srcname, PT, "PT", t + 1, c)
        chunk_back(*pending)
        sc.barrier()
        sc.flush()


GELU_C = 2.0 * 0.7978845608028654


def odd_phase(cx, src, dst, srcname, dstname, nrm_ap, W, C):
    nc, sc, S = cx.nc, cx.sc, cx.S
    T = 512
    NT = S // T
    tg = cx.tag()
    with contextlib.ExitStack() as st:
        def sb(name, shape, dt):
            return st.enter_context(nc.sbuf_tensor(name + tg, shape, dt))

        def ps(name, shape, dt):
            return st.enter_context(nc.psum_tensor(name + tg, shape, dt))

        win = sb("win", [128, 8, 2568], BF16)
        wout = sb("wout", [128, 8, D], BF16)
        wbd = sb("wbd", [128, 2, 4, 128], BF16)
        wqk = sb("wqk", [128, 2, 4, 128], BF16)
        B = alloc_norm(sb, T, nld=1)
        xnT = B["xnT"]
        nhalf = B["nhalf"]
        ident = B["ident"]
        identF = sb("identF", [128, 128], F32)
        sel = sb("sel", [4, 4, 128], F32)
        xres = [sb("xres%d" % i, [128, D], F32) for i in range(2)]
        ncol = sb("ncol", [128, 4], F32)
        lcw = sb("lcw", [128, 4, 4], F32)
        mcw = sb("mcw", [128, 4, 4], F32)
        pcol = sb("pcol", [128, 6, 4], F32)
        sp8 = sb("sp8", [128, 4], F32)
        sp16 = sb("sp16", [128, 4], F32)
        hb = sb("hb", [128, 2, 4], F32)
        hsp = sb("hsp", [128, 2, 4], F32)
        phalf = sb("phalf", [128, 1], F32)
        sw = [sb("sw%d" % i, [128, 4], F32) for i in range(6)]
        gb = sb("gb", [4, 2], F32)
        ngb = sb("ngb", [4, 1], F32)
        lxbuf = sb("lxbuf", [128, 4, 3 + T], F32)
        mubuf = sb("mubuf", [128, 4, 3 + T], F32)
        tA2 = [sb("tA%d" % i, [128, T], F32) for i in range(2)]
        tB2 = [sb("tB%d" % i, [128, T], F32) for i in range(2)]
        tC2 = [sb("tC%d" % i, [128, T], F32) for i in range(2)]
        tD2 = [sb("tD%d" % i, [128, T], F32) for i in range(2)]
        tE1 = sb("tE", [128, T], F32)
        tE2 = [tE1, tE1]
        tF2 = [sb("tF%d" % i, [128, T], F32) for i in range(2)]
        mA = sb("mA", [128, T], F32)
        mB = sb("mB", [128, T], F32)
        lxcb1 = sb("lxcb", [128, T], BF16)
        lxcb2 = [lxcb1, lxcb1]
        lruT2 = [sb("lruT%d" % i, [128, 4, T], BF16) for i in range(2)]
        hst = sb("hst", [128, 4], F32)
        mcb = sb("mcb", [128, 4, T], BF16)
        qT2 = [sb("qT%d" % i, [128, 4, T], BF16) for i in range(2)]
        kT2 = [sb("kT%d" % i, [128, 4, T], BF16) for i in range(2)]
        rmask = sb("rmask", [4, T], F32)
        rneg = sb("rneg", [4, T], F32)
        g_ = sb("g_", [4, T], F32)
        spf = sb("spf", [4, T], F32)
        csf = sb("csf", [4, T], F32)
        cm = sb("cm", [4, T], F32)
        wr = spf
        f1r = sb("f1r", [4, T], F32)
        f2r = sb("f2r", [4, T], F32)
        enr = sb("enr", [4, T], F32)
        dtm = sb("dtm", [4, T], F32)
        mall = sb("mall", [4, 8], F32)
        mx = sb("mx", [4, 4], F32)
        ncse = sb("ncse", [4, 4], F32)
        amax = sb("amax", [4, 4], F32)
        gx = sb("gx", [4, 8], F32)
        gt = sb("gt", [4, 4], F32)
        tcol2 = [sb("tcol%d" % i, [128, 4, 4, 4], F32) for i in range(2)]
        gsc2 = [sb("gsc%d" % i, [128, 4, 8], F32) for i in range(2)]
        vtok = sb("vtok", [128, 512], BF16)
        sigo = sb("sigo", [128, 512], F32)
        onec = sb("onec", [128, 1], BF16)
        maskT = sb("maskT", [128, 128], BF16)
        pT = sb("pT", [128, 4, 128], BF16)
        kw = sb("kw", [128, 4, 128], BF16)
        Cst = sb("Cst", [128, 4, 128], F32)
        Cbf = sb("Cbf", [128, 4, 128], BF16)
        nst = sb("nst", [128, 4], F32)
        nbf = sb("nbf", [128, 4], BF16)
        tm1 = sb("tm1", [128, 512], F32)
        tm2 = sb("tm2", [128, 512], F32)
        tm3 = tm2
        wcb = sb("wcb", [128, 4], BF16)
        f12 = sb("f12", [128, 8], F32)
        tn2 = sb("tn2", [128, 4], F32)
        tn = sb("tn", [128, 4], F32)
        dn = sb("dn", [128, 4], F32)
        dn2 = sb("dn2", [128, 4], F32)
        rden = sb("rden", [128, 4], F32)
        hm = sb("hm", [128, 512], F32)
        junk = sb("junk", [128, 128], BF16)
        ssq = sb("ssq", [128, 4], F32)
        msq = sb("msq", [128, 4], F32)
        rsq = sb("rsq", [128, 4], F32)
        o3 = [sb("o3%d" % i, [128, 512], BF16) for i in range(2)]
        oTm = sb("oTm", [128, 4, 128], BF16)
        PA = ps("PA", [128, 1024], F32)
        PB = ps("PB", [128, 1024], F32)
        PC = ps("PC", [128, 1024], F32)
        PD = ps("PD", [128, 1024], F32)
        Q = [PA[:, 0:512], PA[:, 512:1024], PB[:, 0:512], PB[:, 512:1024],
             PC[:, 0:512], PC[:, 512:1024], PD[:, 0:512], PD[:, 512:1024]]
        PT = PD[:, 0:512].bitcast(BF16)

        init_norm(sc, B, nrm_ap, C["ident"])
        w_in, w_out = W["w_in"], W["w_out"]
        kc = [0]

        def kch():
            kc[0] += 1
            return "ok%d" % kc[0]

        def cload(eng, out_ap, in_ap, key, **kw_):
            sc.add(eng, lambda e: e.dma_start(out=out_ap, in_=in_ap, **kw_), writes=[key], dma=kch())

        w_in_v = w_in.rearrange("(k p) c -> p k c", p=128)
        OG = [(0, 1024), (1024, 1536), (2560, 2568), (1536, 2560)]
        for gi, (c0_, c1_) in enumerate(OG):
            sc.add("pool", lambda e, c0_=c0_, c1_=c1_: e.dma_start(out=win[:, :, c0_:c1_], in_=w_in_v[:, :, c0_:c1_]),
                   writes=[("win", gi)], dma="w%d" % gi)

        def wkeys(c0):
            for gi, (a, b_) in enumerate(OG):
                if a <= c0 < b_:
                    return [("win", gi)]
            raise ValueError(c0)
        sc.add("pool", lambda e: e.memset(wbd[:, :, :, :], 0.0), writes=["wbd"])
        for ai, nm in enumerate(("lru_wa", "lru_wx")):
            for bb in range(2):
                sc.add("pool", lambda e, ai=ai, nm=nm, bb=bb: e.dma_start(
                    out=wbd[bb * 64:(bb + 1) * 64, ai, :, bb * 64:(bb + 1) * 64],
                    in_=W[nm].rearrange("(c two) i j -> two i c j", two=2)[bb]),
                    reads=["wbd"], writes=[("wbdb", ai, bb)], dma="wb%d" % (ai * 2 + bb))
        for ai, nm in enumerate(("ml_wq", "ml_wk")):
            sc.add("pool", lambda e, ai=ai, nm=nm: e.dma_start(out=wqk[:, ai, :, :], in_=W[nm].rearrange("h i j -> i h j")),
                   writes=[("wqk", ai)], dma="wq%d" % ai)
        cload("pool", identF[:, :], C["identF"], "identF")
        cload("pool", sel[:, :, :], C["sel"], "sel")
        cload("pool", maskT[:, :], C["maskT"], "maskT")
        cload("sp", ncol[:, :], W["ml_norm"].rearrange("(k p) -> p k", p=128), "ncol", allow_slow_non_contiguous=True)
        for j in range(4):
            cload("sp", lcw[:, j, :], W["lru_conv_w"][j].rearrange("(k p) -> p k", p=128), ("lcw", j),
                  allow_slow_non_contiguous=True)
            cload("sp", mcw[:, j, :], W["ml_conv_w"][j].rearrange("(k p) -> p k", p=128), ("mcw", j),
                  allow_slow_non_contiguous=True)
        for qi, nm in enumerate(("lru_conv_b", "ml_conv_b", "lru_ba", "lru_bx", "lru_lambda")):
            cload("sp", pcol[:, qi, :], W[nm].rearrange("(k p) -> p k", p=128), ("pcol", qi),
                  allow_slow_non_contiguous=True)
        cload("sp", gb[:, 0:1], W["ml_bi"].rearrange("(h o) -> h o", o=1), ("gb", 0))
        cload("sp", gb[:, 1:2], W["ml_bf"].rearrange("(h o) -> h o", o=1), ("gb", 1))
        sc.add("dve", lambda e: e.tensor_scalar(ngb[:, :], gb[:, 1:2], -1.0, None, ALU.mult),
               reads=[("gb", 1)], writes=["ngb"])
        cload("sp", rmask[:, :], C["rmask"].partition_broadcast(4), "rmask")
        cload("sp", rneg[:, :], C["rneg"].partition_broadcast(4), "rneg")
        for k in range(8):
            sc.add("sp", lambda e, k=k: e.dma_start(out=hm[:, :].rearrange("p (a b) -> p a b", a=1)[:, 0, :],
                                                    in_=w_out[k * 128:(k + 1) * 128, 0:512]),
                   writes=["hm"], dma="cst3")
            sc.add("sp", lambda e, k=k: e.dma_start(out=sigo[:, :], in_=w_out[k * 128:(k + 1) * 128, 512:1024]),
                   writes=["sigo"], dma="cst4")
            if k < 4:
                sc.add("act", lambda e, k=k: e.activation(wout[:, k, 0:512], hm[:, :], AF.Copy, scale=0.5),
                       reads=["hm"], writes=[("wout", k)])
                sc.add("act", lambda e, k=k: e.activation(wout[:, k, 512:1024], sigo[:, :], AF.Copy, scale=0.5),
                       reads=["sigo"], writes=[("wout", k)])
            else:
                sc.add("dve", lambda e, k=k: e.tensor_scalar(wout[:, k, 0:512], hm[:, :], ncol[:, k - 4:k - 3], None,
                                                             ALU.mult), reads=["hm", "ncol"], writes=[("wout", k)])
                sc.add("dve", lambda e, k=k: e.tensor_scalar(wout[:, k, 512:1024], sigo[:, :], ncol[:, k - 4:k - 3],
                                                             None, ALU.mult), reads=["sigo", "ncol"], writes=[("wout", k)])
        lam = pcol[:, 4, :]
        s0, s1, s2, s3, s4, s5 = [t_[:, :] for t_ in sw]
        sc.add("dve", lambda e: e.tensor_scalar(s0, lam, -1.0, None, ALU.mult), reads=[("pcol", 4)], writes=["s0"])
        sc.add("dve", lambda e: e.tensor_tensor(s0, s0, lam, ALU.max), reads=[("pcol", 4), "s0"], writes=["s0"])
        sc.add("act", lambda e: e.activation(s1, s0, AF.Exp, scale=-1.0), reads=["s0"], writes=["s1"])
        sc.add("dve", lambda e: e.tensor_scalar(s2, s1, 2.0, None, ALU.add), reads=["s1"], writes=["s2"])
        sc.add("dve", lambda e: e.reciprocal(s2, s2), reads=["s2"], writes=["s2"])
        sc.add("dve", lambda e: e.tensor_tensor(s2, s2, s1, ALU.mult), reads=["s2", "s1"], writes=["s2"])
        sc.add("dve", lambda e: e.tensor_tensor(s3, s2, s2, ALU.mult), reads=["s2"], writes=["s3"])
        sc.add("dve", lambda e: e.memset(s4, 1.0 / 15.0), writes=["s4"])
        for kk in (13, 11, 9, 7, 5, 3, 1):
            sc.add("dve", lambda e: e.tensor_tensor(s4, s4, s3, ALU.mult), reads=["s4", "s3"], writes=["s4"])
            sc.add("dve", lambda e, kk=kk: e.tensor_scalar(s4, s4, 1.0 / kk, None, ALU.add), reads=["s4"], writes=["s4"])
        sc.add("dve", lambda e: e.tensor_tensor(s4, s4, s2, ALU.mult), reads=["s4", "s2"], writes=["s4"])
        sc.add("dve", lambda e: e.tensor_scalar(s5, lam, -1.0, 0.0, ALU.mult, ALU.max), reads=[("pcol", 4)], writes=["s5"])
        sc.add("dve", lambda e: e.scalar_tensor_tensor(s5, s4, 2.0, s5, ALU.mult, ALU.add), reads=["s4", "s5"], writes=["s5"])
        sc.add("dve", lambda e: e.tensor_scalar(sp8[:, :], s5, -8.0, None, ALU.mult), reads=["s5"], writes=["sp8"])
        sc.add("dve", lambda e: e.tensor_scalar(sp16[:, :], s5, -16.0, None, ALU.mult), reads=["s5"], writes=["sp16"])
        sc.add("dve", lambda e: e.tensor_scalar(hsp[:, 0, :], s5, -4.0, None, ALU.mult), reads=["s5"], writes=["hsp"])
        sc.add("dve", lambda e: e.tensor_scalar(hsp[:, 1, :], s5, -8.0, None, ALU.mult), reads=["s5", "hsp"], writes=["hsp"])
        sc.add("dve", lambda e: e.tensor_scalar(hb[:, 0, :], pcol[:, 2, :], 0.5, None, ALU.mult), reads=[("pcol", 2)], writes=["hb"])
        sc.add("dve", lambda e: e.tensor_scalar(hb[:, 1, :], pcol[:, 3, :], 0.5, None, ALU.mult), reads=[("pcol", 3), "hb"], writes=["hb"])
        sc.add("pool", lambda e: e.memset(phalf[:, :], 0.5), writes=["phalf"])
        sc.add("pool", lambda e: e.memset(lxbuf[:, :, 0:3], 0.0), writes=["lxbuf"])
        sc.add("pool", lambda e: e.memset(mubuf[:, :, 0:3], 0.0), writes=["mubuf"])
        sc.add("pool", lambda e: e.memset(hst[:, :], 0.0), writes=["hst"])
        sc.add("pool", lambda e: e.memset(mall[:, :], 0.0), writes=["mall"])
        sc.add("pool", lambda e: e.memset(Cst[:, :, :], 0.0), writes=["Cst"])
        sc.add("pool", lambda e: e.memset(Cbf[:, :, :], 0.0), writes=["Cbf"])
        sc.add("pool", lambda e: e.memset(nst[:, :], 0.0), writes=["nst"])
        sc.add("pool", lambda e: e.memset(nbf[:, :], 0.0), writes=["nbf"])
        sc.add("pool", lambda e: e.memset(onec[:, :], 1.0), writes=["onec"])

        def proj_fm(out_ps, c0, ncols, b, outkey):
            def mm(pe):
                ins = None
                for k in range(8):
                    ins = pe.matmul(out_ps, win[:, k, c0:c0 + ncols], xnT[b][:, k, :], start=(k == 0), stop=(k == 7))
                return ins
            sc.add("pe", mm, reads=wkeys(c0) + [("xnT", b, j) for j in range(4)], writes=[outkey])

        def proj_tm(out_ps, c0, b, j, outkey):
            def mm(pe):
                ins = None
                for k in range(8):
                    ins = pe.matmul(out_ps, xnT[b][:, k, j * 128:(j + 1) * 128], win[:, k, c0:c0 + 512],
                                    start=(k == 0), stop=(k == 7))
                return ins
            sc.add("pe", mm, reads=wkeys(c0) + [("xnT", b, j)], writes=[outkey])

        def conv_pool(buf, c, taps, bias_ap, out_ap, tmp_ap, bufkey, outkey, tapkey, biaskey, tmpkey):
            sc.add("pool", lambda e: e.tensor_scalar(out_ap, buf[:, c, 0:T], taps[:, 0, c:c + 1], bias_ap, ALU.mult, ALU.add),
                   reads=[bufkey, (tapkey, 0), biaskey], writes=[outkey])
            for j in range(1, 4):
                sc.add("pool", lambda e, j=j: e.tensor_scalar(tmp_ap, buf[:, c, j:j + T], taps[:, j, c:c + 1], None, ALU.mult),
                       reads=[bufkey, (tapkey, j)], writes=[tmpkey])
                sc.add("pool", lambda e: e.tensor_tensor(out_ap, out_ap, tmp_ap, ALU.add),
                       reads=[outkey, tmpkey], writes=[outkey])

        def conv(buf, c, taps, bias_ap, out_ap, bufkey, outkey, tapkey, biaskey):
            sc.add("dve", lambda e: e.tensor_scalar(out_ap, buf[:, c, 0:T], taps[:, 0, c:c + 1], bias_ap, ALU.mult, ALU.add),
                   reads=[bufkey, (tapkey, 0), biaskey], writes=[outkey])
            for j in range(1, 4):
                sc.add("dve", lambda e, j=j: e.scalar_tensor_tensor(out_ap, buf[:, c, j:j + T], taps[:, j, c:c + 1], out_ap,
                                                                    ALU.mult, ALU.add),
                       reads=[bufkey, outkey, (tapkey, j)], writes=[outkey])

        def lru_piece(t, c):
            b = t % 2
            lruT, qT, kT, tcol, gsc = lruT2[b], qT2[b], kT2[b], tcol2[b], gsc2[b]
            pp = c % 2
            tA, tB, tC, tD, tE, tF, lxcb = [x[pp] for x in (tA2, tB2, tC2, tD2, tE2, tF2, lxcb2)]
            if True:
                qa = Q[(2 * c) % 4]
                qk = ("Q", (2 * c) % 4)
                qb = Q[(2 * c + 1) % 4]
                qbk = ("Q", (2 * c + 1) % 4)
                proj_fm(qa, 512 + c * 128, 128, b, qk)
                sc.add("act", lambda e, c=c, qa=qa: e.activation(lxbuf[:, c, 3:3 + T], qa, AF.Copy),
                       reads=[qk], writes=[("lxbuf", c)])
                proj_fm(qb, c * 128, 128, b, qbk)
                sc.add("act", lambda e, qb=qb: e.activation(tF[:, :], qb, AF.Copy), reads=[qbk], writes=[("tF", pp)])
                conv(lxbuf, c, lcw, pcol[:, 0, c:c + 1], tA[:, :], ("lxbuf", c), ("tA", pp), "lcw", ("pcol", 0))
                sc.add("pool", lambda e, c=c: e.tensor_copy(lxbuf[:, c, 0:3], lxbuf[:, c, T:T + 3]),
                       reads=[("tA", pp), ("lxbuf", c)], writes=[("lxbuf", c)])
                sc.add("act", lambda e: e.activation(lxcb[:, :], tA[:, :], AF.Copy), reads=[("tA", pp)], writes=["lxcb"])
                for ai in range(2):
                    sc.add("pe", lambda pe, ai=ai, c=c: pe.matmul(Q[4 + ai], wbd[:, ai, c, :], lxcb[:, :],
                                                                   start=True, stop=True),
                           reads=["lxcb", "wbd"] + [("wbdb", a_, bb) for a_ in range(2) for bb in range(2)],
                           writes=[("Q", 4 + ai)])
                sc.add("act", lambda e, c=c: e.activation(tB[:, :], Q[4], AF.Tanh, bias=hb[:, 0, c:c + 1], scale=0.5),
                       reads=[("Q", 4), "hb"], writes=[("tB", pp)])
                sc.add("act", lambda e, c=c: e.activation(tC[:, :], Q[5], AF.Tanh, bias=hb[:, 1, c:c + 1], scale=0.5),
                       reads=[("Q", 5), "hb"], writes=[("tC", pp)])
                sc.add("dve", lambda e: e.scalar_tensor_tensor(tC[:, :], tC[:, :], 1.0, tA[:, :], ALU.add, ALU.mult),
                       reads=[("tC", pp), ("tA", pp)], writes=[("tC", pp)])
                sc.add("act", lambda e: e.activation(tA[:, :], tF[:, :], AF.Square), reads=[("tF", pp), ("tC", pp)], writes=[("tA", pp)])
                sc.add("dve", lambda e: e.tensor_scalar(tA[:, :], tA[:, :], 0.044715, 1.0, ALU.mult, ALU.add),
                       reads=[("tA", pp)], writes=[("tA", pp)])
                sc.add("pool", lambda e: e.tensor_tensor(tA[:, :], tA[:, :], tF[:, :], ALU.mult),
                       reads=[("tA", pp), ("tF", pp)], writes=[("tA", pp)])
                sc.add("act", lambda e: e.activation(tA[:, :], tA[:, :], AF.Tanh, scale=0.5 * GELU_C),
                       reads=[("tA", pp)], writes=[("tA", pp)])
                sc.add("dve", lambda e: e.scalar_tensor_tensor(tF[:, :], tA[:, :], 1.0, tF[:, :], ALU.add, ALU.mult),
                       reads=[("tA", pp), ("tF", pp)], writes=[("tF", pp)])
                sc.add("act", lambda e, c=c: e.activation(tD[:, :], tB[:, :], AF.Exp, scale=hsp[:, 0, c:c + 1],
                                                          bias=hsp[:, 0, c:c + 1]),
                       reads=[("tB", pp), "hsp"], writes=[("tD", pp)])
                sc.add("act", lambda e, c=c: e.activation(tE[:, :], tB[:, :], AF.Exp, scale=hsp[:, 1, c:c + 1],
                                                          bias=hsp[:, 1, c:c + 1]),
                       reads=[("tB", pp), "hsp"], writes=["tE"])
                sc.add("act", lambda e: e.activation(tE[:, :], tE[:, :], AF.Sqrt, scale=-0.25, bias=0.25),
                       reads=["tE"], writes=["tE"])
                sc.add("pool", lambda e: e.tensor_tensor(tC[:, :], tC[:, :], tE[:, :], ALU.mult),
                       reads=[("tC", pp), "tE"], writes=[("tC", pp)])
                sc.add("dve", lambda e, c=c: e.tensor_tensor_scan(tB[:, :], tD[:, :], tC[:, :], hst[:, c:c + 1],
                                                                  ALU.mult, ALU.add),
                       reads=[("tD", pp), ("tC", pp), "hst", ("tB", pp)], writes=[("tB", pp)])
                sc.add("act", lambda e, c=c: e.activation(hst[:, c:c + 1], tB[:, T - 1:T], AF.Copy),
                       reads=[("tB", pp)], writes=["hst"])
                sc.add("pool", lambda e, c=c: e.tensor_tensor(lruT[:, c, :], tB[:, :], tF[:, :], ALU.mult),
                       reads=[("tB", pp), ("tF", pp)], writes=[("lruT", b, c)])
        def ml_piece(t, h):
            b = t % 2
            lruT, qT, kT, tcol, gsc = lruT2[b], qT2[b], kT2[b], tcol2[b], gsc2[b]
            if True:
                qa = Q[h % 4]
                qk = ("Q", h % 4)
                proj_fm(qa, 1024 + h * 128, 128, b, qk)
                sc.add("act", lambda e, h=h, qa=qa: e.activation(mubuf[:, h, 3:3 + T], qa, AF.Copy),
                       reads=[qk], writes=[("mubuf", h)])
                conv(mubuf, h, mcw, pcol[:, 1, h:h + 1], mA[:, :], ("mubuf", h), "mA", "mcw", ("pcol", 1))
                sc.add("pool", lambda e, h=h: e.tensor_copy(mubuf[:, h, 0:3], mubuf[:, h, T:T + 3]),
                       reads=["mA", ("mubuf", h)], writes=[("mubuf", h)])
                sc.add("act", lambda e: e.activation(mB[:, :], mA[:, :], AF.Tanh, scale=0.5), reads=["mA", "mB"], writes=["mB"])
                sc.add("dve", lambda e, h=h: e.scalar_tensor_tensor(mcb[:, h, :], mB[:, :], 1.0, mA[:, :], ALU.add, ALU.mult),
                       reads=["mA", "mB"], writes=[("mcb", h)])
                for ai in range(2):
                    sc.add("pe", lambda pe, ai=ai, h=h: pe.matmul(Q[4 + ai], wqk[:, ai, h, :], mcb[:, h, :],
                                                                   start=True, stop=True),
                           reads=[("mcb", h)] + [("wqk", a_) for a_ in range(2)], writes=[("Q", 4 + ai)])
                sc.add("act", lambda e, h=h: e.activation(qT[:, h, :], Q[4], AF.Copy, scale=0.5), reads=[("Q", 4)],
                       writes=[("qT", b, h)])
                sc.add("act", lambda e, h=h: e.activation(kT[:, h, :], Q[5], AF.Copy, scale=0.5 * 128.0 ** -0.5),
                       reads=[("Q", 5)], writes=[("kT", b, h)])
        def gates_piece(t):
            b = t % 2
            lruT, qT, kT, tcol, gsc = lruT2[b], qT2[b], kT2[b], tcol2[b], gsc2[b]
            proj_fm(Q[6][0:4, :], 2560, 4, b, ("Q", 6))
            proj_fm(Q[7][0:4, :], 2564, 4, b, ("Q", 7))
            sc.add("act", lambda e: e.activation(spf[:, :], Q[7][0:4, :], AF.Exp, scale=-1.0, bias=ngb[:, 0:1]),
                   reads=[("Q", 7), "ngb"], writes=["spf"])
            sc.add("act", lambda e: e.activation(spf[:, :], spf[:, :], AF.Ln, bias=1.0), reads=["spf"], writes=["spf"])
            sc.add("dve", lambda e: e.tensor_tensor_scan(csf[:, :], rmask[:, :], spf[:, :], 0.0, ALU.mult, ALU.add),
                   reads=["spf", "rmask"], writes=["csf"])
            sc.add("dve", lambda e: e.scalar_tensor_tensor(g_[:, :], Q[6][0:4, :], gb[:, 0:1], csf[:, :], ALU.add, ALU.add),
                   reads=[("Q", 6), ("gb", 0), "csf"], writes=["g"])
            g3 = g_[:, :].rearrange("p (c l) -> p c l", c=4)
            csf3 = csf[:, :].rearrange("p (c l) -> p c l", c=4)
            sc.add("dve", lambda e: e.reduce_max(mx[:, :], g3, AX.X), reads=["g"], writes=["mx"])
            sc.add("dve", lambda e: e.tensor_scalar(ncse[:, :], csf3[:, :, 127], -1.0, None, ALU.mult),
                   reads=["csf"], writes=["ncse"])
            sc.add("dve", lambda e: e.tensor_tensor(amax[:, :], mx[:, :], ncse[:, :], ALU.add),
                   reads=["mx", "ncse"], writes=["amax"])
            sc.add("dve", lambda e: e.tensor_tensor_scan(mall[:, 1:5], ncse[:, :], amax[:, :], mall[:, 0:1], ALU.add, ALU.max),
                   reads=["ncse", "amax", "mall"], writes=["mall"])
            sc.add("dve", lambda e: e.tensor_tensor_scan(cm[:, :], rneg[:, :], g_[:, :], -1.0e30, ALU.add, ALU.max),
                   reads=["g", "rneg"], writes=["cm"])
            cm3 = cm[:, :].rearrange("p (c l) -> p c l", c=4)
            m_in = mall[:, 0:4].unsqueeze(2).broadcast_to([4, 4, 128])
            mxb = mx[:, :].unsqueeze(2).broadcast_to([4, 4, 128])
            sc.add("dve", lambda e: e.tensor_tensor(cm3, cm3, m_in, ALU.max), reads=["cm", "mall"], writes=["cm"])
            d3 = dtm[:, :].rearrange("p (c l) -> p c l", c=4)
            sc.add("pool", lambda e: e.tensor_tensor(d3, g3, mxb, ALU.subtract), reads=["g", "mx"], writes=["dtm"])
            sc.add("act", lambda e: e.activation(wr[:, :], dtm[:, :], AF.Exp), reads=["dtm"], writes=["spf"])
            sc.add("pool", lambda e: e.tensor_tensor(d3, cm3, mxb, ALU.subtract), reads=["cm", "mx", "dtm"], writes=["dtm"])
            sc.add("act", lambda e: e.activation(f1r[:, :], dtm[:, :], AF.Exp, scale=-1.0), reads=["dtm"], writes=["f1r"])
            sc.add("pool", lambda e: e.tensor_tensor(d3, cm3, m_in, ALU.subtract), reads=["cm", "mall", "dtm"], writes=["dtm"])
            sc.add("act", lambda e: e.activation(f2r[:, :], dtm[:, :], AF.Exp, scale=-1.0), reads=["dtm"], writes=["f2r"])
            sc.add("pool", lambda e: e.tensor_tensor(dtm[:, :], csf[:, :], cm[:, :], ALU.subtract),
                   reads=["cm", "csf", "dtm"], writes=["dtm"])
            sc.add("act", lambda e: e.activation(enr[:, :], dtm[:, :], AF.Exp), reads=["dtm"], writes=["enr"])
            sc.add("dve", lambda e: e.tensor_tensor(gt[:, :], ncse[:, :], mall[:, 0:4], ALU.add),
                   reads=["ncse", "mall"], writes=["gt"])
            sc.add("dve", lambda e: e.tensor_tensor(gt[:, :], gt[:, :], mall[:, 1:5], ALU.subtract),
                   reads=["gt", "mall"], writes=["gt"])
            sc.add("act", lambda e: e.activation(gx[:, 0:4], gt[:, :], AF.Exp), reads=["gt"], writes=["gx"])
            sc.add("dve", lambda e: e.tensor_tensor(gt[:, :], amax[:, :], mall[:, 1:5], ALU.subtract),
                   reads=["amax", "mall", "gx"], writes=["gt"])
            sc.add("act", lambda e: e.activation(gx[:, 4:8], gt[:, :], AF.Exp), reads=["gt"], writes=["gx"])
            sc.add("dve", lambda e: e.tensor_copy(mall[:, 0:1], mall[:, 4:5]), reads=["mall", "gt", "cm", "dtm"],
                   writes=["mall"])
            def mm_g(pe):
                ins = None
                for h in range(4):
                    ins = pe.matmul(Q[6][:, h * 8:(h + 1) * 8], sel[:, h, :], gx[:, :], start=True, stop=True)
                return ins
            sc.add("pe", mm_g, reads=["sel", "gx"], writes=[("Q", 6)])
            sc.add("act", lambda e: e.activation(gsc[:, :, :].rearrange("p h x -> p (h x)"), Q[6][:, 0:32], AF.Copy),
                   reads=[("Q", 6)], writes=[("gsc", b)])
            def tr(pe):
                ins = None
                for c in range(4):
                    for qi, row in enumerate((wr, f1r, f2r, enr)):
                        o0 = (c * 4 + qi) * 4
                        ins = pe.transpose(Q[7][:, o0:o0 + 4], row[:, c * 128:(c + 1) * 128], identF[0:4, 0:4])
                return ins
            sc.add("pe", tr, reads=["spf", "f1r", "f2r", "enr", "identF"], writes=[("Q", 7)])
            sc.add("act", lambda e: e.activation(tcol[:, :, :, :].rearrange("p c q h -> p (c q h)"), Q[7][:, 0:64], AF.Copy),
                   reads=[("Q", 7)], writes=[("tcol", b)])

        def chunk_front(t, c):
            b = t % 2
            lruT, qT, kT, tcol, gsc = lruT2[b], qT2[b], kT2[b], tcol2[b], gsc2[b]
            cl = slice(c * 128, (c + 1) * 128)
            wcolf = tcol[:, c, 0, :]
            f1c = tcol[:, c, 1, :]
            f2c = tcol[:, c, 2, :]
            enc = tcol[:, c, 3, :]

            def b4(ap):
                return ap.unsqueeze(2).broadcast_to([128, 4, 128])

            def v4(ap):
                return ap.rearrange("p (h v) -> p h v", h=4)

            proj_tm(Q[0], 1536, b, c, ("Q", 0))
            sc.add("dve", lambda e: e.tensor_tensor(v4(vtok[:, :]), v4(Q[0]), b4(wcolf), ALU.mult),
                   reads=[("Q", 0), ("tcol", b)], writes=["vtok"])
            sc.add("act", lambda e: e.activation(wcb[:, :], wcolf, AF.Copy), reads=[("tcol", b)], writes=["wcb"])
            proj_tm(Q[1], 2048, b, c, ("Q", 1))
            sc.add("act", lambda e: e.activation(sigo[:, :], Q[1], AF.Tanh, scale=0.5), reads=[("Q", 1)], writes=["sigo"])

            def mm_s(pe):
                ins = None
                for h in range(4):
                    ins = pe.matmul(Q[2][:, h * 128:(h + 1) * 128], kT[:, h, cl], qT[:, h, cl], start=True, stop=True)
                return ins
            sc.add("pe", mm_s, reads=[("qT", b, h) for h in range(4)] + [("kT", b, h) for h in range(4)], writes=[("Q", 2)])
            sc.add("dve", lambda e: e.tensor_tensor(pT[:, :, :], v4(Q[2]),
                                                    maskT[:, :].unsqueeze(1).broadcast_to([128, 4, 128]), ALU.mult),
                   reads=[("Q", 2), "maskT"], writes=["pT"])

            def tr(pe):
                ins = None
                for h in range(4):
                    ins = pe.transpose(PT[:, h * 128:(h + 1) * 128], kT[:, h, cl], ident[:, :])
                return ins
            sc.add("pe", tr, reads=[("kT", b, h) for h in range(4)] + ["ident"], writes=[("Q", 6)])
            sc.add("act", lambda e: e.activation(kw[:, :, :].rearrange("p h d -> p (h d)"), PT[:, 0:512], AF.Copy),
                   reads=[("Q", 6)], writes=["kw"])

            def mm_x(pe):
                ins = None
                for h in range(4):
                    hs = slice(h * 128, (h + 1) * 128)
                    pe.matmul(Q[3][:, hs], pT[:, h, :], vtok[:, hs], start=True, stop=True)
                    pe.matmul(Q[4][:, hs], qT[:, h, cl], Cbf[:, h, :], start=True, stop=True)
                    pe.matmul(Q[5][:, h:h + 1], pT[:, h, :], wcb[:, h:h + 1], start=True, stop=True)
                    ins = pe.matmul(Q[5][:, 4 + h:5 + h], qT[:, h, cl], nbf[:, h:h + 1], start=True, stop=True)
                return ins
            sc.add("pe", mm_x, reads=["pT", "vtok", "Cbf", "nbf", "wcb"] + [("qT", b, h) for h in range(4)],
                   writes=[("Q", 3), ("Q", 4), ("Q", 5)])

            def mm_u(pe):
                ins = None
                for h in range(4):
                    hs = slice(h * 128, (h + 1) * 128)
                    pe.matmul(Q[7][:, hs], kw[:, h, :], vtok[:, hs], start=True, stop=True)
                    ins = pe.matmul(Q[1][:, h:h + 1], kw[:, h, :], wcb[:, h:h + 1], start=True, stop=True)
                return ins
            sc.add("pe", mm_u, reads=["kw", "vtok", "wcb"], writes=[("Q", 7), ("Q", 1)])

            sc.add("dve", lambda e: e.tensor_tensor(tn[:, :], Q[5][:, 4:8], f2c, ALU.mult),
                   reads=[("Q", 5), ("tcol", b)], writes=["tn"])
            sc.add("dve", lambda e: e.tensor_tensor(dn[:, :], Q[5][:, 0:4], f1c, ALU.mult),
                   reads=[("Q", 5), ("tcol", b)], writes=["dn"])
            sc.add("dve", lambda e: e.tensor_tensor(dn[:, :], dn[:, :], tn[:, :], ALU.add), reads=["dn", "tn"], writes=["dn"])
            sc.add("dve", lambda e: e.tensor_scalar(dn2[:, :], dn[:, :], -1.0, None, ALU.mult), reads=["dn"], writes=["dn2"])
            sc.add("dve", lambda e: e.tensor_tensor(dn[:, :], dn[:, :], dn2[:, :], ALU.max), reads=["dn", "dn2"], writes=["dn"])
            sc.add("dve", lambda e: e.tensor_tensor(dn[:, :], dn[:, :], enc, ALU.max), reads=["dn", ("tcol", b)], writes=["dn"])
            sc.add("dve", lambda e: e.reciprocal(rden[:, :], dn[:, :]), reads=["dn"], writes=["rden"])
            sc.add("dve", lambda e: e.scalar_tensor_tensor(f12[:, 0:4], f1c, 0.5, rden[:, :], ALU.mult, ALU.mult),
                   reads=["rden", ("tcol", b)], writes=["f12a"])
            sc.add("dve", lambda e: e.scalar_tensor_tensor(f12[:, 4:8], f2c, 0.5, rden[:, :], ALU.mult, ALU.mult),
                   reads=["rden", ("tcol", b)], writes=["f12b"])
            sc.add("dve", lambda e: e.tensor_tensor(v4(tm1[:, :]), v4(Q[4]), b4(f12[:, 4:8]), ALU.mult),
                   reads=[("Q", 4), "f12b"], writes=["tm1"])
            sc.add("dve", lambda e: e.tensor_tensor(v4(tm2[:, :]), v4(Q[3]), b4(f12[:, 0:4]), ALU.mult),
                   reads=[("Q", 3), "f12a"], writes=["tm2"])
            sc.add("pool", lambda e: e.tensor_tensor(tm1[:, :], tm1[:, :], tm2[:, :], ALU.add),
                   reads=["tm1", "tm2"], writes=["tm1"])
            sc.add("dve", lambda e: e.scalar_tensor_tensor(hm[:, :], sigo[:, :], 1.0, tm1[:, :], ALU.add, ALU.mult),
                   reads=["tm1", "sigo"], writes=["hm"])
            for h in range(4):
                hs = slice(h * 128, (h + 1) * 128)
                sc.add("act", lambda e, h=h, hs=hs: e.activation(junk[:, :], hm[:, hs], AF.Square, accum_out=ssq[:, h:h + 1]),
                       reads=["hm"], writes=["junk", "ssq"])
            _rstd_ops(sc, ssq[:, :], rsq[:, :], nhalf[:, 0:1].broadcast_to([128, 4]), 128.0, "ssq", "rsq", "msq", msq[:, :])
            g = t * 4 + c
            ob = g % 2
            sc.add("pool", lambda e: e.tensor_tensor(v4(o3[ob][:, :]), v4(hm[:, :]), b4(rsq[:, :]), ALU.mult),
                   reads=["hm", "rsq"], writes=[("o3", ob)])

            gold = gsc[:, :, c]
            gnew = gsc[:, :, 4 + c]
            sc.add("dve", lambda e: e.tensor_tensor(v4(tm3[:, :]), v4(Q[7]), b4(gnew), ALU.mult),
                   reads=[("Q", 7), ("gsc", b)], writes=["tm2"])
            sc.add("pool", lambda e: e.tensor_tensor(Cst[:, :, :], Cst[:, :, :], b4(gold), ALU.mult),
                   reads=["Cst", ("gsc", b)], writes=["Cst"])
            sc.add("pool", lambda e: e.tensor_tensor(Cst[:, :, :], Cst[:, :, :], v4(tm3[:, :]), ALU.add),
                   reads=["Cst", "tm2"], writes=["Cst"])
            sc.add("act", lambda e: e.activation(Cbf[:, :, :].rearrange("p h v -> p (h v)"),
                                                 Cst[:, :, :].rearrange("p h v -> p (h v)"), AF.Copy),
                   reads=["Cst"], writes=["Cbf"])
            sc.add("dve", lambda e: e.tensor_tensor(tn2[:, :], Q[1][:, 0:4], gnew, ALU.mult),
                   reads=[("Q", 1), ("gsc", b)], writes=["tn2"])
            sc.add("dve", lambda e: e.tensor_tensor(nst[:, :], nst[:, :], gold, ALU.mult), reads=["nst", ("gsc", b)], writes=["nst"])
            sc.add("dve", lambda e: e.tensor_tensor(nst[:, :], nst[:, :], tn2[:, :], ALU.add), reads=["nst", "tn2"], writes=["nst"])
            sc.add("act", lambda e: e.activation(nbf[:, :], nst[:, :], AF.Copy), reads=["nst"], writes=["nbf"])

        def chunk_back(t, c):
            b = t % 2
            lruT = lruT2[b]
            g = t * 4 + c
            ob = g % 2
            sl = g % 2
            r0 = g * 128
            cl = slice(c * 128, (c + 1) * 128)
            sc.add("sp", lambda e: e.dma_start(out=xres[sl][:, :], in_=src[r0:r0 + 128, :]),
                   reads=[(srcname, g)], writes=[("xres", sl)], dma="xres%d" % sl)

            def tr(pe):
                ins = None
                for m in range(4):
                    ins = pe.transpose(PT[:, 512 + m * 128:512 + (m + 1) * 128], o3[ob][:, m * 128:(m + 1) * 128], ident[:, :])
                return ins
            sc.add("pe", tr, reads=[("o3", ob), "ident"], writes=[("Q", 6)])
            sc.add("act", lambda e: e.activation(oTm[:, :, :].rearrange("p m t -> p (m t)"), PT[:, 512:1024], AF.Copy),
                   reads=[("Q", 6)], writes=["oTm"])
            for dh in range(2):
                def mm(pe, dh=dh):
                    ins = None
                    for m in range(4):
                        ins = pe.matmul(Q[dh], lruT[:, m, cl], wout[:, m, dh * 512:(dh + 1) * 512],
                                        start=(m == 0), stop=False)
                    for m in range(4):
                        ins = pe.matmul(Q[dh], oTm[:, m, :], wout[:, 4 + m, dh * 512:(dh + 1) * 512],
                                        start=False, stop=(m == 3))
                    return ins
                sc.add("pe", mm, reads=["oTm"] + [("lruT", b, m) for m in range(4)] + [("wout", m) for m in range(8)],
                       writes=[("Q", dh)])
                sc.add("dve", lambda e, dh=dh: e.tensor_tensor(xres[sl][:, dh * 512:(dh + 1) * 512], Q[dh],
                                                               xres[sl][:, dh * 512:(dh + 1) * 512], ALU.add),
                       reads=[("Q", dh), ("xres", sl)], writes=[("xres", sl)])
            sc.add("sp", lambda e: e.dma_start(out=dst[r0:r0 + 128, :], in_=xres[sl][:, :]),
                   reads=[("xres", sl)], writes=[(dstname, g)], dma="xst%d" % sl)

        def phase_b_piece(t, i):
            lru_piece(t, i)
            ml_piece(t, i)

        for j in range(4):
            norm_sub(sc, B, src, srcname, PT[:, 0:1024], ("Q", 6), 0, j)
        for i in range(4):
            phase_b_piece(0, i)
        gates_piece(0)
        for t in range(NT):
            pending = None
            nxt = t + 1 < NT
            for c in range(4):
                chunk_front(t, c)
                if pending is not None:
                    chunk_back(*pending)
                pending = (t, c)
                if nxt:
                    if c < 2:
                        norm_sub(sc, B, src, srcname, PT[:, 0:1024], ("Q", 6), t + 1, 2 * c)
                        norm_sub(sc, B, src, srcname, PT[:, 0:1024], ("Q", 6), t + 1, 2 * c + 1)
                    else:
                        phase_b_piece(t + 1, 2 * (c - 2))
                        phase_b_piece(t + 1, 2 * (c - 2) + 1)
            chunk_back(*pending)
            if nxt:
                gates_piece(t + 1)
        sc.barrier()
        sc.flush()


def build_program(S, phases=("ffn1_0",), raw_last=False):
    nc = bass.Bass("TRN2", target_bir_lowering=False)

    def din(name, shape):
        return nc.dram_tensor(name, list(shape), F32, kind="ExternalInput").ap()

    x = din("x", [S, D])
    W = {}
    W["ffn1_norm"] = din("ffn1_norm", [2, D])
    W["ffn1_wgu"] = din("ffn1_wgu", [2, D, 2 * DFF])
    W["ffn1_wd"] = din("ffn1_wd", [2, DFF, D])
    W["ffn2_norm"] = din("ffn2_norm", [2, D])
    W["ffn2_wgu"] = din("ffn2_wgu", [2, D, 2 * DFF])
    W["ffn2_wd"] = din("ffn2_wd", [2, DFF, D])
    W["final_norm"] = din("final_norm", [D])
    W["mix_norm"] = din("mix_norm", [2, D])
    E = {"w_in": din("e_w_in", [1, D, 3600])[0], "w_lr_up": din("e_w_lr_up", [1, 16, 256])[0],
         "b_lr": din("e_b_lr", [1, 256])[0], "head_norm": din("e_head_norm", [1, D])[0],
         "w_out": din("e_w_out", [1, D, D])[0]}
    c_ident = din("c_ident", [128, 128])
    C = {"ident": c_ident, "cos": din("c_cos", [128, S]), "sin": din("c_sin", [128, S]),
         "dq": din("c_dq", [4, 128]), "dk": din("c_dk", [4, 128]), "maskT": din("c_maskT", [128, 128]),
         "rmask": din("c_rmask", [512]), "rneg": din("c_rneg", [512]), "identF": c_ident,
         "perm": din("c_perm", [128, 128]),
         "sel": din("c_sel", [4, 4, 128])}
    O = {"w_in": din("o_w_in", [1, D, 2568])[0], "w_out": din("o_w_out", [1, D, D])[0],
         "lru_conv_w": din("o_lru_conv_w", [1, 4, 512])[0], "lru_conv_b": din("o_lru_conv_b", [1, 512])[0],
         "lru_wa": din("o_lru_wa", [1, 8, 64, 64])[0], "lru_ba": din("o_lru_ba", [1, 512])[0],
         "lru_wx": din("o_lru_wx", [1, 8, 64, 64])[0], "lru_bx": din("o_lru_bx", [1, 512])[0],
         "lru_lambda": din("o_lru_lambda", [1, 512])[0],
         "ml_conv_w": din("o_ml_conv_w", [1, 4, 512])[0], "ml_conv_b": din("o_ml_conv_b", [1, 512])[0],
         "ml_wq": din("o_ml_wq", [1, 4, 128, 128])[0], "ml_wk": din("o_ml_wk", [1, 4, 128, 128])[0],
         "ml_bi": din("o_ml_bi", [1, 4])[0], "ml_bf": din("o_ml_bf", [1, 4])[0],
         "ml_norm": din("o_ml_norm", [1, 512])[0]}
    out = nc.dram_tensor("out", [S, D], F32, kind="ExternalOutput").ap()
    hA = nc.dram_tensor("hA", [S, D], F32).ap()
    hB = nc.dram_tensor("hB", [S, D], F32).ap()

    sc = Sched(nc)
    cx = Ctx(nc, sc, S)
    bufs = {"x": x, "hA": hA, "hB": hB, "out": out}
    cur = "x"
    plist = list(phases)
    for i, ph in enumerate(plist):
        last = i == len(plist) - 1
        if last:
            nxt = "out"
        else:
            nxt = "hA" if cur != "hA" else "hB"
        kind, layer = ph.rsplit("_", 1)
        layer = int(layer)
        if kind in ("ffn1", "ffn2"):
            fin = W["final_norm"] if (kind == "ffn2" and layer == 1 and not raw_last) else None
            ffn_phase(cx, bufs[cur], bufs[nxt], cur, nxt, W[kind + "_norm"][layer],
                      W[kind + "_wgu"][layer], W[kind + "_wd"][layer], c_ident, fin_ap=fin)
        elif kind == "mix" and layer == 1:
            odd_phase(cx, bufs[cur], bufs[nxt], cur, nxt, W["mix_norm"][1], O, C)
        elif kind == "mix" and layer == 0:
            even_phase(cx, bufs[cur], bufs[nxt], cur, nxt, W["mix_norm"][0], E, C)
        else:
            raise ValueError(ph)
        cur = nxt
    sc.close()
    return nc


def host_consts(S):
    c = {"c_ident": np.eye(128, dtype=np.float32)}
    half = 64
    inv = 10000.0 ** (-np.arange(half, dtype=np.float64) / half)
    pos = np.arange(S, dtype=np.float64)
    ang = pos[None, :] * inv[:, None]
    cos = np.cos(ang).astype(np.float32)
    sin = np.sin(ang).astype(np.float32)
    c["c_cos"] = np.concatenate([cos, cos], 0)
    c["c_sin"] = np.concatenate([-sin, sin], 0)
    l = np.arange(128, dtype=np.float64)
    gam = np.array(RET_G, dtype=np.float64)
    c["c_dq"] = (gam[:, None] ** (l[None, :] + 1.0)).astype(np.float32)
    c["c_dk"] = (gam[:, None] ** (-(l[None, :] + 1.0)) * 128.0 ** -0.5).astype(np.float32)
    c["c_maskT"] = (np.arange(128)[:, None] <= np.arange(128)[None, :]).astype(np.float32)
    rm = np.ones(512, dtype=np.float32)
    rm[::128] = 0.0
    c["c_rmask"] = rm
    rn = np.zeros(512, dtype=np.float32)
    rn[::128] = -1.0e30
    c["c_rneg"] = rn
    sel = np.zeros((4, 4, 128), dtype=np.float32)
    for h in range(4):
        sel[h, h, :] = 1.0
    c["c_sel"] = sel
    pm = np.zeros((128, 128), dtype=np.float32)
    for d_ in range(128):
        pm[(d_ + 64) % 128, d_] = 1.0
    c["c_perm"] = pm
    return c


def prep_inputs(ins):
    return {k: np.ascontiguousarray(v, dtype=np.float32) for k, v in ins.items()}


ALL_PHASES = ("ffn1_0", "mix_0", "ffn2_0", "ffn1_1", "mix_1", "ffn2_1")
_PROG_CACHE = {}


def kernel(**inputs):
    x = np.asarray(inputs["x"], dtype=np.float32)
    Bn, S, _ = x.shape
    if S not in _PROG_CACHE:
        _PROG_CACHE[S] = build_program(S, phases=ALL_PHASES)
    nc = _PROG_CACHE[S]
    shared = {k: np.ascontiguousarray(np.asarray(v), dtype=np.float32) for k, v in inputs.items() if k != "x"}
    shared.update(host_consts(S))
    in_maps = []
    for b in range(Bn):
        m = dict(shared)
        m["x"] = np.ascontiguousarray(x[b])
        in_maps.append(m)
    res = run_bass_kernel_spmd(nc, in_maps, core_ids=list(range(Bn)))
    return np.stack([np.asarray(r["out"], dtype=np.float32) for r in res.results], axis=0)
```
